# Optimizing a Trainium2 kernel written in Bass

```python
import jax
import jax.numpy as jnp
from jax import lax
import numpy as np


D_MODEL = 2048
BATCH = 4
SEQ = 2048
DEPTH = 4

CTX_LEN = 256
GRID_W = 64
RMS_EPS = 1e-6
N_BRANCH = 3
BRANCH_DIM = 1024

POOL_WINDOWS = (2, 4, 8, 16)
N_POOL_GROUPS = len(POOL_WINDOWS)
POOL_DIM = BRANCH_DIM
POOL_GROUP = POOL_DIM // N_POOL_GROUPS

MLA_HEADS = 8
MLA_Q_LORA = 512
MLA_KV_LORA = 512
MLA_NOPE = 128
MLA_ROPE = 64
MLA_V = BRANCH_DIM // MLA_HEADS
MLA_QK = MLA_NOPE + MLA_ROPE
ROPE_BASE = 10000.0
Q_BLOCK = 128

GLA_HEADS = 4
GLA_DK = 256
GLA_DV = BRANCH_DIM // GLA_HEADS
GLA_GATE_RANK = 16
GLA_GATE_TAU = 16.0
GLA_CHUNK = 64

FFN_HIDDEN = -(-8 * D_MODEL // (3 * 256)) * 256

KV_SPLITS = (MLA_KV_LORA, MLA_ROPE, GLA_HEADS * GLA_DK, GLA_HEADS * GLA_DV, GLA_GATE_RANK, GLA_GATE_RANK)
Q_SPLITS = (MLA_Q_LORA, GLA_HEADS * GLA_DK, GLA_HEADS * GLA_DV, POOL_DIM, N_BRANCH * D_MODEL)
KV_COLS = sum(KV_SPLITS)
IN_COLS = KV_COLS + sum(Q_SPLITS)

kernel_name = "hybrid_pool_mla_gla_dit_block"


def rms_norm(x, g):
    xf = x.astype(jnp.float32)
    y = xf * lax.rsqrt(jnp.mean(xf * xf, axis=-1, keepdims=True) + RMS_EPS)
    return (y * g.astype(jnp.float32)).astype(x.dtype)


def modulate(h, shift, scale):
    return h * (1.0 + scale) + shift


def split_cols(z, sizes):
    return jnp.split(z, [int(i) for i in np.cumsum(sizes)[:-1]], axis=-1)


def to_heads(t, n_heads):
    b, s, _ = t.shape
    return t.reshape(b, s, n_heads, -1).transpose(0, 2, 1, 3)


def from_heads(t):
    b, h, s, d = t.shape
    return t.transpose(0, 2, 1, 3).reshape(b, s, h * d)


def flip_seq(t):
    return jnp.flip(t, axis=2)


def axial_rope_tables(n_rows, dtype):
    half = MLA_ROPE // 2
    inv_freq = 1.0 / (ROPE_BASE ** (jnp.arange(0, half, 2, dtype=jnp.float32) / half))
    row = jnp.repeat(jnp.arange(n_rows), GRID_W).astype(jnp.float32)
    col = jnp.tile(jnp.arange(GRID_W), n_rows).astype(jnp.float32)
    ang_r = row[:, None] * inv_freq[None, :]
    ang_c = col[:, None] * inv_freq[None, :]
    return tuple(t.astype(dtype)[None, :, None, :] for t in
                 (jnp.cos(ang_r), jnp.sin(ang_r), jnp.cos(ang_c), jnp.sin(ang_c)))


def rotate(x, cos, sin):
    x1, x2 = jnp.split(x, 2, axis=-1)
    return jnp.concatenate([x1 * cos - x2 * sin, x2 * cos + x1 * sin], axis=-1)


def apply_axial_rope(t, rope):
    cos_r, sin_r, cos_c, sin_c = rope
    t_nope, t_row, t_col = split_cols(t, (MLA_NOPE, MLA_ROPE // 2, MLA_ROPE // 2))
    return jnp.concatenate([t_nope, rotate(t_row, cos_r, sin_r), rotate(t_col, cos_c, sin_c)], axis=-1)


def mla_queries(cq, q_norm_g, w_q_up, q_head_g, rope):
    b, s, _ = cq.shape
    q = (rms_norm(cq, q_norm_g) @ w_q_up).reshape(b, s, MLA_HEADS, MLA_QK)
    q = rms_norm(q, q_head_g)
    if rope is not None:
        q = apply_axial_rope(q, rope)
    return q.transpose(0, 2, 1, 3)


def mla_keys_values(ckv, krope, kv_norm_g, w_kv_up, k_head_g, rope):
    b, s, _ = ckv.shape
    kv = (rms_norm(ckv, kv_norm_g) @ w_kv_up).reshape(b, s, MLA_HEADS, MLA_NOPE + MLA_V)
    k_nope, v = kv[..., :MLA_NOPE], kv[..., MLA_NOPE:]
    k_rope = jnp.broadcast_to(krope[:, :, None, :], (b, s, MLA_HEADS, MLA_ROPE))
    k = rms_norm(jnp.concatenate([k_nope, k_rope], axis=-1), k_head_g)
    if rope is not None:
        k = apply_axial_rope(k, rope)
    return k.transpose(0, 2, 1, 3), v.transpose(0, 2, 1, 3)


def block_attention(q, k, v):
    b, h, s, dh = q.shape
    nb = s // Q_BLOCK
    scale = dh ** -0.5
    qb = q.reshape(b, h, nb, Q_BLOCK, dh).transpose(2, 0, 1, 3, 4)

    def attend(qi):
        sc = jnp.einsum('bhqd,bhkd->bhqk', qi, k).astype(jnp.float32) * scale
        p = jax.nn.softmax(sc, axis=-1).astype(v.dtype)
        return jnp.einsum('bhqk,bhkd->bhqd', p, v)

    o = lax.map(attend, qb)
    return o.transpose(1, 2, 0, 3, 4).reshape(b, h, s, v.shape[-1])


def gla_chunk_scan(q, k, v, log_a, s0):
    b, h, s, dk = q.shape
    dv = v.shape[-1]
    n = s // GLA_CHUNK

    def chunks(t):
        return t.reshape(b, h, n, GLA_CHUNK, t.shape[-1]).transpose(2, 0, 1, 3, 4)

    causal = jnp.tril(jnp.ones((GLA_CHUNK, GLA_CHUNK), dtype=bool))[:, :, None]

    def step(state, inp):
        qc, kc, vc, ac = inp
        cum = jnp.cumsum(ac, axis=2)
        inter = jnp.einsum('bhtd,bhde->bhte', qc * jnp.exp(cum), state)
        rel = jnp.where(causal, cum[:, :, :, None, :] - cum[:, :, None, :, :], -jnp.inf)
        scores = jnp.einsum('bhtd,bhsd,bhtsd->bhts', qc, kc, jnp.exp(rel))
        intra = jnp.einsum('bhts,bhse->bhte', scores, vc)
        last = cum[:, :, -1:, :]
        new_state = state * jnp.exp(last).swapaxes(-1, -2) + jnp.einsum(
            'bhsd,bhse->bhde', kc * jnp.exp(last - cum), vc)
        return new_state.astype(state.dtype), inter + intra

    s_fin, o = lax.scan(step, s0, (chunks(q), chunks(k), chunks(v), chunks(log_a)))
    return s_fin, o.transpose(1, 2, 0, 3, 4).reshape(b, h, s, dv)


def gla_final_state(k, v, log_a):
    cum = jnp.cumsum(log_a, axis=2)
    return jnp.einsum('bhsd,bhse->bhde', k * jnp.exp(cum[:, :, -1:, :] - cum), v)


def gla_output(o, gg, out_norm_g):
    b, h, s, dv = o.shape
    on = rms_norm(o.transpose(0, 2, 1, 3), out_norm_g)
    return on.reshape(b, s, h * dv) * jax.nn.silu(gg)


def multiscale_pool(u):
    b, s, _ = u.shape
    uf = u.astype(jnp.float32).reshape(b, s, N_POOL_GROUPS, POOL_GROUP)
    cs = jnp.concatenate([jnp.zeros_like(uf[:, :1]), jnp.cumsum(uf, axis=1)], axis=1)
    t = jnp.arange(s)
    outs = []
    for gi, w in enumerate(POOL_WINDOWS):
        lo = jnp.clip(t - w // 2, 0, s - 1)
        hi = jnp.clip(t + w // 2 - 1, 0, s - 1)
        win_sum = cs[:, hi + 1, gi] - cs[:, lo, gi]
        cnt = (hi - lo + 1).astype(jnp.float32)[None, :, None]
        outs.append(win_sum / cnt - uf[:, :, gi])
    return jnp.stack(outs, axis=2).astype(u.dtype)


def pool_branch(u, pool_w, pool_scale):
    y = jnp.einsum('btgc,gcd->btgd', multiscale_pool(u), pool_w)
    return y.reshape(u.shape) * pool_scale


def merge_branches(pool_o, mla_o, gla_o, gate_logits, w_branch, w_out):
    b, s, _ = pool_o.shape
    br = jnp.stack([pool_o, mla_o, gla_o], axis=2)
    proj = jnp.einsum('btnc,ncd->btnd', br, w_branch)
    gates = jax.nn.sigmoid(gate_logits.reshape(b, s, N_BRANCH, D_MODEL))
    return jnp.sum(gates * proj, axis=2) @ w_out


def swiglu(h, w1, w3, w2):
    return (jax.nn.silu(h @ w1) * (h @ w3)) @ w2


def hybrid_mixer(h_lat, h_ctx, need_ctx_out, rope, w_in, mla_q_norm_g, mla_w_q_up, mla_kv_norm_g,
                 mla_w_kv_up, mla_q_head_g, mla_k_head_g, gla_w_gate_up, gla_b_gate, gla_out_norm_g,
                 pool_w, pool_scale, w_branch, w_out):
    z_lat = h_lat @ w_in
    z_ctx = h_ctx @ (w_in if need_ctx_out else w_in[:, :KV_COLS])
    ckv_l, kr_l, gk_l, gv_l, lrf_l, lrb_l = split_cols(z_lat[..., :KV_COLS], KV_SPLITS)
    cq_l, gq_l, gg_l, pin_l, gate_l = split_cols(z_lat[..., KV_COLS:], Q_SPLITS)
    ckv_c, kr_c, gk_c, gv_c, lrf_c, lrb_c = split_cols(z_ctx[..., :KV_COLS], KV_SPLITS)

    k_l, v_l = mla_keys_values(ckv_l, kr_l, mla_kv_norm_g, mla_w_kv_up, mla_k_head_g, rope)
    k_c, v_c = mla_keys_values(ckv_c, kr_c, mla_kv_norm_g, mla_w_kv_up, mla_k_head_g, None)
    q_l = mla_queries(cq_l, mla_q_norm_g, mla_w_q_up, mla_q_head_g, rope)
    mla_l = from_heads(block_attention(q_l, jnp.concatenate([k_c, k_l], axis=2),
                                       jnp.concatenate([v_c, v_l], axis=2)))

    def log_decay(lr, d):
        return to_heads(jax.nn.log_sigmoid(lr @ gla_w_gate_up[d] + gla_b_gate[d]) / GLA_GATE_TAU, GLA_HEADS)

    q_scale = GLA_DK ** -0.5
    kg_c, vg_c = to_heads(gk_c, GLA_HEADS), to_heads(gv_c, GLA_HEADS)
    laf_c, lab_c = log_decay(lrf_c, 0), log_decay(lrb_c, 1)
    if need_ctx_out:
        cq_c, gq_c, gg_c, pin_c, gate_c = split_cols(z_ctx[..., KV_COLS:], Q_SPLITS)
        qg_c = to_heads(gq_c, GLA_HEADS) * q_scale
        zero = jnp.zeros(kg_c.shape[:2] + (GLA_DK, GLA_DV), vg_c.dtype)
        s_cf, o_cf = gla_chunk_scan(qg_c, kg_c, vg_c, laf_c, zero)
        s_cb, o_cb = gla_chunk_scan(flip_seq(qg_c), flip_seq(kg_c), flip_seq(vg_c), flip_seq(lab_c), zero)
        gla_c = gla_output(o_cf + flip_seq(o_cb), gg_c, gla_out_norm_g)
    else:
        s_cf = gla_final_state(kg_c, vg_c, laf_c)
        s_cb = gla_final_state(flip_seq(kg_c), flip_seq(vg_c), flip_seq(lab_c))
    qg_l = to_heads(gq_l, GLA_HEADS) * q_scale
    kg_l, vg_l = to_heads(gk_l, GLA_HEADS), to_heads(gv_l, GLA_HEADS)
    _, o_lf = gla_chunk_scan(qg_l, kg_l, vg_l, log_decay(lrf_l, 0), s_cf)
    _, o_lb = gla_chunk_scan(flip_seq(qg_l), flip_seq(kg_l), flip_seq(vg_l),
                             flip_seq(log_decay(lrb_l, 1)), s_cb)
    gla_l = gla_output(o_lf + flip_seq(o_lb), gg_l, gla_out_norm_g)

    pool_l = pool_branch(pin_l, pool_w, pool_scale)
    y_lat = merge_branches(pool_l, mla_l, gla_l, gate_l, w_branch, w_out)
    if not need_ctx_out:
        return y_lat, None

    q_c = mla_queries(cq_c, mla_q_norm_g, mla_w_q_up, mla_q_head_g, None)
    mla_c = from_heads(block_attention(q_c, k_c, v_c))
    pool_c = pool_branch(pin_c, pool_w, pool_scale)
    y_ctx = merge_branches(pool_c, mla_c, gla_c, gate_c, w_branch, w_out)
    return y_lat, y_ctx


def setup_inputs(seed: int = 0) -> dict:
    key = jax.random.key(seed)
    ks = jax.random.split(key, 32)
    f32 = jnp.float32
    D, L = D_MODEL, DEPTH

    def nrm(k, shape, fan_in, gain=1.0):
        return gain * fan_in ** -0.5 * jax.random.normal(k, shape, f32)

    def ones_noise(k, shape):
        return 1.0 + 0.02 * jax.random.normal(k, shape, f32)

    return {
        "x": jax.random.normal(ks[0], (BATCH, SEQ, D), f32),
        "c": jax.random.normal(ks[1], (BATCH, D), f32),
        "ctx": jax.random.normal(ks[2], (BATCH, CTX_LEN, D), f32),
        "c_ctx": jax.random.normal(ks[3], (D,), f32),
        "w_mod": nrm(ks[4], (L, D, 6 * D), D, 0.5),
        "b_mod": 0.01 * jax.random.normal(ks[5], (L, 6 * D), f32),
        "norm1_g": ones_noise(ks[6], (L, D)),
        "norm2_g": ones_noise(ks[7], (L, D)),
        "w_in": nrm(ks[8], (L, D, IN_COLS), D),
        "mla_q_norm_g": ones_noise(ks[9], (L, MLA_Q_LORA)),
        "mla_w_q_up": nrm(ks[10], (L, MLA_Q_LORA, MLA_HEADS * MLA_QK), MLA_Q_LORA),
        "mla_kv_norm_g": ones_noise(ks[11], (L, MLA_KV_LORA)),
        "mla_w_kv_up": nrm(ks[12], (L, MLA_KV_LORA, MLA_HEADS * (MLA_NOPE + MLA_V)), MLA_KV_LORA),
        "mla_q_head_g": ones_noise(ks[13], (L, MLA_QK)),
        "mla_k_head_g": ones_noise(ks[14], (L, MLA_QK)),
        "gla_w_gate_up": nrm(ks[15], (L, 2, GLA_GATE_RANK, GLA_HEADS * GLA_DK), GLA_GATE_RANK),
        "gla_b_gate": 0.1 * jax.random.normal(ks[16], (L, 2, GLA_HEADS * GLA_DK), f32),
        "gla_out_norm_g": ones_noise(ks[17], (L, GLA_DV)),
        "pool_w": nrm(ks[18], (L, N_POOL_GROUPS, POOL_GROUP, POOL_GROUP), POOL_GROUP),
        "pool_scale": 1.0 + 0.1 * jax.random.normal(ks[19], (L, POOL_DIM), f32),
        "w_branch": nrm(ks[20], (L, N_BRANCH, BRANCH_DIM, D), BRANCH_DIM),
        "w_out": nrm(ks[21], (L, D, D), D),
        "ffn_w1": nrm(ks[22], (L, D, FFN_HIDDEN), D),
        "ffn_w3": nrm(ks[23], (L, D, FFN_HIDDEN), D),
        "ffn_w2": nrm(ks[24], (L, FFN_HIDDEN, D), FFN_HIDDEN),
    }


def reference(x, c, ctx, c_ctx, w_mod, b_mod, norm1_g, norm2_g, w_in, mla_q_norm_g, mla_w_q_up,
              mla_kv_norm_g, mla_w_kv_up, mla_q_head_g, mla_k_head_g, gla_w_gate_up, gla_b_gate,
              gla_out_norm_g, pool_w, pool_scale, w_branch, w_out, ffn_w1, ffn_w3, ffn_w2):
    ROWS = x.shape[1] // GRID_W
    rope = axial_rope_tables(ROWS, x.dtype)
    silu_c = jax.nn.silu(c)
    silu_cc = jax.nn.silu(c_ctx)
    xc = ctx
    for l in range(DEPTH):
        last = l == DEPTH - 1
        mod = silu_c @ w_mod[l] + b_mod[l]
        sh1, sc1, g1, sh2, sc2, g2 = jnp.split(mod[:, None, :], 6, axis=-1)
        n_mod = 2 if last else 6
        mods_c = jnp.split(silu_cc @ w_mod[l][:, :n_mod * D_MODEL] + b_mod[l][:n_mod * D_MODEL], n_mod)

        h_lat = modulate(rms_norm(x, norm1_g[l]), sh1, sc1)
        h_ctx = modulate(rms_norm(xc, norm1_g[l]), mods_c[0], mods_c[1])
        y_lat, y_ctx = hybrid_mixer(h_lat, h_ctx, not last, rope, w_in[l], mla_q_norm_g[l], mla_w_q_up[l],
                                    mla_kv_norm_g[l], mla_w_kv_up[l], mla_q_head_g[l], mla_k_head_g[l],
                                    gla_w_gate_up[l], gla_b_gate[l], gla_out_norm_g[l], pool_w[l],
                                    pool_scale[l], w_branch[l], w_out[l])
        x = x + g1 * y_lat
        x = x + g2 * swiglu(modulate(rms_norm(x, norm2_g[l]), sh2, sc2), ffn_w1[l], ffn_w3[l], ffn_w2[l])
        if not last:
            xc = xc + mods_c[2] * y_ctx
            xc = xc + mods_c[5] * swiglu(modulate(rms_norm(xc, norm2_g[l]), mods_c[3], mods_c[4]),
                                         ffn_w1[l], ffn_w3[l], ffn_w2[l])
    return x
```

```python
import numpy as np
from contextlib import ExitStack
import concourse.bass as bass
import concourse.mybir as mybir
from concourse.bass_utils import run_bass_kernel_spmd

F32, BF16 = mybir.dt.float32, mybir.dt.bfloat16
AF = mybir.ActivationFunctionType
ALU = mybir.AluOpType
AX = mybir.AxisListType

D = 2048
DC = 16
TC = 256
TL = 2048
T = TC + TL
NTT = T // 128
L = 4
EPS = 1e-6
FF = 5632
FC = FF // 128
TILES5 = [(0, 256), (256, 512), (768, 512), (1280, 512), (1792, 512)]
C_CKV, C_KR, C_GK, C_GV, C_LRF, C_LRB, C_CQ, C_GQ, C_GG, C_PIN, C_GATE = 0, 512, 576, 1600, 2624, 2640, 2656, 3168, 4192, 5216, 6240
NSLOT = 8


class Stop(Exception):
    pass


class Eng:
    def __init__(s, K, name, e, is_dma=False):
        s.K, s.name, s.e, s.is_dma = K, name, e, is_dma
        s.seen = {}
        if is_dma:
            s.slots = [K.newsem(f"{name}{i}") for i in range(NSLOT)]
            s.slot_cnt = [0] * NSLOT
            s.n = 0
        else:
            s.sem = K.newsem(name)
            s.cnt = 0


class Buf:
    __slots__ = ("name", "w", "r")

    def __init__(s, name=""):
        s.name, s.w, s.r = name, None, {}


class Kern:
    def __init__(s, nc, es):
        s.nc, s.es = nc, es
        s.PE = Eng(s, "pe", nc.tensor)
        s.ACT = Eng(s, "act", nc.scalar)
        s.DVE = Eng(s, "dve", nc.vector)
        s.POOL = Eng(s, "pool", nc.gpsimd)
        s.SQ = Eng(s, "sq", nc.sync, True)
        s.GQ = Eng(s, "gq", nc.gpsimd, True)
        s.GQ.seen = s.POOL.seen
        s.engs = [s.PE, s.ACT, s.DVE, s.POOL]
        s.qs = [s.SQ, s.GQ]
        s.nbank = 0

    def newsem(s, name):
        return s.es.enter_context(s.nc.semaphore(name))

    def sb(s, name, shape, dt):
        s.uid = getattr(s, "uid", 0) + 1
        return s.es.enter_context(s.nc.sbuf_tensor(f"{name}_u{s.uid}", list(shape), dt))

    def wait(s, E, tok):
        if tok is None:
            return
        sem, val = tok
        if E is s.PE and sem is s.PE.sem:
            return
        if E.seen.get(id(sem), 0) >= val:
            return
        E.e.wait_ge(sem, val)
        E.seen[id(sem)] = val

    def _deps(s, E, R, W):
        for b in R:
            s.wait(E, b.w)
            if b.name.startswith("bank"):
                for t in list(b.r.values()):
                    if t[0] is not getattr(E, "sem", None):
                        s.wait(E, t)
        for b in W:
            s.wait(E, b.w)
            for t in list(b.r.values()):
                s.wait(E, t)

    def _upd(s, tok, R, W):
        for b in R:
            old = b.r.get(id(tok[0]))
            if old is None or old[1] < tok[1]:
                b.r[id(tok[0])] = tok
        for b in W:
            b.w = tok
            b.r = {}

    def op(s, E, fn, R=(), W=(), inc=True):
        s._deps(E, R, W)
        ins = fn()
        if inc:
            E.cnt += 1
            ins.then_inc(E.sem, 1)
            tok = (E.sem, E.cnt)
        else:
            tok = (E.sem, E.cnt + 1)
        s._upd(tok, R, W)
        return ins

    def dma(s, Q, out, in_, R=(), W=(), **kw):
        slot = Q.n % NSLOT
        Q.n += 1
        if Q.slot_cnt[slot] > 0:
            s.wait(Q, (Q.slots[slot], 16 * Q.slot_cnt[slot]))
        s._deps(Q, R, W)
        ins = Q.e.dma_start(out=out, in_=in_, **kw)
        Q.slot_cnt[slot] += 1
        ins.then_inc(Q.slots[slot], 16)
        tok = (Q.slots[slot], 16 * Q.slot_cnt[slot])
        s._upd(tok, R, W)
        return ins

    def all_tokens(s):
        toks = [(E.sem, E.cnt) for E in s.engs if E.cnt > 0]
        for Q in s.qs:
            for i in range(NSLOT):
                if Q.slot_cnt[i] > 0:
                    toks.append((Q.slots[i], 16 * Q.slot_cnt[i]))
        return toks

    def barrier(s, only=None):
        toks = s.all_tokens()
        for E in (only or (s.engs + [s.SQ])):
            for t in toks:
                s.wait(E, t)


def build(nlayers=L, dbg=None):
    nc = bass.Bass("TRN2", target_bir_lowering=False)
    dbg = dbg or {}

    def din(name, shape, dt=F32):
        return nc.dram_tensor(name, list(shape), dt, kind="ExternalInput").ap()

    def dscr(name, shape, dt):
        kind = "ExternalOutput" if name in dbg else "Internal"
        return nc.dram_tensor(name, list(shape), dt, kind=kind).ap()

    xin = din("xin", [T, D])
    cvec = din("cvec", [32, 128])
    w_mod = din("w_mod", [L, D, 6 * D])
    b_mod = din("b_mod", [L * 96, 128])
    norm1_g = din("norm1_g", [L * 16, 128])
    norm2_g = din("norm2_g", [L * 16, 128])
    w_in = din("w_in", [L, D, 12384])
    ident_d = din("ident_in", [128, 128])
    out_d = nc.dram_tensor("out", [TL, D], F32, kind="ExternalOutput").ap()
    w_kv_up = din("w_kv_up", [L, 512, 2048])
    w_q_up = din("w_q_up", [L, 512, 1536])
    qng_d = din("qng", [L * 4, 128])
    kvng_d = din("kvng", [L * 4, 128])
    hgn_d = din("hg_n", [2 * L, 128])
    hgr_d = din("hg_r", [2 * L, 64])
    cos_d = din("cosT", [64, T])
    sin_d = din("sinT", [64, T])
    rm_d = din("Rm", [64, 64])
    wg_d = din("gla_wg", [L, 2, 16, 1024])
    gb_d = din("gla_b", [L * 16, 128])
    gon_d = din("gon_rep", [L, 128, 256])
    msk_d = din("masks", [2, 128, 128])
    pool_w = din("pool_w", [L, 4, 256, 256])
    pscale_d = din("pool_scale", [L * 8, 128])
    invc_d = din("invcnt", [4, 128, T])
    w_branch = din("w_branch", [L, 3, 1024, D])
    w_out = din("w_out", [L, D, D])
    ffn_w1 = din("ffn_w1", [L, D, FF])
    ffn_w3 = din("ffn_w3", [L, D, FF])
    ffn_w2 = din("ffn_w2", [L, FF, D])

    xT = dscr("xT", [DC, 128, T], F32)
    ckvT = dscr("ckvT", [4, 128, T], F32)
    cqT = dscr("cqT", [4, 128, T], F32)
    krT = dscr("krT", [64, T], F32)
    lrfT = dscr("lrfT", [16, T], F32)
    lrbT = dscr("lrbT", [16, T], F32)
    gkT = dscr("gkT", [8, 128, T], BF16)
    gqT = dscr("gqT", [8, 128, T], BF16)
    pinT = dscr("pinT", [8, 128, T], BF16)
    gateT = dscr("gateT", [48, 128, T], BF16)
    gvtm = dscr("gvtm", [NTT, 128, 1024], BF16)
    ggtm = dscr("ggtm", [NTT, 128, 1024], BF16)
    mlaT = dscr("mlaT", [8, 128, T], BF16)
    glaT = dscr("glaT", [8, 128, T], BF16)
    poolT = dscr("poolT", [8, 128, T], BF16)
    if "gla_dbg" in dbg:
        dL = dscr("dL", [128, T], F32); dC = dscr("dC", [128, T], F32); dO = dscr("dO", [128, NTT, 256], F32)
        dC1 = dscr("dC1", [128, T], F32)

    with ExitStack() as es:
        K = Kern(nc, es)
        PE, ACT, DVE, POOL, SQ, GQ = K.PE, K.ACT, K.DVE, K.POOL, K.SQ, K.GQ
        ps_t = es.enter_context(nc.psum_tensor("ps", [128, 8, 512], F32))
        banks = [Buf(f"bank{i}") for i in range(8)]

        def bank(i=None):
            if i is None:
                i = K.nbank % 8
                K.nbank += 1
            return banks[i], ps_t[:, i, :]

        ident = K.sb("ident", [128, 128], F32)
        identb = K.sb("identb", [128, 128], BF16)
        onesb = K.sb("onesb", [128, 128], BF16)
        modT = K.sb("modT", [128, L, 96, 2], F32)
        A1 = K.sb("A1", [128, L, 16, 2], F32)
        A2 = K.sb("A2", [128, L, 16, 2], F32)
        n1gT = K.sb("n1gT", [128, L * 16], F32)
        n2gT = K.sb("n2gT", [128, L * 16], F32)
        bmodT = K.sb("bmodT", [128, L * 96], F32)
        sT = K.sb("sT", [128, 32], BF16)
        qngT = K.sb("qngT", [128, L * 4], F32)
        kvngT = K.sb("kvngT", [128, L * 4], F32)
        hgnT = K.sb("hgnT", [128, 2 * L], F32)
        hgrT = K.sb("hgrT", [64, 2 * L], F32)
        Rmb = K.sb("Rmb", [64, 64], BF16)
        ngbT = K.sb("ngbT", [128, L * 16], F32)
        pscT = K.sb("pscT", [128, L * 8], F32)
        mskS = K.sb("mskS", [128, 2, 128], F32)
        B_const = Buf("const")
        B_mod = Buf("mod")
        B_hT = Buf("hT")
        B_xT = Buf("xT_dram")

        K.dma(SQ, ident[:], ident_d, W=[B_const])
        K.op(DVE, lambda: nc.vector.tensor_copy(out=identb[:], in_=ident[:]), R=[B_const], W=[B_const])
        K.op(DVE, lambda: nc.vector.memset(onesb[:], 1.0), W=[B_const])

        def load_cols(src_ap, R, dst_ap, scope, func=None, w=128):
            tmp = K.sb(f"lc_{scope}", [128, 128], F32)
            b_tmp = Buf()
            K.dma(SQ, tmp[0:R, 0:w], src_ap, W=[b_tmp])
            bb, ps = bank()
            K.op(PE, lambda: nc.tensor.matmul(ps[0:w, 0:R], tmp[0:R, 0:w], ident[0:R, 0:R], start=True, stop=True),
                 R=[b_tmp, B_const], W=[bb])
            if func is None:
                K.op(DVE, lambda: nc.vector.tensor_copy(out=dst_ap, in_=ps[0:w, 0:R]), R=[bb], W=[B_mod])
            else:
                K.op(ACT, lambda: nc.scalar.activation(out=dst_ap, in_=ps[0:w, 0:R], func=func), R=[bb], W=[B_mod])

        with ExitStack() as st:
            K.es = st
            load_cols(cvec, 32, sT[:, :], "cv", func=AF.Silu)
            load_cols(norm1_g, L * 16, n1gT[:, :], "n1")
            load_cols(norm2_g, L * 16, n2gT[:, :], "n2")
            load_cols(qng_d, L * 4, qngT[:, :], "qng")
            load_cols(kvng_d, L * 4, kvngT[:, :], "kvng")
            load_cols(hgn_d, 2 * L, hgnT[:, :], "hgn")
            load_cols(hgr_d, 2 * L, hgrT[:, :], "hgr", w=64)
            K.dma(GQ, Rmb[:], rm_d, W=[B_const])
            load_cols(gb_d, L * 16, ngbT[:, :], "gb")
            K.op(DVE, lambda: nc.vector.tensor_scalar(out=ngbT[:], in0=ngbT[:], scalar1=-1.0, scalar2=None, op0=ALU.mult),
                 R=[B_mod], W=[B_mod])
            load_cols(pscale_d, L * 8, pscT[:, :], "psc")
            K.dma(SQ, mskS[:], msk_d.rearrange("a p n -> p a n"), W=[B_const])
            for i in range(3):
                load_cols(b_mod[i * 128:(i + 1) * 128, :], 128, bmodT[:, i * 128:(i + 1) * 128], f"bm{i}")
            xl = [K.sb(f"xl{i}", [128, D], F32) for i in range(2)]
            xo = [K.sb(f"xo{i}", [128, DC, 128], F32) for i in range(2)]
            b_xl = [Buf(), Buf()]
            b_xo = [Buf(), Buf()]
            xT_v = xT.rearrange("c p t -> p c t")
            for tt in range(NTT):
                i = tt % 2
                K.dma(SQ, xl[i][:], xin[tt * 128:(tt + 1) * 128, :], W=[b_xl[i]])
                for g in range(4):
                    bb, ps = bank()
                    for j in range(4):
                        c = g * 4 + j
                        K.op(PE, lambda c=c, j=j, ps=ps, i=i: nc.tensor.matmul(
                            ps[:, j * 128:(j + 1) * 128], xl[i][:, c * 128:(c + 1) * 128], ident[:], start=True, stop=True),
                            R=[b_xl[i], B_const], W=[bb], inc=(j == 3))
                    E = ACT if g % 2 == 0 else DVE
                    dst = xo[i][:, g * 4:(g + 1) * 4, :]
                    src = ps.rearrange("p (a b) -> p a b", a=4)
                    if E is ACT:
                        K.op(ACT, lambda dst=dst, src=src: nc.scalar.copy(out=dst, in_=src), R=[bb], W=[b_xo[i]])
                    else:
                        K.op(DVE, lambda dst=dst, src=src: nc.vector.tensor_copy(out=dst, in_=src), R=[bb], W=[b_xo[i]])
                K.dma(SQ, xT_v[:, :, tt * 128:(tt + 1) * 128], xo[i][:], R=[b_xo[i]], W=[B_xT])
            wsl = [K.sb(f"wmod{i}", [128, 16, 512], BF16) for i in range(3)]
            b_wsl = [Buf() for _ in range(3)]
            nld = 0
            for l in range(nlayers):
                bb, ps = bank()
                for cg in range(24):
                    i = nld % 3
                    nld += 1
                    K.dma(GQ, wsl[i][:], w_mod[l, :, cg * 512:(cg + 1) * 512].rearrange("(k p) n -> p k n", p=128), W=[b_wsl[i]])
                    for j in range(4):
                        ch = cg * 4 + j
                        for k in range(16):
                            K.op(PE, lambda i=i, j=j, k=k, ch=ch, ps=ps: nc.tensor.matmul(
                                ps[:, ch * 2:(ch + 1) * 2], wsl[i][:, k, j * 128:(j + 1) * 128], sT[:, k::16],
                                start=(k == 0), stop=(k == 15)),
                                R=[b_wsl[i], B_mod], W=[bb], inc=(k == 15))
                K.op(DVE, lambda l=l, ps=ps: nc.vector.tensor_tensor(
                    out=modT[:, l, :, :], in0=ps[:, 0:192].rearrange("p (a b) -> p a b", b=2),
                    in1=bmodT[:, l * 96:(l + 1) * 96].unsqueeze(2).to_broadcast([128, 96, 2]), op=ALU.add),
                    R=[bb, B_mod], W=[B_mod])
                for (Ax, ng, lo) in ((A1, n1gT, 16), (A2, n2gT, 64)):
                    K.op(DVE, lambda l=l, Ax=Ax, ng=ng, lo=lo: nc.vector.scalar_tensor_tensor(
                        out=Ax[:, l, :, :], in0=modT[:, l, lo:lo + 16, :], scalar=1.0,
                        in1=ng[:, l * 16:(l + 1) * 16].unsqueeze(2).to_broadcast([128, 16, 2]),
                        op0=ALU.add, op1=ALU.mult), R=[B_mod], W=[B_mod])
            K.barrier()
        K.es = es

        if "stop_pre" in dbg:
            return finish(nc, K, out_d, modT, dbg)


        def mla_stage(l):
            SCALE = 192.0 ** -0.5
            B_z = Buf()
            kvnT = K.sb("kvnT", [128, 4, T], BF16)
            qnT = K.sb("qnT", [128, 4, T], BF16)
            b_kvn, b_qn = Buf(), Buf()
            wkv = K.sb("wkv", [128, 4, 2048], BF16)
            wq = K.sb("wq", [128, 4, 1536], BF16)
            b_w = Buf()
            for hf in range(2):
                K.dma(GQ, wkv[:, :, hf * 1024:(hf + 1) * 1024],
                      w_kv_up[l, :, hf * 1024:(hf + 1) * 1024].rearrange("(k p) n -> p k n", p=128), W=[b_w])
            K.dma(GQ, wq[:], w_q_up[l].rearrange("(k p) n -> p k n", p=128), W=[b_w])
            cosS = K.sb("cosS", [64, T], BF16)
            sinS = K.sb("sinS", [64, T], BF16)
            b_tab = Buf()
            K.dma(GQ, cosS[:], cos_d, W=[b_tab], max_dma_last_dim=4096)
            K.dma(GQ, sinS[:], sin_d, W=[b_tab], max_dma_last_dim=4096)
            krsq = K.sb("krsq", [128, T], BF16)
            K.op(DVE, lambda: nc.vector.memset(krsq[64:128, :], 0.0), W=[b_tab])
            krot = K.sb("krot", [64, T], F32)
            b_kr = Buf()
            raw = K.sb("raw", [128, T], F32)
            rawr = K.sb("rawr", [64, T], F32)
            rawrb = K.sb("rawrb", [64, T], BF16)
            sq = K.sb("sq", [128, T], BF16)
            sqr = K.sb("sqr", [128, T], BF16)
            K.op(DVE, lambda: nc.vector.memset(sqr[64:128, :], 0.0), W=[b_tab])
            rstd = K.sb("rstd", [128, T], F32)
            r1 = rstd
            t1 = K.sb("t1", [64, T], F32)
            t2 = K.sb("t2", [64, T], F32)
            b_raw, b_rawr, b_sq, b_r = Buf(), Buf(), Buf(), Buf()
            b_t = Buf()

            outer = K.es
            st_in = ExitStack()
            K.es = st_in
            krS = K.sb("krS", [64, T], F32)
            K.dma(SQ, krS[:], krT, W=[b_tab])
            lt = [K.sb(f"lt{i}", [128, 4, 512], F32) for i in range(2)]
            b_lt = [Buf(), Buf()]
            lsq = K.sb("lsq", [128, 4, 512], BF16)
            b_lsq = Buf()
            nld = 0
            for (src, gT, dstT, b_dst) in ((ckvT, kvngT, kvnT, b_kvn), (cqT, qngT, qnT, b_qn)):
                src_v = src.rearrange("c p t -> p c t")
                for (off, n) in TILES5:
                    i = nld % 2
                    nld += 1
                    K.dma(SQ, lt[i][:, :, 0:n], src_v[:, :, off:off + n], W=[b_lt[i]])
                    K.op(ACT, lambda i=i, n=n: nc.scalar.activation(out=lsq[:, :, 0:n], in_=lt[i][:, :, 0:n], func=AF.Square),
                         R=[b_lt[i]], W=[b_lsq])
                    bb, ps = bank()
                    for c in range(4):
                        K.op(PE, lambda c=c, n=n, ps=ps: nc.tensor.matmul(ps[:, 0:n], onesb[:], lsq[:, c, 0:n],
                                                                           start=(c == 0), stop=(c == 3)),
                             R=[b_lsq, B_const], W=[bb], inc=(c == 3))
                    K.op(ACT, lambda n=n, ps=ps: nc.scalar.activation(out=r1[:, 0:n], in_=ps[:, 0:n], func=AF.Sqrt,
                                                                      bias=EPS, scale=1.0 / 512), R=[bb], W=[b_r])
                    K.op(DVE, lambda n=n: nc.vector.reciprocal(out=rstd[:, 0:n], in_=r1[:, 0:n]), R=[b_r], W=[b_r])
                    for c in range(4):
                        K.op(DVE, lambda c=c, n=n, i=i, off=off, gT=gT, dstT=dstT: nc.vector.scalar_tensor_tensor(
                            out=dstT[:, c, off:off + n], in0=lt[i][:, c, 0:n], scalar=gT[:, l * 4 + c:l * 4 + c + 1],
                            in1=rstd[:, 0:n], op0=ALU.mult, op1=ALU.mult), R=[b_lt[i], b_r, B_mod], W=[b_dst])

            K.op(ACT, lambda: nc.scalar.activation(out=krsq[0:64, :], in_=krS[:], func=AF.Square), R=[b_tab], W=[b_kr])

            def rope_apply(srcf, srcb, b_src, dst, b_dst):
                for (off, n) in TILES5:
                    bb, ps = bank()
                    K.op(PE, lambda off=off, n=n, ps=ps: nc.tensor.matmul(ps[0:64, 0:n], Rmb[:, :], srcb[:, off:off + n],
                                                                           start=True, stop=True),
                         R=[b_src, B_const], W=[bb])
                    K.op(DVE, lambda off=off, n=n, ps=ps: nc.vector.tensor_tensor(
                        out=t2[:, off:off + n], in0=ps[0:64, 0:n], in1=sinS[:, off:off + n], op=ALU.mult),
                        R=[bb, b_tab], W=[b_t])
                K.op(DVE, lambda: nc.vector.tensor_tensor(out=t1[:], in0=srcf[:], in1=cosS[:], op=ALU.mult),
                     R=[b_src, b_tab, b_t], W=[b_t])
                K.op(DVE, lambda: nc.vector.tensor_tensor(out=dst[:], in0=t1[:], in1=t2[:], op=ALU.add),
                     R=[b_t], W=[b_dst])

            K.op(DVE, lambda: nc.vector.tensor_scalar(out=rawr[:], in0=krS[:], scalar1=hgrT[:, L + l:L + l + 1], scalar2=None,
                                                      op0=ALU.mult), R=[b_tab, B_mod], W=[b_rawr])
            K.op(ACT, lambda: nc.scalar.copy(out=rawrb[:], in_=rawr[:]), R=[b_rawr], W=[b_rawr])
            rope_apply(rawr, rawrb, b_rawr, krot, b_kr)
            K.barrier()
            st_in.close()
            K.es = outer
            if "mla_a" in dbg:
                return True

            b_hd = [Buf(), Buf()]
            khat = [K.sb(f"khat{i}", [128, T], BF16) for i in range(2)]
            krope = [K.sb(f"krope{i}", [128, T], BF16) for i in range(2)]
            qhat = [K.sb(f"qhat{i}", [128, T], BF16) for i in range(2)]
            qrope = [K.sb(f"qrope{i}", [128, T], BF16) for i in range(2)]
            for i in range(2):
                K.op(DVE, lambda i=i: nc.vector.memset(krope[i][64:128, :], 0.0), W=[b_hd[i]])
                K.op(DVE, lambda i=i: nc.vector.memset(qrope[i][64:128, :], 0.0), W=[b_hd[i]])
            vh = [K.sb(f"vh{i}", [128, NTT, 128], BF16) for i in range(2)]
            PT = [K.sb(f"PT{i}", [128, 512], BF16) for i in range(2)]
            b_PT = [Buf() for _ in range(2)]
            mstg = [K.sb("mstg", [128, T], BF16)] * 2
            b_mstg = [Buf()] * 2
            rl = K.sb("rl", [128, 512], F32)
            b_rl = Buf()
            npt = 0

            def head_norm(nope_w, rope_w, srcT, b_src, g_col, dst_hat, dst_rope, b_dst, rope_src):
                for ti, (off, n) in enumerate(TILES5):
                    bb, ps = bank()
                    for c in range(4):
                        K.op(PE, lambda c=c, off=off, n=n, ps=ps: nc.tensor.matmul(
                            ps[:, 0:n], nope_w(c), srcT[:, c, off:off + n], start=(c == 0), stop=(c == 3)),
                            R=[b_w, b_src], W=[bb], inc=(c == 3))
                    K.op(ACT, lambda off=off, n=n, ps=ps: nc.scalar.copy(out=raw[:, off:off + n], in_=ps[:, 0:n]),
                         R=[bb], W=[b_raw])
                    if rope_w is not None:
                        bb2, ps2 = bank()
                        for c in range(4):
                            K.op(PE, lambda c=c, off=off, n=n, ps2=ps2: nc.tensor.matmul(
                                ps2[0:64, 0:n], rope_w(c), srcT[:, c, off:off + n], start=(c == 0), stop=(c == 3)),
                                R=[b_w, b_src], W=[bb2], inc=(c == 3))
                        K.op(DVE, lambda off=off, n=n, ps2=ps2: nc.vector.tensor_scalar(
                            out=rawr[:, off:off + n], in0=ps2[0:64, 0:n], scalar1=hgrT[:, l:l + 1], scalar2=None,
                            op0=ALU.mult), R=[bb2, B_mod], W=[b_rawr])
                        K.op(ACT, lambda off=off, n=n, ps2=ps2: nc.scalar.activation(
                            out=sqr[0:64, off:off + n], in_=ps2[0:64, 0:n], func=AF.Square), R=[], W=[bb2, b_sq])
                K.op(ACT, lambda: nc.scalar.activation(out=sq[:], in_=raw[:], func=AF.Square), R=[b_raw], W=[b_sq])
                rsq = sqr if rope_w is not None else krsq
                for ti, (off, n) in enumerate(TILES5):
                    bb, ps = bank()
                    K.op(PE, lambda off=off, n=n, ps=ps: nc.tensor.matmul(ps[:, 0:n], onesb[:], sq[:, off:off + n],
                                                                           start=True, stop=False),
                         R=[b_sq, B_const], W=[bb], inc=False)
                    K.op(PE, lambda off=off, n=n, ps=ps: nc.tensor.matmul(ps[:, 0:n], onesb[:, :], rsq[:, off:off + n],
                                                                           start=False, stop=True),
                         R=[b_sq, b_kr, B_const], W=[bb])
                    K.op(ACT, lambda off=off, n=n, ps=ps: nc.scalar.activation(
                        out=r1[:, off:off + n], in_=ps[:, 0:n], func=AF.Sqrt, bias=EPS, scale=1.0 / 192), R=[bb], W=[b_r])
                K.op(DVE, lambda: nc.vector.reciprocal(out=rstd[:], in_=r1[:]), R=[b_r], W=[b_r])
                K.op(DVE, lambda: nc.vector.scalar_tensor_tensor(
                    out=dst_hat[:], in0=raw[:], scalar=g_col, in1=rstd[:], op0=ALU.mult, op1=ALU.mult),
                    R=[b_raw, b_r, B_mod], W=[b_dst])
                if rope_w is not None:
                    K.op(ACT, lambda: nc.scalar.copy(out=rawrb[:], in_=rawr[:]), R=[b_rawr], W=[b_rawr])
                    rope_apply(rawr, rawrb, b_rawr, t1, b_t)
                    K.op(DVE, lambda: nc.vector.tensor_tensor(out=dst_rope[0:64, :], in0=t1[:], in1=rstd[0:64, :], op=ALU.mult),
                         R=[b_t, b_r], W=[b_dst])
                else:
                    K.op(DVE, lambda: nc.vector.tensor_tensor(out=dst_rope[0:64, :], in0=krot[:], in1=rstd[0:64, :], op=ALU.mult),
                         R=[b_kr, b_r], W=[b_dst])

            for h in range(8):
                hb = h % 2
                head_norm(lambda c, h=h: wkv[:, c, h * 256:h * 256 + 128], None, kvnT, b_kvn,
                          hgnT[:, L + l:L + l + 1], khat[hb], krope[hb], b_hd[hb], None)
                if "mla_b1" in dbg:
                    K.barrier()
                    return True
                for g in range(5):
                    bb, ps = bank()
                    tts = list(range(g * 4, min(g * 4 + 4, NTT)))
                    for j, tt in enumerate(tts):
                        for c in range(4):
                            K.op(PE, lambda c=c, j=j, tt=tt, ps=ps, h=h: nc.tensor.matmul(
                                ps[:, j * 128:(j + 1) * 128], kvnT[:, c, tt * 128:(tt + 1) * 128],
                                wkv[:, c, h * 256 + 128:h * 256 + 256], start=(c == 0), stop=(c == 3)),
                                R=[b_w, b_kvn], W=[bb], inc=(c == 3 and j == len(tts) - 1))
                    nn = len(tts)
                    K.op(ACT, lambda g=g, nn=nn, ps=ps, hb=hb: nc.scalar.copy(
                        out=vh[hb][:, g * 4:g * 4 + nn, :], in_=ps[:, 0:nn * 128].rearrange("p (a b) -> p a b", b=128)),
                        R=[bb], W=[b_hd[hb]])
                if "mla_b2" in dbg:
                    K.barrier()
                    return True
                head_norm(lambda c, h=h: wq[:, c, h * 192:h * 192 + 128], lambda c, h=h: wq[:, c, h * 192 + 128:h * 192 + 192],
                          qnT, b_qn, hgnT[:, l:l + 1], qhat[hb], qrope[hb], b_hd[hb], True)
                if "mla_b" in dbg:
                    K.barrier()
                    return True
                ms = mstg[hb]
                for qi, (qoff, nq) in enumerate(TILES5):
                    kts = [0, 1] if qoff < TC else list(range(NTT))
                    bO, psO = bank(qi % 2)
                    bL, psL = bank(2 + qi % 2)

                    def s_mm(kt, slot):
                        bS, psS = bank(4 + slot % 4)
                        K.op(PE, lambda: nc.tensor.matmul(psS[:, 0:nq], khat[hb][:, kt * 128:(kt + 1) * 128],
                                                          qhat[hb][:, qoff:qoff + nq], start=True, stop=False),
                             R=[b_hd[hb]], W=[bS], inc=False)
                        K.op(PE, lambda: nc.tensor.matmul(psS[:, 0:nq], krope[hb][:, kt * 128:(kt + 1) * 128],
                                                          qrope[hb][:, qoff:qoff + nq], start=False, stop=True),
                             R=[b_hd[hb]], W=[bS])
                        return bS, psS
                    pend = s_mm(kts[0], 0)
                    for ki, kt in enumerate(kts):
                        bS, psS = pend
                        if ki + 1 < len(kts):
                            pend = s_mm(kts[ki + 1], ki + 1)
                        pi = npt % 2
                        npt += 1
                        K.op(ACT, lambda psS=psS, pi=pi: nc.scalar.activation(out=PT[pi][:, 0:nq], in_=psS[:, 0:nq], func=AF.Exp,
                                                                              scale=SCALE), R=[bS], W=[b_PT[pi]])
                        K.op(PE, lambda kt=kt, pi=pi, ki=ki: nc.tensor.matmul(psO[:, 0:nq], vh[hb][:, kt, :], PT[pi][:, 0:nq],
                                                                              start=(ki == 0), stop=(ki == len(kts) - 1)),
                             R=[b_hd[hb], b_PT[pi]], W=[bO], inc=False)
                        K.op(PE, lambda pi=pi, ki=ki: nc.tensor.matmul(psL[:, 0:nq], onesb[:], PT[pi][:, 0:nq],
                                                                       start=(ki == 0), stop=(ki == len(kts) - 1)),
                             R=[b_PT[pi], B_const], W=[bL])
                    K.op(DVE, lambda: nc.vector.reciprocal(out=rl[:, 0:nq], in_=psL[:, 0:nq]), R=[bL], W=[b_rl])
                    K.op(DVE, lambda: nc.vector.tensor_tensor(out=ms[:, qoff:qoff + nq], in0=psO[:, 0:nq], in1=rl[:, 0:nq],
                                                              op=ALU.mult), R=[bO, b_rl], W=[b_mstg[hb]])
                K.dma(SQ, mlaT[h], ms[:], R=[b_mstg[hb]], W=[B_z])
            K.barrier()

        def gla_stage(l):
            B_o = Buf()
            wg = K.sb("wg", [16, 2, 1024], BF16)
            lr = K.sb("lr", [16, 2, T], BF16)
            gon = K.sb("gon", [128, 256], F32)
            b_in = Buf()
            K.dma(GQ, wg[:], wg_d[l].rearrange("d k n -> k d n"), W=[b_in])
            K.dma(GQ, lr[:, 0, :], lrfT, W=[b_in], max_dma_last_dim=4096)
            K.dma(GQ, lr[:, 1, :], lrbT, W=[b_in], max_dma_last_dim=4096)
            K.dma(SQ, gon[:], gon_d[l], W=[b_in])
            rmask = K.sb("rmask", [128, T], F32)
            K.op(DVE, lambda: nc.vector.memset(rmask[:], 1.0), W=[b_in])
            K.op(DVE, lambda: nc.vector.memset(rmask[:, 0::128], 0.0), W=[b_in])
            Lf = K.sb("Lf", [128, T], F32)
            Cs = K.sb("Cs", [128, T], F32)
            E1 = K.sb("E1", [128, T], F32)
            E2 = Lf
            ctot = K.sb("ctot", [128, NTT], F32)
            b_L, b_C, b_E = Buf(), Buf(), Buf()
            gkl = K.sb("gkl", [128, T], BF16)
            gql = K.sb("gql", [128, T], BF16)
            b_gl = Buf()
            qt = [K.sb(f"qt{d}", [128, 2, T], BF16) for d in range(2)]
            kt = [K.sb(f"kt{d}", [128, 2, T], BF16) for d in range(2)]
            kh = [K.sb(f"kh{d}", [128, 2, T], BF16) for d in range(2)]
            EB = [K.sb(f"EB{d}", [128, 2, NTT], F32) for d in range(2)]
            b_qk = [Buf(), Buf()]
            vh = K.sb("gvh", [128, NTT, 256], BF16)
            ggh = K.sb("ggh", [128, NTT, 256], BF16)
            b_v = Buf()
            S = [K.sb(f"S{d}", [128, 2, 256], F32) for d in range(2)]
            Sb = [K.sb(f"Sb{d}", [128, 2, 256], BF16) for d in range(2)]
            b_S = [Buf(), Buf()]
            Am = [K.sb(f"Am{d}", [128, 128], BF16) for d in range(2)]
            b_Am = [Buf(), Buf()]
            khtok = [K.sb(f"khtok{d}", [128, 256], BF16) for d in range(2)]
            b_kht = [Buf(), Buf()]
            oacc = K.sb("oacc", [128, NTT, 256], F32)
            b_oa = Buf()
            sqo = K.sb("sqo", [128, NTT, 256], BF16)
            ss = K.sb("ss", [128, NTT], F32)
            b_ss = Buf()
            glatok = K.sb("glatok", [128, NTT, 256], BF16)
            b_gt = Buf()
            gstg = K.sb("gstg", [128, T], BF16)
            b_gs = Buf()
            order = [list(range(NTT)), [1, 0] + list(range(NTT - 1, 1, -1))]
            for h in range(4):
                K.dma(SQ, vh[:], gvtm[:, :, h * 256:(h + 1) * 256].rearrange("t p n -> p t n"), W=[b_v])
                K.dma(SQ, ggh[:], ggtm[:, :, h * 256:(h + 1) * 256].rearrange("t p n -> p t n"), W=[b_v])
                for d in range(2):
                    for dkc in range(2):
                        ch = h * 2 + dkc
                        K.dma(SQ, gkl[:], gkT[ch], W=[b_gl])
                        K.dma(SQ, gql[:], gqT[ch], W=[b_gl])
                        for (off, n) in TILES5:
                            bb, ps = bank()
                            K.op(PE, lambda: nc.tensor.matmul(ps[:, 0:n], wg[:, d, ch * 128:(ch + 1) * 128], lr[:, d, off:off + n],
                                                              start=True, stop=True), R=[b_in], W=[bb])
                            col = l * 16 + d * 8 + ch
                            K.op(ACT, lambda: nc.scalar.activation(out=Lf[:, off:off + n], in_=ps[:, 0:n], func=AF.Exp,
                                                                   bias=ngbT[:, col:col + 1], scale=-1.0), R=[bb, B_mod], W=[b_L])
                        K.op(ACT, lambda: nc.scalar.activation(out=Lf[:], in_=Lf[:], func=AF.Ln, bias=1.0, scale=1.0),
                             R=[b_L], W=[b_L])
                        K.op(DVE, lambda: nc.vector.tensor_tensor_scan(out=Cs[:], data0=rmask[:], data1=Lf[:], initial=0.0,
                                                                       op0=ALU.mult, op1=ALU.add), R=[b_L, b_in], W=[b_C])
                        if "gla_dbg" in dbg and h == 0 and d == 0 and dkc == 0:
                            K.dma(SQ, dL, Lf[:], R=[b_L], W=[Buf()])
                            K.dma(SQ, dC, Cs[:], R=[b_C], W=[Buf()])
                        if d == 1:
                            K.op(DVE, lambda: nc.vector.tensor_tensor(out=Lf[:], in0=Lf[:], in1=Cs[:], op=ALU.subtract),
                                 R=[b_C], W=[b_L])
                            K.op(DVE, lambda: nc.vector.tensor_copy(out=ctot[:], in_=Cs[:, 127::128]), R=[b_C], W=[b_L])
                            K.op(DVE, lambda: nc.vector.tensor_tensor(
                                out=Cs[:].rearrange("p (c t) -> p c t", t=128), in0=Lf[:].rearrange("p (c t) -> p c t", t=128),
                                in1=ctot[:].unsqueeze(2).to_broadcast([128, NTT, 128]), op=ALU.add),
                                R=[b_L], W=[b_C])
                        if "gla_dbg" in dbg and h == 0 and d == 1 and dkc == 0:
                            K.dma(SQ, dC1, Cs[:], R=[b_C], W=[Buf()])
                        K.op(ACT, lambda: nc.scalar.activation(out=E1[:], in_=Cs[:], func=AF.Exp, scale=-1.0 / 16), R=[b_C], W=[b_E])
                        K.op(ACT, lambda: nc.scalar.activation(out=E2[:], in_=Cs[:], func=AF.Exp, scale=1.0 / 16), R=[b_C], W=[b_E, b_L])
                        ebv = E1[:, 127::128] if d == 0 else E1[:, 0::128]
                        K.op(DVE, lambda: nc.vector.tensor_copy(out=EB[d][:, dkc, :], in_=ebv), R=[b_E], W=[b_qk[d]])
                        K.op(DVE, lambda: nc.vector.scalar_tensor_tensor(out=qt[d][:, dkc, :], in0=gql[:], scalar=1.0 / 16, in1=E1[:],
                                                                         op0=ALU.mult, op1=ALU.mult), R=[b_gl, b_E], W=[b_qk[d]])
                        K.op(DVE, lambda: nc.vector.tensor_tensor(out=kt[d][:, dkc, :], in0=gkl[:], in1=E2[:], op=ALU.mult),
                             R=[b_gl, b_E, b_L], W=[b_qk[d]])
                        K.op(DVE, lambda: nc.vector.tensor_tensor(
                            out=kh[d][:, dkc, :].rearrange("p (c t) -> p c t", t=128),
                            in0=kt[d][:, dkc, :].rearrange("p (c t) -> p c t", t=128),
                            in1=EB[d][:, dkc, :].unsqueeze(2).to_broadcast([128, NTT, 128]), op=ALU.mult),
                            R=[], W=[b_qk[d]])
                    K.op(DVE, lambda: nc.vector.memset(S[d][:], 0.0), W=[b_S[d]])
                    K.op(DVE, lambda: nc.vector.memset(Sb[d][:], 0.0), W=[b_S[d]])
                seen = set()
                for i in range(NTT):
                    for d in range(2):
                        c = order[d][i]
                        cs = slice(c * 128, (c + 1) * 128)
                        bA, psA = bank(0 + d)
                        for dkc in range(2):
                            K.op(PE, lambda: nc.tensor.matmul(psA[:, 0:128], kt[d][:, dkc, cs], qt[d][:, dkc, cs],
                                                              start=(dkc == 0), stop=(dkc == 1)), R=[b_qk[d]], W=[bA], inc=(dkc == 1))
                        K.op(DVE, lambda: nc.vector.tensor_tensor(out=Am[d][:], in0=psA[:, 0:128], in1=mskS[:, d, :], op=ALU.mult),
                             R=[bA, B_const], W=[b_Am[d]])
                        bO, psO = bank(2 + d)
                        for dkc in range(2):
                            K.op(PE, lambda: nc.tensor.matmul(psO[:, 0:256], qt[d][:, dkc, cs], Sb[d][:, dkc, :],
                                                              start=(dkc == 0), stop=False), R=[b_qk[d], b_S[d]], W=[bO], inc=False)
                        K.op(PE, lambda: nc.tensor.matmul(psO[:, 0:256], Am[d][:], vh[:, c, :], start=False, stop=True),
                             R=[b_Am[d], b_v], W=[bO])
                        if c not in seen:
                            seen.add(c)
                            K.op(DVE, lambda: nc.vector.tensor_copy(out=oacc[:, c, :], in_=psO[:, 0:256]), R=[bO], W=[b_oa])
                        else:
                            K.op(DVE, lambda: nc.vector.tensor_tensor(out=oacc[:, c, :], in0=psO[:, 0:256], in1=oacc[:, c, :],
                                                                      op=ALU.add), R=[bO], W=[b_oa])
                        bT, psT = bank(4 + d)
                        for dkc in range(2):
                            K.op(PE, lambda: nc.tensor.matmul(psT[:, dkc * 128:(dkc + 1) * 128], kh[d][:, dkc, cs], identb[:],
                                                              start=True, stop=True), R=[b_qk[d], B_const], W=[bT], inc=(dkc == 1))
                        K.op(ACT, lambda: nc.scalar.copy(out=khtok[d][:], in_=psT[:, 0:256]), R=[bT], W=[b_kht[d]])
                        bU, psU = bank(6 + d)
                        for dkc in range(2):
                            K.op(PE, lambda: nc.tensor.matmul(psU[:, dkc * 256:(dkc + 1) * 256], khtok[d][:, dkc * 128:(dkc + 1) * 128],
                                                              vh[:, c, :], start=True, stop=True), R=[b_kht[d], b_v], W=[bU],
                                 inc=(dkc == 1))
                        for dkc in range(2):
                            K.op(DVE, lambda: nc.vector.scalar_tensor_tensor(
                                out=S[d][:, dkc, :], in0=S[d][:, dkc, :], scalar=EB[d][:, dkc, c:c + 1],
                                in1=psU[:, dkc * 256:(dkc + 1) * 256], op0=ALU.mult, op1=ALU.add), R=[bU, b_qk[d]], W=[b_S[d]])
                        K.op(ACT, lambda: nc.scalar.copy(out=Sb[d][:], in_=S[d][:]), R=[], W=[b_S[d]])
                if "gla_dbg" in dbg and h == 0:
                    K.dma(SQ, dO, oacc[:], R=[b_oa], W=[Buf()])
                K.op(ACT, lambda: nc.scalar.activation(out=sqo[:], in_=oacc[:], func=AF.Square), R=[b_oa], W=[b_ss])
                K.op(DVE, lambda: nc.vector.tensor_reduce(out=ss[:], in_=sqo[:], axis=AX.X, op=ALU.add), R=[], W=[b_ss])
                K.op(ACT, lambda: nc.scalar.activation(out=ss[:], in_=ss[:], func=AF.Sqrt, bias=EPS, scale=1.0 / 256), R=[], W=[b_ss])
                K.op(DVE, lambda: nc.vector.reciprocal(out=ss[:], in_=ss[:]), R=[], W=[b_ss])
                K.op(DVE, lambda: nc.vector.tensor_tensor(out=oacc[:], in0=oacc[:], in1=ss[:].unsqueeze(2).to_broadcast([128, NTT, 256]),
                                                          op=ALU.mult), R=[b_ss], W=[b_oa])
                K.op(DVE, lambda: nc.vector.tensor_tensor(out=oacc[:], in0=oacc[:],
                                                          in1=gon[:].unsqueeze(1).to_broadcast([128, NTT, 256]), op=ALU.mult),
                     R=[b_in], W=[b_oa])
                K.op(ACT, lambda: nc.scalar.activation(out=sqo[:], in_=ggh[:], func=AF.Silu), R=[b_v], W=[b_ss])
                K.op(DVE, lambda: nc.vector.tensor_tensor(out=glatok[:], in0=oacc[:], in1=sqo[:], op=ALU.mult),
                     R=[b_oa, b_ss], W=[b_gt])
                for dvc in range(2):
                    for g in range(5):
                        tts = list(range(g * 4, min(g * 4 + 4, NTT)))
                        bb, ps = bank()
                        for j, tt in enumerate(tts):
                            K.op(PE, lambda: nc.tensor.matmul(ps[:, j * 128:(j + 1) * 128], glatok[:, tt, dvc * 128:(dvc + 1) * 128],
                                                              identb[:], start=True, stop=True), R=[b_gt, B_const], W=[bb],
                                 inc=(j == len(tts) - 1))
                        nn = len(tts) * 128
                        K.op(ACT, lambda: nc.scalar.copy(out=gstg[:, g * 512:g * 512 + nn], in_=ps[:, 0:nn]), R=[bb], W=[b_gs])
                    K.dma(SQ, glaT[h * 2 + dvc], gstg[:], R=[b_gs], W=[B_o])
            K.barrier()

        def pool_stage(l):
            B_o = Buf()
            WIN = (2, 4, 8, 16)
            pin = K.sb("pin", [128, T], BF16)
            b_pin = Buf()
            Pc = K.sb("Pc", [128, TC + 16], F32)
            Pl = K.sb("Pl", [128, TL + 16], F32)
            b_P = Buf()
            K.op(DVE, lambda: nc.vector.memset(Pc[:], 0.0), W=[b_P])
            K.op(DVE, lambda: nc.vector.memset(Pl[:], 0.0), W=[b_P])
            qa = K.sb("qa", [128, TL + 16], F32)
            qb = K.sb("qb", [128, TL + 16], F32)
            b_q = Buf()
            icnt = K.sb("icnt", [128, T], F32)
            b_ic = Buf()
            pooled = K.sb("pooled", [128, 8, T], BF16)
            b_pl = Buf()
            pw = K.sb("pw", [128, 2, 256], BF16)
            b_pw = Buf()
            pstg = K.sb("pstg", [128, T], BF16)
            b_ps = Buf()
            for ch in range(8):
                g = ch // 2
                w = WIN[g]
                K.dma(SQ, pin[:], pinT[ch], W=[b_pin])
                if ch % 2 == 0:
                    K.dma(SQ, icnt[:], invc_d[g], W=[b_ic])
                K.op(ACT, lambda: nc.scalar.copy(out=Pc[:, 8:8 + TC], in_=pin[:, 0:TC]), R=[b_pin], W=[b_P])
                K.op(ACT, lambda: nc.scalar.copy(out=Pl[:, 8:8 + TL], in_=pin[:, TC:T]), R=[b_pin], W=[b_P])
                for (P, n, off) in ((Pc, TC, 0), (Pl, TL, TC)):
                    cur, ln = P, n + 16
                    step = 1
                    nxts = [qa, qb]
                    ni = 0
                    while step < w:
                        nx = nxts[ni % 2]
                        ni += 1
                        K.op(DVE, lambda cur=cur, nx=nx, ln=ln, step=step: nc.vector.tensor_tensor(
                            out=nx[:, 0:ln - step], in0=cur[:, 0:ln - step], in1=cur[:, step:ln], op=ALU.add), R=[b_P], W=[b_q])
                        cur, ln = nx, ln - step
                        step *= 2
                    o0 = 8 - w // 2
                    other = nxts[ni % 2]
                    K.op(DVE, lambda cur=cur, other=other, o0=o0, n=n, off=off: nc.vector.tensor_tensor(
                        out=other[:, 0:n], in0=cur[:, o0:o0 + n], in1=icnt[:, off:off + n], op=ALU.mult), R=[b_ic], W=[b_q])
                    K.op(DVE, lambda other=other, P=P, n=n, off=off: nc.vector.tensor_tensor(
                        out=pooled[:, ch, off:off + n], in0=other[:, 0:n], in1=P[:, 8:8 + n], op=ALU.subtract), R=[b_P], W=[b_q, b_pl])
            for g in range(4):
                K.dma(GQ, pw[:], pool_w[l, g].rearrange("(k p) n -> p k n", p=128), W=[b_pw])
                for dc in range(2):
                    for (off, n) in TILES5:
                        bb, ps = bank()
                        for cc in range(2):
                            K.op(PE, lambda: nc.tensor.matmul(ps[:, 0:n], pw[:, cc, dc * 128:(dc + 1) * 128], pooled[:, g * 2 + cc, off:off + n],
                                                              start=(cc == 0), stop=(cc == 1)), R=[b_pw, b_pl], W=[bb], inc=(cc == 1))
                        col = l * 8 + g * 2 + dc
                        K.op(ACT, lambda: nc.scalar.activation(out=pstg[:, off:off + n], in_=ps[:, 0:n], func=AF.Copy,
                                                               scale=pscT[:, col:col + 1]), R=[bb, B_mod], W=[b_ps])
                    K.dma(SQ, poolT[g * 2 + dc], pstg[:], R=[b_ps], W=[B_o])
            K.barrier()

        GT = 576
        GROUPS = [(0, 576, [(0, 256, 1), (256, 320, 0)])] + [(576 * i, 576, [(0, 288, 0), (288, 288, 0)]) for i in range(1, 4)]
        def tl_stage(l):
            xT_v = xT.rearrange("c p t -> p c t")
            wall = K.sb("tw", [128, 4 * 8192], BF16)
            wsl = [wall[:, i * 8192:(i + 1) * 8192] for i in range(4)]
            b_wsl = [Buf() for _ in range(4)]
            b_w2 = [Buf(), Buf()]
            nld = [0]
            nld2 = [0]

            def wload(src_ap, kc, cw):
                i = nld[0] % 4
                nld[0] += 1
                v = wsl[i][:, 0:kc * cw].rearrange("p (k n) -> p k n", n=cw)
                K.dma(GQ, v, src_ap.rearrange("(k p) n -> p k n", p=128), W=[b_wsl[i], b_w2[0], b_w2[1]])
                return v, b_wsl[i]

            def w2load(src_ap):
                i = nld2[0] % 2
                nld2[0] += 1
                v = wall[:, i * 11264:(i + 1) * 11264].rearrange("p (k n) -> p k n", n=256)
                K.dma(GQ, v, src_ap.rearrange("(k p) n -> p k n", p=128), W=[b_w2[i], b_wsl[0], b_wsl[1], b_wsl[2], b_wsl[3]])
                return v, b_w2[i]
            xg = K.sb("xg", [128, DC, GT], F32)
            b_xg = Buf()
            mT = K.sb("mT", [128, DC, GT], BF16)
            b_mT = Buf()
            for (g0, gsz, subs) in GROUPS:
                K.dma(SQ, xg[:, :, 0:gsz], xT_v[:, :, g0:g0 + gsz], R=[B_xT], W=[b_xg])
                with ExitStack() as sti:
                    K.es = sti
                    br = [K.sb(f"br{n}", [128, 8, GT], BF16) for n in range(3)]
                    b_br = Buf()
                    for n, src in enumerate((poolT, mlaT, glaT)):
                        K.dma(SQ, br[n][:, :, 0:gsz], src.rearrange("c p t -> p c t")[:, :, g0:g0 + gsz], W=[b_br])
                    gt = [K.sb(f"gt{i}", [128, 4, GT], BF16) for i in range(2)]
                    b_gt = [Buf(), Buf()]
                    macc = K.sb("macc", [128, 4, GT], F32)
                    mtmp = K.sb("mtmp", [128, 320], F32)
                    b_ma = Buf()
                    ng = 0
                    for cg in range(4):
                        for n in range(3):
                            wv, bw = wload(w_branch[l, n, :, cg * 512:(cg + 1) * 512], 8, 512)
                            gi2 = ng % 2
                            ng += 1
                            K.dma(SQ, gt[gi2][:, :, 0:gsz], gateT[n * 16 + cg * 4:n * 16 + cg * 4 + 4].rearrange("c p t -> p c t")[:, :, g0:g0 + gsz],
                                  W=[b_gt[gi2]])
                            for j in range(4):
                                dch = cg * 4 + j
                                for (so, sn, jj) in subs:
                                    bb, ps = bank()
                                    for c in range(8):
                                        K.op(PE, lambda: nc.tensor.matmul(ps[:, 0:sn], wv[:, c, j * 128:(j + 1) * 128], br[n][:, c, so:so + sn],
                                                                          start=(c == 0), stop=(c == 7)), R=[bw, b_br], W=[bb], inc=(c == 7))
                                    if n == 0:
                                        K.op(DVE, lambda: nc.vector.tensor_tensor(out=macc[:, j, so:so + sn], in0=ps[:, 0:sn],
                                                                                  in1=gt[gi2][:, j, so:so + sn], op=ALU.mult),
                                             R=[bb, b_gt[gi2]], W=[b_ma])
                                    else:
                                        K.op(DVE, lambda: nc.vector.tensor_tensor(out=mtmp[:, 0:sn], in0=ps[:, 0:sn],
                                                                                  in1=gt[gi2][:, j, so:so + sn], op=ALU.mult),
                                             R=[bb, b_gt[gi2]], W=[b_ma])
                                        if n == 1:
                                            K.op(DVE, lambda: nc.vector.tensor_tensor(out=macc[:, j, so:so + sn], in0=macc[:, j, so:so + sn],
                                                                                      in1=mtmp[:, 0:sn], op=ALU.add), R=[], W=[b_ma])
                                        else:
                                            K.op(DVE, lambda: nc.vector.tensor_tensor(out=mT[:, dch, so:so + sn], in0=macc[:, j, so:so + sn],
                                                                                      in1=mtmp[:, 0:sn], op=ALU.add), R=[b_ma], W=[b_mT])
                    K.barrier(only=[K.PE, K.ACT, K.DVE, K.SQ])
                K.es = tl_es[0]
                for cg in range(4):
                    wv, bw = wload(w_out[l, :, cg * 512:(cg + 1) * 512], 16, 512)
                    for j in range(4):
                        dch = cg * 4 + j
                        for (so, sn, jj) in subs:
                            bb, ps = bank()
                            for k in range(16):
                                K.op(PE, lambda: nc.tensor.matmul(ps[:, 0:sn], wv[:, k, j * 128:(j + 1) * 128], mT[:, k, so:so + sn],
                                                                  start=(k == 0), stop=(k == 15)), R=[bw, b_mT], W=[bb], inc=(k == 15))
                            K.op(DVE, lambda: nc.vector.scalar_tensor_tensor(
                                out=xg[:, dch, so:so + sn], in0=ps[:, 0:sn], scalar=modT[:, l, 32 + dch, jj:jj + 1],
                                in1=xg[:, dch, so:so + sn], op0=ALU.mult, op1=ALU.add), R=[bb, B_mod], W=[b_xg])
                with ExitStack() as sti:
                    K.es = sti
                    sqb = K.sb("f_sq", [128, DC, 320], BF16)
                    rs = K.sb("f_rs", [128, 320], F32)
                    tmp = [K.sb(f"f_t{i}", [128, 320], F32) for i in range(2)]
                    b_sq, b_rs, b_tm = Buf(), Buf(), [Buf(), Buf()]
                    for (so, sn, jj) in subs:
                        K.op(ACT, lambda: nc.scalar.activation(out=sqb[:, :, 0:sn], in_=xg[:, :, so:so + sn], func=AF.Square),
                             R=[b_xg], W=[b_sq])
                        bb, ps = bank()
                        for c in range(DC):
                            K.op(PE, lambda: nc.tensor.matmul(ps[:, 0:sn], onesb[:], sqb[:, c, 0:sn], start=(c == 0), stop=(c == DC - 1)),
                                 R=[b_sq, B_const], W=[bb], inc=(c == DC - 1))
                        K.op(ACT, lambda: nc.scalar.activation(out=rs[:, 0:sn], in_=ps[:, 0:sn], func=AF.Sqrt, bias=EPS, scale=1.0 / D),
                             R=[bb], W=[b_rs])
                        K.op(DVE, lambda: nc.vector.reciprocal(out=rs[:, 0:sn], in_=rs[:, 0:sn]), R=[], W=[b_rs])
                        for c in range(DC):
                            q = c % 2
                            K.op(DVE, lambda: nc.vector.scalar_tensor_tensor(
                                out=tmp[q][:, 0:sn], in0=xg[:, c, so:so + sn], scalar=A2[:, l, c, jj:jj + 1], in1=rs[:, 0:sn],
                                op0=ALU.mult, op1=ALU.mult), R=[b_xg, b_rs, B_mod], W=[b_tm[q]])
                            K.op(ACT, lambda: nc.scalar.activation(out=mT[:, c, so:so + sn], in_=tmp[q][:, 0:sn], func=AF.Identity,
                                                                   bias=modT[:, l, 48 + c, jj:jj + 1], scale=1.0),
                                 R=[b_tm[q], B_mod], W=[b_mT])
                    uT = K.sb("uT", [128, FC, GT], BF16)
                    b_uT = Buf()
                    s1 = [K.sb(f"s1_{i}", [128, 320], F32) for i in range(2)]
                    b_s1 = [Buf(), Buf()]
                    ns = 0
                    for fg in range(FC // 4):
                        w1v, bw1 = wload(ffn_w1[l, :, fg * 512:(fg + 1) * 512], 16, 512)
                        w3v, bw3 = wload(ffn_w3[l, :, fg * 512:(fg + 1) * 512], 16, 512)
                        for j in range(4):
                            f = fg * 4 + j
                            for (so, sn, jj) in subs:
                                b1, p1 = bank()
                                b3, p3 = bank()
                                for k in range(16):
                                    K.op(PE, lambda: nc.tensor.matmul(p1[:, 0:sn], w1v[:, k, j * 128:(j + 1) * 128], mT[:, k, so:so + sn],
                                                                      start=(k == 0), stop=(k == 15)), R=[bw1, b_mT], W=[b1], inc=(k == 15))
                                for k in range(16):
                                    K.op(PE, lambda: nc.tensor.matmul(p3[:, 0:sn], w3v[:, k, j * 128:(j + 1) * 128], mT[:, k, so:so + sn],
                                                                      start=(k == 0), stop=(k == 15)), R=[bw3, b_mT], W=[b3], inc=(k == 15))
                                si = ns % 2
                                ns += 1
                                K.op(ACT, lambda: nc.scalar.activation(out=s1[si][:, 0:sn], in_=p1[:, 0:sn], func=AF.Silu),
                                     R=[b1], W=[b_s1[si]])
                                K.op(DVE, lambda: nc.vector.tensor_tensor(out=uT[:, f, so:so + sn], in0=p3[:, 0:sn], in1=s1[si][:, 0:sn],
                                                                          op=ALU.mult), R=[b3, b_s1[si]], W=[b_uT])
                    for dch in range(DC):
                        if dch % 2 == 0:
                            wv, bw = w2load(ffn_w2[l, :, dch * 128:(dch + 2) * 128])
                        jo = (dch % 2) * 128
                        for (so, sn, jj) in subs:
                            bb, ps = bank()
                            for f in range(FC):
                                K.op(PE, lambda: nc.tensor.matmul(ps[:, 0:sn], wv[:, f, jo:jo + 128], uT[:, f, so:so + sn],
                                                                  start=(f == 0), stop=(f == FC - 1)), R=[bw, b_uT], W=[bb], inc=(f == FC - 1))
                            K.op(DVE, lambda: nc.vector.scalar_tensor_tensor(
                                out=xg[:, dch, so:so + sn], in0=ps[:, 0:sn], scalar=modT[:, l, 80 + dch, jj:jj + 1],
                                in1=xg[:, dch, so:so + sn], op0=ALU.mult, op1=ALU.add), R=[bb, B_mod], W=[b_xg])
                    K.dma(SQ, xT_v[:, :, g0:g0 + gsz], xg[:, :, 0:gsz], R=[b_xg], W=[B_xT])
                    K.barrier(only=[K.PE, K.ACT, K.DVE, K.SQ])
                K.es = tl_es[0]
            K.barrier()
        tl_es = [None]

        def out_stage():
            xT_v = xT.rearrange("c p t -> p c t")
            xt = [K.sb(f"o_x{i}", [128, DC, 128], F32) for i in range(2)]
            og = [K.sb(f"o_g{i}", [128, D], F32) for i in range(2)]
            b_xt, b_og = [Buf(), Buf()], [Buf(), Buf()]
            B_out = Buf()
            for tt in range(2, NTT):
                i = tt % 2
                K.dma(SQ, xt[i][:], xT_v[:, :, tt * 128:(tt + 1) * 128], R=[B_xT], W=[b_xt[i]])
                for g in range(4):
                    bb, ps = bank()
                    for j in range(4):
                        c = g * 4 + j
                        K.op(PE, lambda: nc.tensor.matmul(ps[:, j * 128:(j + 1) * 128], xt[i][:, c, :], ident[:], start=True, stop=True),
                             R=[b_xt[i], B_const], W=[bb], inc=(j == 3))
                    if g % 2 == 0:
                        K.op(ACT, lambda: nc.scalar.copy(out=og[i][:, g * 512:(g + 1) * 512], in_=ps[:, :]), R=[bb], W=[b_og[i]])
                    else:
                        K.op(DVE, lambda: nc.vector.tensor_copy(out=og[i][:, g * 512:(g + 1) * 512], in_=ps[:, :]), R=[bb], W=[b_og[i]])
                K.dma(SQ, out_d[(tt - 2) * 128:(tt - 1) * 128, :], og[i][:], R=[b_og[i]], W=[B_out])
            K.barrier()

        for l in range(nlayers):
            with ExitStack() as st:
                K.es = st
                hT = K.sb("hT", [128, DC, T], BF16)
                with ExitStack() as st1:
                    K.es = st1
                    norm_mod(nc, K, bank, xT, B_xT, hT, B_hT, A1, modT, 0, l, B_mod, onesb, B_const, TILES5)
                    K.barrier()
                K.es = st
                stg32 = [K.sb(f"stg32_{i}", [128, T], F32) for i in range(2)]
                stg16 = [K.sb(f"stg16_{i}", [128, T], BF16) for i in range(3)]
                b_s32 = [Buf() for _ in range(2)]
                b_s16 = [Buf() for _ in range(3)]
                wsl = [K.sb(f"win{i}", [128, 16, 512], BF16) for i in range(3)]
                b_wsl = [Buf() for _ in range(3)]
                cnt = {"ld": 0, "s32": 0, "s16": 0, "ev": 0}
                B_z = Buf("z")

                def fm_group(col0, width, chunks):
                    i = cnt["ld"] % 3
                    cnt["ld"] += 1
                    K.dma(GQ, wsl[i][:, :, 0:width], w_in[l, :, col0:col0 + width].rearrange("(k p) n -> p k n", p=128),
                          W=[b_wsl[i]])
                    for (co, M, dst, dt, func) in chunks:
                        bks = [bank() for _ in TILES5]
                        for k in range(16):
                            for ti, (off, n) in enumerate(TILES5):
                                bb, ps = bks[ti]
                                K.op(PE, lambda i=i, k=k, co=co, M=M, off=off, n=n, ps=ps: nc.tensor.matmul(
                                    ps[0:M, 0:n], wsl[i][:, k, co:co + M], hT[:, k, off:off + n],
                                    start=(k == 0), stop=(k == 15)),
                                    R=[b_wsl[i], B_hT], W=[bb], inc=(k == 15))
                        if dt is F32:
                            si = cnt["s32"] % 2
                            cnt["s32"] += 1
                            stg, bs = stg32[si], b_s32[si]
                        else:
                            si = cnt["s16"] % 3
                            cnt["s16"] += 1
                            stg, bs = stg16[si], b_s16[si]
                        for ti, (off, n) in enumerate(TILES5):
                            bb, ps = bks[ti]
                            useact = (func is not None) or (cnt["ev"] % 2 == 0)
                            cnt["ev"] += 1
                            if useact:
                                K.op(ACT, lambda M=M, off=off, n=n, ps=ps, stg=stg, func=func: nc.scalar.activation(
                                    out=stg[0:M, off:off + n], in_=ps[0:M, 0:n], func=(func or AF.Copy)), R=[bb], W=[bs])
                            else:
                                K.op(DVE, lambda M=M, off=off, n=n, ps=ps, stg=stg: nc.vector.tensor_copy(
                                    out=stg[0:M, off:off + n], in_=ps[0:M, 0:n]), R=[bb], W=[bs])
                        K.dma(SQ, dst, stg[0:M, :], R=[bs], W=[B_z])

                fm_group(C_CKV, 512, [(j * 128, 128, ckvT[j], F32, None) for j in range(4)])
                fm_group(C_KR, 64, [(0, 64, krT, F32, None)])
                fm_group(C_LRF, 32, [(0, 16, lrfT, F32, None), (16, 16, lrbT, F32, None)])
                fm_group(C_CQ, 512, [(j * 128, 128, cqT[j], F32, None) for j in range(4)])
                for g in range(2):
                    fm_group(C_GK + g * 512, 512, [(j * 128, 128, gkT[g * 4 + j], BF16, None) for j in range(4)])
                for g in range(2):
                    fm_group(C_GQ + g * 512, 512, [(j * 128, 128, gqT[g * 4 + j], BF16, None) for j in range(4)])
                for g in range(2):
                    fm_group(C_PIN + g * 512, 512, [(j * 128, 128, pinT[g * 4 + j], BF16, None) for j in range(4)])
                for g in range(12):
                    fm_group(C_GATE + g * 512, 512, [(j * 128, 128, gateT[g * 4 + j], BF16, AF.Sigmoid) for j in range(4)])
                wtm = K.sb("wtm", [128, 16, 1024], BF16)
                b_wtm = Buf()
                stm = [K.sb(f"stm{i}", [128, 1024], BF16) for i in range(2)]
                b_stm = [Buf(), Buf()]
                for (c0, dst) in ((C_GV, gvtm), (C_GG, ggtm)):
                    for hf in range(2):
                        K.dma(GQ, wtm[:, :, hf * 512:(hf + 1) * 512],
                              w_in[l, :, c0 + hf * 512:c0 + (hf + 1) * 512].rearrange("(k p) n -> p k n", p=128), W=[b_wtm])
                    for tt in range(NTT):
                        si = tt % 2
                        for hf in range(2):
                            bb, ps = bank()
                            for k in range(16):
                                K.op(PE, lambda k=k, tt=tt, hf=hf, ps=ps: nc.tensor.matmul(
                                    ps[:, :], hT[:, k, tt * 128:(tt + 1) * 128], wtm[:, k, hf * 512:(hf + 1) * 512],
                                    start=(k == 0), stop=(k == 15)), R=[b_wtm, B_hT], W=[bb], inc=(k == 15))
                            if hf == 0:
                                K.op(ACT, lambda ps=ps, si=si: nc.scalar.copy(out=stm[si][:, 0:512], in_=ps[:, :]),
                                     R=[bb], W=[b_stm[si]])
                            else:
                                K.op(DVE, lambda ps=ps, si=si: nc.vector.tensor_copy(out=stm[si][:, 512:1024], in_=ps[:, :]),
                                     R=[bb], W=[b_stm[si]])
                        K.dma(SQ, dst[tt], stm[si][:], R=[b_stm[si]], W=[B_z])
                K.barrier()
            K.es = es
            if "stop_l2" in dbg:
                return finish(nc, K, out_d, modT, dbg)
            with ExitStack() as st:
                K.es = st
                stop = mla_stage(l)
            K.es = es
            if stop or "stop_mla" in dbg:
                return finish(nc, K, out_d, modT, dbg)
            with ExitStack() as st:
                K.es = st
                gla_stage(l)
            K.es = es
            if "stop_gla" in dbg:
                return finish(nc, K, out_d, modT, dbg)
            with ExitStack() as st:
                K.es = st
                pool_stage(l)
            K.es = es
            if "stop_pool" in dbg:
                return finish(nc, K, out_d, modT, dbg)
            with ExitStack() as st:
                K.es = st
                tl_es[0] = st
                tl_stage(l)
            K.es = es
            if "stop_tl" in dbg:
                return finish(nc, K, out_d, modT, dbg)
        with ExitStack() as st:
            K.es = st
            out_stage()
        K.es = es

        return finish(nc, K, out_d, modT, dbg)


def norm_mod(nc, K, bank, xT, B_xT, hT, B_hT, Ax, modT, shift_lo, l, B_mod, onesb, B_const, tiles):
    PE, ACT, DVE, SQ = K.PE, K.ACT, K.DVE, K.SQ
    xt = [K.sb(f"nm_x{i}", [128, DC, 512], F32) for i in range(2)]
    b_xt = [Buf(), Buf()]
    sqb = K.sb("nm_sq", [128, DC, 512], BF16)
    b_sq = Buf()
    r1 = K.sb("nm_r1", [128, 512], F32)
    rstd = K.sb("nm_rstd", [128, 512], F32)
    b_r = Buf()
    tmp = [K.sb(f"nm_t{i}", [128, 512], F32) for i in range(2)]
    b_tmp = [Buf(), Buf()]
    xT_v = xT.rearrange("c p t -> p c t")
    for ti, (off, n) in enumerate(tiles):
        i = ti % 2
        j = 1 if off < TC else 0
        K.dma(SQ, xt[i][:, :, 0:n], xT_v[:, :, off:off + n], R=[B_xT], W=[b_xt[i]])
        K.op(ACT, lambda i=i, n=n: nc.scalar.activation(out=sqb[:, :, 0:n], in_=xt[i][:, :, 0:n], func=AF.Square),
             R=[b_xt[i]], W=[b_sq])
        bb, ps = bank()
        for c in range(DC):
            K.op(PE, lambda c=c, n=n, ps=ps: nc.tensor.matmul(ps[:, 0:n], onesb[:], sqb[:, c, 0:n],
                                                               start=(c == 0), stop=(c == DC - 1)),
                 R=[b_sq, B_const], W=[bb], inc=(c == DC - 1))
        K.op(ACT, lambda n=n, ps=ps: nc.scalar.activation(out=r1[:, 0:n], in_=ps[:, 0:n], func=AF.Sqrt,
                                                          bias=EPS, scale=1.0 / D), R=[bb], W=[b_r])
        K.op(DVE, lambda n=n: nc.vector.reciprocal(out=rstd[:, 0:n], in_=r1[:, 0:n]), R=[b_r], W=[b_r])
        for c in range(DC):
            q = c % 2
            K.op(DVE, lambda c=c, n=n, i=i, q=q, j=j: nc.vector.scalar_tensor_tensor(
                out=tmp[q][:, 0:n], in0=xt[i][:, c, 0:n], scalar=Ax[:, l, c, j:j + 1], in1=rstd[:, 0:n],
                op0=ALU.mult, op1=ALU.mult), R=[b_xt[i], b_r, B_mod], W=[b_tmp[q]])
            K.op(ACT, lambda c=c, n=n, q=q, j=j, off=off: nc.scalar.activation(
                out=hT[:, c, off:off + n], in_=tmp[q][:, 0:n], func=AF.Identity,
                bias=modT[:, l, shift_lo + c, j:j + 1], scale=1.0), R=[b_tmp[q], B_mod], W=[B_hT])


def finish(nc, K, out_d, modT, dbg):
    B = Buf()
    if "no_out" not in dbg:
        pass
    K.barrier()
    return nc


def prep_inputs(inp, b):
    f = lambda a: np.ascontiguousarray(a, dtype=np.float32)
    m = {}
    m["xin"] = f(np.concatenate([inp["ctx"][b], inp["x"][b]], 0))
    m["cvec"] = f(np.concatenate([inp["c"][b].reshape(16, 128), inp["c_ctx"].reshape(16, 128)], 0))
    m["w_mod"] = f(inp["w_mod"])
    m["b_mod"] = f(inp["b_mod"].reshape(-1, 128))
    m["norm1_g"] = f(inp["norm1_g"].reshape(-1, 128))
    m["norm2_g"] = f(inp["norm2_g"].reshape(-1, 128))
    m["w_in"] = f(inp["w_in"])
    m["w_kv_up"] = f(inp["mla_w_kv_up"])
    m["w_q_up"] = f(inp["mla_w_q_up"])
    m["qng"] = f(inp["mla_q_norm_g"].reshape(-1, 128))
    m["kvng"] = f(inp["mla_kv_norm_g"].reshape(-1, 128))
    m["hg_n"] = f(np.concatenate([inp["mla_q_head_g"][:, :128], inp["mla_k_head_g"][:, :128]], 0))
    m["hg_r"] = f(np.concatenate([inp["mla_q_head_g"][:, 128:], inp["mla_k_head_g"][:, 128:]], 0))
    m["gla_wg"] = f(inp["gla_w_gate_up"])
    m["gla_b"] = f(inp["gla_b_gate"].reshape(-1, 128))
    m["gon_rep"] = f(np.broadcast_to(inp["gla_out_norm_g"][:, None, :], (L, 128, 256)))
    m["pool_w"] = f(inp["pool_w"])
    m["pool_scale"] = f(inp["pool_scale"].reshape(-1, 128))
    m["w_branch"] = f(inp["w_branch"])
    m["w_out"] = f(inp["w_out"])
    m["ffn_w1"] = f(inp["ffn_w1"])
    m["ffn_w3"] = f(inp["ffn_w3"])
    m["ffn_w2"] = f(inp["ffn_w2"])
    return m


NCORES = 4


def kernel(**inputs):
    inp = {k: np.asarray(v) for k, v in inputs.items()}
    consts = host_consts()
    nc = build(L)
    shared = None
    in_maps = []
    for core in range(NCORES):
        b = core % 4
        if core < 4:
            m = prep_inputs(inp, b)
            if shared is None:
                shared = m
            else:
                for k in m:
                    if k not in ("xin", "cvec"):
                        m[k] = shared[k]
            m.update(consts)
            in_maps.append(m)
        else:
            in_maps.append(in_maps[b])
    res = run_bass_kernel_spmd(nc, in_maps, core_ids=list(range(NCORES)))
    out = np.stack([np.asarray(res.results[b]["out"], dtype=np.float32) for b in range(4)], 0)
    return out


def host_consts():
    c = {}
    half = 32
    inv_freq = 1.0 / (10000.0 ** (np.arange(0, half, 2, dtype=np.float32) / half))
    row = np.repeat(np.arange(TL // 64), 64).astype(np.float32)
    col = np.tile(np.arange(64), TL // 64).astype(np.float32)
    ang_r = row[:, None] * inv_freq[None, :]
    ang_c = col[:, None] * inv_freq[None, :]
    cosL = np.concatenate([np.cos(ang_r), np.cos(ang_r), np.cos(ang_c), np.cos(ang_c)], 1)
    sinL = np.concatenate([np.sin(ang_r), np.sin(ang_r), np.sin(ang_c), np.sin(ang_c)], 1)
    cosT = np.concatenate([np.ones((TC, 64), np.float32), cosL.astype(np.float32)], 0).T
    sinT = np.concatenate([np.zeros((TC, 64), np.float32), sinL.astype(np.float32)], 0).T
    c["cosT"] = np.ascontiguousarray(cosT, dtype=np.float32)
    c["sinT"] = np.ascontiguousarray(sinT, dtype=np.float32)
    Rm = np.zeros((64, 64), np.float32)
    for g in (0, 32):
        for i in range(16):
            Rm[g + 16 + i, g + i] = -1.0
            Rm[g + i, g + 16 + i] = 1.0
    c["Rm"] = Rm
    c["ident_in"] = np.eye(128, dtype=np.float32)
    s_idx = np.arange(128)[:, None]
    t_idx = np.arange(128)[None, :]
    c["masks"] = np.stack([(s_idx <= t_idx), (s_idx >= t_idx)], 0).astype(np.float32)
    inv = np.zeros((4, T), np.float32)
    for gi, w in enumerate((2, 4, 8, 16)):
        for (o, n) in ((0, TC), (TC, TL)):
            t = np.arange(n)
            lo = np.clip(t - w // 2, 0, n - 1)
            hi = np.clip(t + w // 2 - 1, 0, n - 1)
            inv[gi, o:o + n] = 1.0 / (hi - lo + 1)
    c["invcnt"] = np.ascontiguousarray(np.broadcast_to(inv[:, None, :], (4, 128, T)), dtype=np.float32)
    return c
```

```python
import numpy as np
from contextlib import ExitStack
import concourse.bass as bass
import concourse.mybir as mybir
from concourse.bass_utils import run_bass_kernel_spmd

F32, BF16 = mybir.dt.float32, mybir.dt.bfloat16
AF = mybir.ActivationFunctionType
ALU = mybir.AluOpType
AX = mybir.AxisListType

D = 2048
DC = 16
TC = 256
TL = 2048
T = TC + TL
NTT = T // 128
L = 4
EPS = 1e-6
FF = 5632
FC = FF // 128
TILES5 = [(0, 256), (256, 512), (768, 512), (1280, 512), (1792, 512)]
C_CKV, C_KR, C_GK, C_GV, C_LRF, C_LRB, C_CQ, C_GQ, C_GG, C_PIN, C_GATE = 0, 512, 576, 1600, 2624, 2640, 2656, 3168, 4192, 5216, 6240
NSLOT = 8


class Stop(Exception):
    pass


class Eng:
    def __init__(s, K, name, e, is_dma=False):
        s.K, s.name, s.e, s.is_dma = K, name, e, is_dma
        s.seen = {}
        if is_dma:
            s.slots = [K.newsem(f"{name}{i}") for i in range(NSLOT)]
            s.slot_cnt = [0] * NSLOT
            s.n = 0
        else:
            s.sem = K.newsem(name)
            s.cnt = 0


class Buf:
    __slots__ = ("name", "w", "r")

    def __init__(s, name=""):
        s.name, s.w, s.r = name, None, {}


class Kern:
    def __init__(s, nc, es):
        s.nc, s.es = nc, es
        s.PE = Eng(s, "pe", nc.tensor)
        s.ACT = Eng(s, "act", nc.scalar)
        s.DVE = Eng(s, "dve", nc.vector)
        s.POOL = Eng(s, "pool", nc.gpsimd)
        s.SQ = Eng(s, "sq", nc.sync, True)
        s.GQ = Eng(s, "gq", nc.gpsimd, True)
        s.GQ.seen = s.POOL.seen
        s.engs = [s.PE, s.ACT, s.DVE, s.POOL]
        s.qs = [s.SQ, s.GQ]
        s.nbank = 0

    def newsem(s, name):
        return s.es.enter_context(s.nc.semaphore(name))

    def sb(s, name, shape, dt):
        s.uid = getattr(s, "uid", 0) + 1
        return s.es.enter_context(s.nc.sbuf_tensor(f"{name}_u{s.uid}", list(shape), dt))

    def wait(s, E, tok):
        if tok is None:
            return
        sem, val = tok
        if E is s.PE and sem is s.PE.sem:
            return
        if E.seen.get(id(sem), 0) >= val:
            return
        E.e.wait_ge(sem, val)
        E.seen[id(sem)] = val

    def _deps(s, E, R, W):
        for b in R:
            s.wait(E, b.w)
            if b.name.startswith("bank"):
                for t in list(b.r.values()):
                    if t[0] is not getattr(E, "sem", None):
                        s.wait(E, t)
        for b in W:
            s.wait(E, b.w)
            for t in list(b.r.values()):
                s.wait(E, t)

    def _upd(s, tok, R, W):
        for b in R:
            old = b.r.get(id(tok[0]))
            if old is None or old[1] < tok[1]:
                b.r[id(tok[0])] = tok
        for b in W:
            b.w = tok
            b.r = {}

    def op(s, E, fn, R=(), W=(), inc=True):
        s._deps(E, R, W)
        ins = fn()
        if inc:
            E.cnt += 1
            ins.then_inc(E.sem, 1)
            tok = (E.sem, E.cnt)
        else:
            tok = (E.sem, E.cnt + 1)
        s._upd(tok, R, W)
        return ins

    def dma(s, Q, out, in_, R=(), W=(), **kw):
        slot = Q.n % NSLOT
        Q.n += 1
        if Q.slot_cnt[slot] > 0:
            s.wait(Q, (Q.slots[slot], 16 * Q.slot_cnt[slot]))
        s._deps(Q, R, W)
        ins = Q.e.dma_start(out=out, in_=in_, **kw)
        Q.slot_cnt[slot] += 1
        ins.then_inc(Q.slots[slot], 16)
        tok = (Q.slots[slot], 16 * Q.slot_cnt[slot])
        s._upd(tok, R, W)
        return ins

    def all_tokens(s):
        toks = [(E.sem, E.cnt) for E in s.engs if E.cnt > 0]
        for Q in s.qs:
            for i in range(NSLOT):
                if Q.slot_cnt[i] > 0:
                    toks.append((Q.slots[i], 16 * Q.slot_cnt[i]))
        return toks

    def barrier(s, only=None):
        toks = s.all_tokens()
        for E in (only or (s.engs + [s.SQ])):
            for t in toks:
                s.wait(E, t)


def build(nlayers=L, dbg=None):
    nc = bass.Bass("TRN2", target_bir_lowering=False)
    dbg = dbg or {}

    def din(name, shape, dt=F32):
        return nc.dram_tensor(name, list(shape), dt, kind="ExternalInput").ap()

    def dscr(name, shape, dt):
        kind = "ExternalOutput" if name in dbg else "Internal"
        return nc.dram_tensor(name, list(shape), dt, kind=kind).ap()

    xin = din("xin", [T, D])
    cvec = din("cvec", [32, 128])
    w_mod = din("w_mod", [L, D, 6 * D])
    b_mod = din("b_mod", [L * 96, 128])
    norm1_g = din("norm1_g", [L * 16, 128])
    norm2_g = din("norm2_g", [L * 16, 128])
    w_in = din("w_in", [L, D, 12384])
    ident_d = din("ident_in", [128, 128])
    out_d = nc.dram_tensor("out", [TL, D], F32, kind="ExternalOutput").ap()
    w_kv_up = din("w_kv_up", [L, 512, 2048])
    w_q_up = din("w_q_up", [L, 512, 1536])
    qng_d = din("qng", [L * 4, 128])
    kvng_d = din("kvng", [L * 4, 128])
    hgn_d = din("hg_n", [2 * L, 128])
    hgr_d = din("hg_r", [2 * L, 64])
    cos_d = din("cosT", [64, T])
    sin_d = din("sinT", [64, T])
    rm_d = din("Rm", [64, 64])
    wg_d = din("gla_wg", [L, 2, 16, 1024])
    gb_d = din("gla_b", [L * 16, 128])
    gon_d = din("gon_rep", [L, 128, 256])
    msk_d = din("masks", [2, 128, 128])
    pool_w = din("pool_w", [L, 4, 256, 256])
    pscale_d = din("pool_scale", [L * 8, 128])
    invc_d = din("invcnt", [4, 128, T])
    w_branch = din("w_branch", [L, 3, 1024, D])
    w_out = din("w_out", [L, D, D])
    ffn_w1 = din("ffn_w1", [L, D, FF])
    ffn_w3 = din("ffn_w3", [L, D, FF])
    ffn_w2 = din("ffn_w2", [L, FF, D])

    xT = dscr("xT", [DC, 128, T], F32)
    ckvT = dscr("ckvT", [4, 128, T], F32)
    cqT = dscr("cqT", [4, 128, T], F32)
    krT = dscr("krT", [64, T], F32)
    lrfT = dscr("lrfT", [16, T], F32)
    lrbT = dscr("lrbT", [16, T], F32)
    gkT = dscr("gkT", [8, 128, T], BF16)
    gqT = dscr("gqT", [8, 128, T], BF16)
    pinT = dscr("pinT", [8, 128, T], BF16)
    gateT = dscr("gateT", [48, 128, T], BF16)
    gvtm = dscr("gvtm", [NTT, 128, 1024], BF16)
    ggtm = dscr("ggtm", [NTT, 128, 1024], BF16)
    mlaT = dscr("mlaT", [8, 128, T], BF16)
    glaT = dscr("glaT", [8, 128, T], BF16)
    poolT = dscr("poolT", [8, 128, T], BF16)
    if "gla_dbg" in dbg:
        dL = dscr("dL", [128, T], F32); dC = dscr("dC", [128, T], F32); dO = dscr("dO", [128, NTT, 256], F32)
        dC1 = dscr("dC1", [128, T], F32)

    with ExitStack() as es:
        K = Kern(nc, es)
        PE, ACT, DVE, POOL, SQ, GQ = K.PE, K.ACT, K.DVE, K.POOL, K.SQ, K.GQ
        ps_t = es.enter_context(nc.psum_tensor("ps", [128, 8, 512], F32))
        banks = [Buf(f"bank{i}") for i in range(8)]

        def bank(i=None):
            if i is None:
                i = K.nbank % 8
                K.nbank += 1
            return banks[i], ps_t[:, i, :]

        ident = K.sb("ident", [128, 128], F32)
        identb = K.sb("identb", [128, 128], BF16)
        onesb = K.sb("onesb", [128, 128], BF16)
        modT = K.sb("modT", [128, L, 96, 2], F32)
        A1 = K.sb("A1", [128, L, 16, 2], F32)
        A2 = K.sb("A2", [128, L, 16, 2], F32)
        n1gT = K.sb("n1gT", [128, L * 16], F32)
        n2gT = K.sb("n2gT", [128, L * 16], F32)
        bmodT = K.sb("bmodT", [128, L * 96], F32)
        sT = K.sb("sT", [128, 32], BF16)
        qngT = K.sb("qngT", [128, L * 4], F32)
        kvngT = K.sb("kvngT", [128, L * 4], F32)
        hgnT = K.sb("hgnT", [128, 2 * L], F32)
        hgrT = K.sb("hgrT", [64, 2 * L], F32)
        Rmb = K.sb("Rmb", [64, 64], BF16)
        ngbT = K.sb("ngbT", [128, L * 16], F32)
        pscT = K.sb("pscT", [128, L * 8], F32)
        mskS = K.sb("mskS", [128, 2, 128], F32)
        B_const = Buf("const")
        B_mod = Buf("mod")
        B_hT = Buf("hT")
        B_xT = Buf("xT_dram")

        K.dma(SQ, ident[:], ident_d, W=[B_const])
        K.op(DVE, lambda: nc.vector.tensor_copy(out=identb[:], in_=ident[:]), R=[B_const], W=[B_const])
        K.op(DVE, lambda: nc.vector.memset(onesb[:], 1.0), W=[B_const])

        def load_cols(src_ap, R, dst_ap, scope, func=None, w=128):
            tmp = K.sb(f"lc_{scope}", [128, 128], F32)
            b_tmp = Buf()
            K.dma(SQ, tmp[0:R, 0:w], src_ap, W=[b_tmp])
            bb, ps = bank()
            K.op(PE, lambda: nc.tensor.matmul(ps[0:w, 0:R], tmp[0:R, 0:w], ident[0:R, 0:R], start=True, stop=True),
                 R=[b_tmp, B_const], W=[bb])
            if func is None:
                K.op(DVE, lambda: nc.vector.tensor_copy(out=dst_ap, in_=ps[0:w, 0:R]), R=[bb], W=[B_mod])
            else:
                K.op(ACT, lambda: nc.scalar.activation(out=dst_ap, in_=ps[0:w, 0:R], func=func), R=[bb], W=[B_mod])

        with ExitStack() as st:
            K.es = st
            load_cols(cvec, 32, sT[:, :], "cv", func=AF.Silu)
            load_cols(norm1_g, L * 16, n1gT[:, :], "n1")
            load_cols(norm2_g, L * 16, n2gT[:, :], "n2")
            load_cols(qng_d, L * 4, qngT[:, :], "qng")
            load_cols(kvng_d, L * 4, kvngT[:, :], "kvng")
            load_cols(hgn_d, 2 * L, hgnT[:, :], "hgn")
            load_cols(hgr_d, 2 * L, hgrT[:, :], "hgr", w=64)
            K.dma(GQ, Rmb[:], rm_d, W=[B_const])
            load_cols(gb_d, L * 16, ngbT[:, :], "gb")
            K.op(DVE, lambda: nc.vector.tensor_scalar(out=ngbT[:], in0=ngbT[:], scalar1=-1.0, scalar2=None, op0=ALU.mult),
                 R=[B_mod], W=[B_mod])
            load_cols(pscale_d, L * 8, pscT[:, :], "psc")
            K.dma(SQ, mskS[:], msk_d.rearrange("a p n -> p a n"), W=[B_const])
            for i in range(3):
                load_cols(b_mod[i * 128:(i + 1) * 128, :], 128, bmodT[:, i * 128:(i + 1) * 128], f"bm{i}")
            xl = [K.sb(f"xl{i}", [128, D], F32) for i in range(2)]
            xo = [K.sb(f"xo{i}", [128, DC, 128], F32) for i in range(2)]
            b_xl = [Buf(), Buf()]
            b_xo = [Buf(), Buf()]
            xT_v = xT.rearrange("c p t -> p c t")
            for tt in range(NTT):
                i = tt % 2
                K.dma(SQ, xl[i][:], xin[tt * 128:(tt + 1) * 128, :], W=[b_xl[i]])
                for g in range(4):
                    bb, ps = bank()
                    for j in range(4):
                        c = g * 4 + j
                        K.op(PE, lambda c=c, j=j, ps=ps, i=i: nc.tensor.matmul(
                            ps[:, j * 128:(j + 1) * 128], xl[i][:, c * 128:(c + 1) * 128], ident[:], start=True, stop=True),
                            R=[b_xl[i], B_const], W=[bb], inc=(j == 3))
                    E = ACT if g % 2 == 0 else DVE
                    dst = xo[i][:, g * 4:(g + 1) * 4, :]
                    src = ps.rearrange("p (a b) -> p a b", a=4)
                    if E is ACT:
                        K.op(ACT, lambda dst=dst, src=src: nc.scalar.copy(out=dst, in_=src), R=[bb], W=[b_xo[i]])
                    else:
                        K.op(DVE, lambda dst=dst, src=src: nc.vector.tensor_copy(out=dst, in_=src), R=[bb], W=[b_xo[i]])
                K.dma(SQ, xT_v[:, :, tt * 128:(tt + 1) * 128], xo[i][:], R=[b_xo[i]], W=[B_xT])
            wsl = [K.sb(f"wmod{i}", [128, 16, 512], BF16) for i in range(3)]
            b_wsl = [Buf() for _ in range(3)]
            nld = 0
            for l in range(nlayers):
                bb, ps = bank()
                for cg in range(24):
                    i = nld % 3
                    nld += 1
                    K.dma(GQ, wsl[i][:], w_mod[l, :, cg * 512:(cg + 1) * 512].rearrange("(k p) n -> p k n", p=128), W=[b_wsl[i]])
                    for j in range(4):
                        ch = cg * 4 + j
                        for k in range(16):
                            K.op(PE, lambda i=i, j=j, k=k, ch=ch, ps=ps: nc.tensor.matmul(
                                ps[:, ch * 2:(ch + 1) * 2], wsl[i][:, k, j * 128:(j + 1) * 128], sT[:, k::16],
                                start=(k == 0), stop=(k == 15)),
                                R=[b_wsl[i], B_mod], W=[bb], inc=(k == 15))
                K.op(DVE, lambda l=l, ps=ps: nc.vector.tensor_tensor(
                    out=modT[:, l, :, :], in0=ps[:, 0:192].rearrange("p (a b) -> p a b", b=2),
                    in1=bmodT[:, l * 96:(l + 1) * 96].unsqueeze(2).to_broadcast([128, 96, 2]), op=ALU.add),
                    R=[bb, B_mod], W=[B_mod])
                for (Ax, ng, lo) in ((A1, n1gT, 16), (A2, n2gT, 64)):
                    K.op(DVE, lambda l=l, Ax=Ax, ng=ng, lo=lo: nc.vector.scalar_tensor_tensor(
                        out=Ax[:, l, :, :], in0=modT[:, l, lo:lo + 16, :], scalar=1.0,
                        in1=ng[:, l * 16:(l + 1) * 16].unsqueeze(2).to_broadcast([128, 16, 2]),
                        op0=ALU.add, op1=ALU.mult), R=[B_mod], W=[B_mod])
            K.barrier()
        K.es = es

        if "stop_pre" in dbg:
            return finish(nc, K, out_d, modT, dbg)


        def mla_stage(l):
            SCALE = 192.0 ** -0.5
            B_z = Buf()
            kvnT = K.sb("kvnT", [128, 4, T], BF16)
            qnT = K.sb("qnT", [128, 4, T], BF16)
            b_kvn, b_qn = Buf(), Buf()
            wkv = K.sb("wkv", [128, 4, 2048], BF16)
            wq = K.sb("wq", [128, 4, 1536], BF16)
            b_w = Buf()
            for hf in range(2):
                K.dma(GQ, wkv[:, :, hf * 1024:(hf + 1) * 1024],
                      w_kv_up[l, :, hf * 1024:(hf + 1) * 1024].rearrange("(k p) n -> p k n", p=128), W=[b_w])
            K.dma(GQ, wq[:], w_q_up[l].rearrange("(k p) n -> p k n", p=128), W=[b_w])
            cosS = K.sb("cosS", [64, T], BF16)
            sinS = K.sb("sinS", [64, T], BF16)
            b_tab = Buf()
            K.dma(GQ, cosS[:], cos_d, W=[b_tab], max_dma_last_dim=4096)
            K.dma(GQ, sinS[:], sin_d, W=[b_tab], max_dma_last_dim=4096)
            krsq = K.sb("krsq", [128, T], BF16)
            K.op(DVE, lambda: nc.vector.memset(krsq[64:128, :], 0.0), W=[b_tab])
            krot = K.sb("krot", [64, T], F32)
            b_kr = Buf()
            raw = K.sb("raw", [128, T], F32)
            rawr = K.sb("rawr", [64, T], F32)
            rawrb = K.sb("rawrb", [64, T], BF16)
            sq = K.sb("sq", [128, T], BF16)
            sqr = K.sb("sqr", [128, T], BF16)
            K.op(DVE, lambda: nc.vector.memset(sqr[64:128, :], 0.0), W=[b_tab])
            rstd = K.sb("rstd", [128, T], F32)
            r1 = rstd
            t1 = K.sb("t1", [64, T], F32)
            t2 = K.sb("t2", [64, T], F32)
            b_raw, b_rawr, b_sq, b_r = Buf(), Buf(), Buf(), Buf()
            b_t = Buf()

            outer = K.es
            st_in = ExitStack()
            K.es = st_in
            krS = K.sb("krS", [64, T], F32)
            K.dma(SQ, krS[:], krT, W=[b_tab])
            lt = [K.sb(f"lt{i}", [128, 4, 512], F32) for i in range(2)]
            b_lt = [Buf(), Buf()]
            lsq = K.sb("lsq", [128, 4, 512], BF16)
            b_lsq = Buf()
            nld = 0
            for (src, gT, dstT, b_dst) in ((ckvT, kvngT, kvnT, b_kvn), (cqT, qngT, qnT, b_qn)):
                src_v = src.rearrange("c p t -> p c t")
                for (off, n) in TILES5:
                    i = nld % 2
                    nld += 1
                    K.dma(SQ, lt[i][:, :, 0:n], src_v[:, :, off:off + n], W=[b_lt[i]])
                    K.op(ACT, lambda i=i, n=n: nc.scalar.activation(out=lsq[:, :, 0:n], in_=lt[i][:, :, 0:n], func=AF.Square),
                         R=[b_lt[i]], W=[b_lsq])
                    bb, ps = bank()
                    for c in range(4):
                        K.op(PE, lambda c=c, n=n, ps=ps: nc.tensor.matmul(ps[:, 0:n], onesb[:], lsq[:, c, 0:n],
                                                                           start=(c == 0), stop=(c == 3)),
                             R=[b_lsq, B_const], W=[bb], inc=(c == 3))
                    K.op(ACT, lambda n=n, ps=ps: nc.scalar.activation(out=r1[:, 0:n], in_=ps[:, 0:n], func=AF.Sqrt,
                                                                      bias=EPS, scale=1.0 / 512), R=[bb], W=[b_r])
                    K.op(DVE, lambda n=n: nc.vector.reciprocal(out=rstd[:, 0:n], in_=r1[:, 0:n]), R=[b_r], W=[b_r])
                    for c in range(4):
                        K.op(DVE, lambda c=c, n=n, i=i, off=off, gT=gT, dstT=dstT: nc.vector.scalar_tensor_tensor(
                            out=dstT[:, c, off:off + n], in0=lt[i][:, c, 0:n], scalar=gT[:, l * 4 + c:l * 4 + c + 1],
                            in1=rstd[:, 0:n], op0=ALU.mult, op1=ALU.mult), R=[b_lt[i], b_r, B_mod], W=[b_dst])

            K.op(ACT, lambda: nc.scalar.activation(out=krsq[0:64, :], in_=krS[:], func=AF.Square), R=[b_tab], W=[b_kr])

            def rope_apply(srcf, srcb, b_src, dst, b_dst):
                for (off, n) in TILES5:
                    bb, ps = bank()
                    K.op(PE, lambda off=off, n=n, ps=ps: nc.tensor.matmul(ps[0:64, 0:n], Rmb[:, :], srcb[:, off:off + n],
                                                                           start=True, stop=True),
                         R=[b_src, B_const], W=[bb])
                    K.op(DVE, lambda off=off, n=n, ps=ps: nc.vector.tensor_tensor(
                        out=t2[:, off:off + n], in0=ps[0:64, 0:n], in1=sinS[:, off:off + n], op=ALU.mult),
                        R=[bb, b_tab], W=[b_t])
                K.op(DVE, lambda: nc.vector.tensor_tensor(out=t1[:], in0=srcf[:], in1=cosS[:], op=ALU.mult),
                     R=[b_src, b_tab, b_t], W=[b_t])
                K.op(DVE, lambda: nc.vector.tensor_tensor(out=dst[:], in0=t1[:], in1=t2[:], op=ALU.add),
                     R=[b_t], W=[b_dst])

            K.op(DVE, lambda: nc.vector.tensor_scalar(out=rawr[:], in0=krS[:], scalar1=hgrT[:, L + l:L + l + 1], scalar2=None,
                                                      op0=ALU.mult), R=[b_tab, B_mod], W=[b_rawr])
            K.op(ACT, lambda: nc.scalar.copy(out=rawrb[:], in_=rawr[:]), R=[b_rawr], W=[b_rawr])
            rope_apply(rawr, rawrb, b_rawr, krot, b_kr)
            K.barrier()
            st_in.close()
            K.es = outer
            if "mla_a" in dbg:
                return True

            b_hd = [Buf(), Buf()]
            khat = [K.sb(f"khat{i}", [128, T], BF16) for i in range(2)]
            krope = [K.sb(f"krope{i}", [128, T], BF16) for i in range(2)]
            qhat = [K.sb(f"qhat{i}", [128, T], BF16) for i in range(2)]
            qrope = [K.sb(f"qrope{i}", [128, T], BF16) for i in range(2)]
            for i in range(2):
                K.op(DVE, lambda i=i: nc.vector.memset(krope[i][64:128, :], 0.0), W=[b_hd[i]])
                K.op(DVE, lambda i=i: nc.vector.memset(qrope[i][64:128, :], 0.0), W=[b_hd[i]])
            vh = [K.sb(f"vh{i}", [128, NTT, 128], BF16) for i in range(2)]
            PT = [K.sb(f"PT{i}", [128, 512], BF16) for i in range(2)]
            b_PT = [Buf() for _ in range(2)]
            mstg = [K.sb("mstg", [128, T], BF16)] * 2
            b_mstg = [Buf()] * 2
            rl = K.sb("rl", [128, 512], F32)
            b_rl = Buf()
            npt = 0

            def head_norm(nope_w, rope_w, srcT, b_src, g_col, dst_hat, dst_rope, b_dst, rope_src):
                for ti, (off, n) in enumerate(TILES5):
                    bb, ps = bank()
                    for c in range(4):
                        K.op(PE, lambda c=c, off=off, n=n, ps=ps: nc.tensor.matmul(
                            ps[:, 0:n], nope_w(c), srcT[:, c, off:off + n], start=(c == 0), stop=(c == 3)),
                            R=[b_w, b_src], W=[bb], inc=(c == 3))
                    K.op(ACT, lambda off=off, n=n, ps=ps: nc.scalar.copy(out=raw[:, off:off + n], in_=ps[:, 0:n]),
                         R=[bb], W=[b_raw])
                    if rope_w is not None:
                        bb2, ps2 = bank()
                        for c in range(4):
                            K.op(PE, lambda c=c, off=off, n=n, ps2=ps2: nc.tensor.matmul(
                                ps2[0:64, 0:n], rope_w(c), srcT[:, c, off:off + n], start=(c == 0), stop=(c == 3)),
                                R=[b_w, b_src], W=[bb2], inc=(c == 3))
                        K.op(DVE, lambda off=off, n=n, ps2=ps2: nc.vector.tensor_scalar(
                            out=rawr[:, off:off + n], in0=ps2[0:64, 0:n], scalar1=hgrT[:, l:l + 1], scalar2=None,
                            op0=ALU.mult), R=[bb2, B_mod], W=[b_rawr])
                        K.op(ACT, lambda off=off, n=n, ps2=ps2: nc.scalar.activation(
                            out=sqr[0:64, off:off + n], in_=ps2[0:64, 0:n], func=AF.Square), R=[], W=[bb2, b_sq])
                K.op(ACT, lambda: nc.scalar.activation(out=sq[:], in_=raw[:], func=AF.Square), R=[b_raw], W=[b_sq])
                rsq = sqr if rope_w is not None else krsq
                for ti, (off, n) in enumerate(TILES5):
                    bb, ps = bank()
                    K.op(PE, lambda off=off, n=n, ps=ps: nc.tensor.matmul(ps[:, 0:n], onesb[:], sq[:, off:off + n],
                                                                           start=True, stop=False),
                         R=[b_sq, B_const], W=[bb], inc=False)
                    K.op(PE, lambda off=off, n=n, ps=ps: nc.tensor.matmul(ps[:, 0:n], onesb[:, :], rsq[:, off:off + n],
                                                                           start=False, stop=True),
                         R=[b_sq, b_kr, B_const], W=[bb])
                    K.op(ACT, lambda off=off, n=n, ps=ps: nc.scalar.activation(
                        out=r1[:, off:off + n], in_=ps[:, 0:n], func=AF.Sqrt, bias=EPS, scale=1.0 / 192), R=[bb], W=[b_r])
                K.op(DVE, lambda: nc.vector.reciprocal(out=rstd[:], in_=r1[:]), R=[b_r], W=[b_r])
                K.op(DVE, lambda: nc.vector.scalar_tensor_tensor(
                    out=dst_hat[:], in0=raw[:], scalar=g_col, in1=rstd[:], op0=ALU.mult, op1=ALU.mult),
                    R=[b_raw, b_r, B_mod], W=[b_dst])
                if rope_w is not None:
                    K.op(ACT, lambda: nc.scalar.copy(out=rawrb[:], in_=rawr[:]), R=[b_rawr], W=[b_rawr])
                    rope_apply(rawr, rawrb, b_rawr, t1, b_t)
                    K.op(DVE, lambda: nc.vector.tensor_tensor(out=dst_rope[0:64, :], in0=t1[:], in1=rstd[0:64, :], op=ALU.mult),
                         R=[b_t, b_r], W=[b_dst])
                else:
                    K.op(DVE, lambda: nc.vector.tensor_tensor(out=dst_rope[0:64, :], in0=krot[:], in1=rstd[0:64, :], op=ALU.mult),
                         R=[b_kr, b_r], W=[b_dst])

            for h in range(8):
                hb = h % 2
                head_norm(lambda c, h=h: wkv[:, c, h * 256:h * 256 + 128], None, kvnT, b_kvn,
                          hgnT[:, L + l:L + l + 1], khat[hb], krope[hb], b_hd[hb], None)
                if "mla_b1" in dbg:
                    K.barrier()
                    return True
                for g in range(5):
                    bb, ps = bank()
                    tts = list(range(g * 4, min(g * 4 + 4, NTT)))
                    for j, tt in enumerate(tts):
                        for c in range(4):
                            K.op(PE, lambda c=c, j=j, tt=tt, ps=ps, h=h: nc.tensor.matmul(
                                ps[:, j * 128:(j + 1) * 128], kvnT[:, c, tt * 128:(tt + 1) * 128],
                                wkv[:, c, h * 256 + 128:h * 256 + 256], start=(c == 0), stop=(c == 3)),
                                R=[b_w, b_kvn], W=[bb], inc=(c == 3 and j == len(tts) - 1))
                    nn = len(tts)
                    K.op(ACT, lambda g=g, nn=nn, ps=ps, hb=hb: nc.scalar.copy(
                        out=vh[hb][:, g * 4:g * 4 + nn, :], in_=ps[:, 0:nn * 128].rearrange("p (a b) -> p a b", b=128)),
                        R=[bb], W=[b_hd[hb]])
                if "mla_b2" in dbg:
                    K.barrier()
                    return True
                head_norm(lambda c, h=h: wq[:, c, h * 192:h * 192 + 128], lambda c, h=h: wq[:, c, h * 192 + 128:h * 192 + 192],
                          qnT, b_qn, hgnT[:, l:l + 1], qhat[hb], qrope[hb], b_hd[hb], True)
                if "mla_b" in dbg:
                    K.barrier()
                    return True
                ms = mstg[hb]
                for qi, (qoff, nq) in enumerate(TILES5):
                    kts = [0, 1] if qoff < TC else list(range(NTT))
                    bO, psO = bank(qi % 2)
                    bL, psL = bank(2 + qi % 2)

                    def s_mm(kt, slot):
                        bS, psS = bank(4 + slot % 4)
                        K.op(PE, lambda: nc.tensor.matmul(psS[:, 0:nq], khat[hb][:, kt * 128:(kt + 1) * 128],
                                                          qhat[hb][:, qoff:qoff + nq], start=True, stop=False),
                             R=[b_hd[hb]], W=[bS], inc=False)
                        K.op(PE, lambda: nc.tensor.matmul(psS[:, 0:nq], krope[hb][:, kt * 128:(kt + 1) * 128],
                                                          qrope[hb][:, qoff:qoff + nq], start=False, stop=True),
                             R=[b_hd[hb]], W=[bS])
                        return bS, psS
                    pend = s_mm(kts[0], 0)
                    for ki, kt in enumerate(kts):
                        bS, psS = pend
                        if ki + 1 < len(kts):
                            pend = s_mm(kts[ki + 1], ki + 1)
                        pi = npt % 2
                        npt += 1
                        K.op(ACT, lambda psS=psS, pi=pi: nc.scalar.activation(out=PT[pi][:, 0:nq], in_=psS[:, 0:nq], func=AF.Exp,
                                                                              scale=SCALE), R=[bS], W=[b_PT[pi]])
                        K.op(PE, lambda kt=kt, pi=pi, ki=ki: nc.tensor.matmul(psO[:, 0:nq], vh[hb][:, kt, :], PT[pi][:, 0:nq],
                                                                              start=(ki == 0), stop=(ki == len(kts) - 1)),
                             R=[b_hd[hb], b_PT[pi]], W=[bO], inc=False)
                        K.op(PE, lambda pi=pi, ki=ki: nc.tensor.matmul(psL[:, 0:nq], onesb[:], PT[pi][:, 0:nq],
                                                                       start=(ki == 0), stop=(ki == len(kts) - 1)),
                             R=[b_PT[pi], B_const], W=[bL])
                    K.op(DVE, lambda: nc.vector.reciprocal(out=rl[:, 0:nq], in_=psL[:, 0:nq]), R=[bL], W=[b_rl])
                    K.op(DVE, lambda: nc.vector.tensor_tensor(out=ms[:, qoff:qoff + nq], in0=psO[:, 0:nq], in1=rl[:, 0:nq],
                                                              op=ALU.mult), R=[bO, b_rl], W=[b_mstg[hb]])
                K.dma(SQ, mlaT[h], ms[:], R=[b_mstg[hb]], W=[B_z])
            K.barrier()

        def gla_stage(l):
            B_o = Buf()
            wg = K.sb("wg", [16, 2, 1024], BF16)
            lr = K.sb("lr", [16, 2, T], BF16)
            gon = K.sb("gon", [128, 256], F32)
            b_in = Buf()
            K.dma(GQ, wg[:], wg_d[l].rearrange("d k n -> k d n"), W=[b_in])
            K.dma(GQ, lr[:, 0, :], lrfT, W=[b_in], max_dma_last_dim=4096)
            K.dma(GQ, lr[:, 1, :], lrbT, W=[b_in], max_dma_last_dim=4096)
            K.dma(SQ, gon[:], gon_d[l], W=[b_in])
            rmask = K.sb("rmask", [128, T], BF16)
            K.op(DVE, lambda: nc.vector.memset(rmask[:], 1.0), W=[b_in])
            K.op(DVE, lambda: nc.vector.memset(rmask[:, 0::128], 0.0), W=[b_in])
            Lf = K.sb("Lf", [128, T], F32)
            Cs = K.sb("Cs", [128, T], F32)
            E1 = K.sb("E1", [128, T], F32)
            E2 = Lf
            ctot = K.sb("ctot", [128, NTT], F32)
            b_L, b_C, b_E = Buf(), Buf(), Buf()
            gkl = K.sb("gkl", [128, T], BF16)
            gql = K.sb("gql", [128, T], BF16)
            b_gl = Buf()
            qt = [K.sb(f"qt{d}", [128, 2, T], BF16) for d in range(2)]
            kt = [K.sb(f"kt{d}", [128, 2, T], BF16) for d in range(2)]
            EB = [K.sb(f"EB{d}", [128, 2, NTT], F32) for d in range(2)]
            b_qk = [Buf(), Buf()]
            vh = K.sb("gvh", [128, NTT, 256], BF16)
            ggh = K.sb("ggh", [128, NTT, 256], BF16)
            b_v = Buf()
            S = [K.sb(f"S{d}", [128, 2, 256], F32) for d in range(2)]
            Sb = [K.sb(f"Sb{d}", [128, 2, 256], BF16) for d in range(2)]
            b_S = [Buf(), Buf()]
            b_Sb = [Buf(), Buf()]
            Am = [K.sb(f"Am{d}", [128, NTT, 128], BF16) for d in range(2)]
            b_Am = [Buf(), Buf()]
            khtok = [K.sb(f"khtok{d}", [128, NTT, 256], BF16) for d in range(2)]
            b_kht = [Buf(), Buf()]
            oacc = K.sb("oacc", [128, NTT, 256], F32)
            b_oa = Buf()
            sqo = K.sb("sqo", [128, NTT, 256], BF16)
            ss = K.sb("ss", [128, NTT], F32)
            b_ss = Buf()
            glatok = K.sb("glatok", [128, NTT, 256], BF16)
            b_gt = Buf()
            gstg = K.sb("gstg", [128, T], BF16)
            b_gs = Buf()
            order = [list(range(NTT)), [1, 0] + list(range(NTT - 1, 1, -1))]
            for h in range(4):
                K.dma(SQ, vh[:], gvtm[:, :, h * 256:(h + 1) * 256].rearrange("t p n -> p t n"), W=[b_v])
                K.dma(SQ, ggh[:], ggtm[:, :, h * 256:(h + 1) * 256].rearrange("t p n -> p t n"), W=[b_v])
                for d in range(2):
                    for dkc in range(2):
                        ch = h * 2 + dkc
                        K.dma(SQ, gkl[:], gkT[ch], W=[b_gl])
                        K.dma(SQ, gql[:], gqT[ch], W=[b_gl])
                        for (off, n) in TILES5:
                            bb, ps = bank()
                            K.op(PE, lambda: nc.tensor.matmul(ps[:, 0:n], wg[:, d, ch * 128:(ch + 1) * 128], lr[:, d, off:off + n],
                                                              start=True, stop=True), R=[b_in], W=[bb])
                            col = l * 16 + d * 8 + ch
                            K.op(ACT, lambda: nc.scalar.activation(out=Lf[:, off:off + n], in_=ps[:, 0:n], func=AF.Exp,
                                                                   bias=ngbT[:, col:col + 1], scale=-1.0), R=[bb, B_mod], W=[b_L])
                        K.op(ACT, lambda: nc.scalar.activation(out=Lf[:], in_=Lf[:], func=AF.Ln, bias=1.0, scale=1.0),
                             R=[b_L], W=[b_L])
                        K.op(DVE, lambda: nc.vector.tensor_tensor_scan(out=Cs[:], data0=rmask[:], data1=Lf[:], initial=0.0,
                                                                       op0=ALU.mult, op1=ALU.add), R=[b_L, b_in], W=[b_C])
                        if "gla_dbg" in dbg and h == 0 and d == 0 and dkc == 0:
                            K.dma(SQ, dL, Lf[:], R=[b_L], W=[Buf()])
                            K.dma(SQ, dC, Cs[:], R=[b_C], W=[Buf()])
                        if d == 1:
                            K.op(DVE, lambda: nc.vector.tensor_tensor(out=Lf[:], in0=Lf[:], in1=Cs[:], op=ALU.subtract),
                                 R=[b_C], W=[b_L])
                            K.op(DVE, lambda: nc.vector.tensor_copy(out=ctot[:], in_=Cs[:, 127::128]), R=[b_C], W=[b_L])
                            K.op(DVE, lambda: nc.vector.tensor_tensor(
                                out=Cs[:].rearrange("p (c t) -> p c t", t=128), in0=Lf[:].rearrange("p (c t) -> p c t", t=128),
                                in1=ctot[:].unsqueeze(2).to_broadcast([128, NTT, 128]), op=ALU.add),
                                R=[b_L], W=[b_C])
                        if "gla_dbg" in dbg and h == 0 and d == 1 and dkc == 0:
                            K.dma(SQ, dC1, Cs[:], R=[b_C], W=[Buf()])
                        K.op(ACT, lambda: nc.scalar.activation(out=E1[:], in_=Cs[:], func=AF.Exp, scale=-1.0 / 16), R=[b_C], W=[b_E])
                        K.op(ACT, lambda: nc.scalar.activation(out=E2[:], in_=Cs[:], func=AF.Exp, scale=1.0 / 16), R=[b_C], W=[b_E, b_L])
                        ebv = E1[:, 127::128] if d == 0 else E1[:, 0::128]
                        K.op(DVE, lambda: nc.vector.tensor_copy(out=EB[d][:, dkc, :], in_=ebv), R=[b_E], W=[b_qk[d]])
                        K.op(DVE, lambda: nc.vector.scalar_tensor_tensor(out=qt[d][:, dkc, :], in0=gql[:], scalar=1.0 / 16, in1=E1[:],
                                                                         op0=ALU.mult, op1=ALU.mult), R=[b_gl, b_E], W=[b_qk[d]])
                        K.op(DVE, lambda: nc.vector.tensor_tensor(out=kt[d][:, dkc, :], in0=gkl[:], in1=E2[:], op=ALU.mult),
                             R=[b_gl, b_E, b_L], W=[b_qk[d]])
                    K.op(DVE, lambda: nc.vector.memset(S[d][:], 0.0), W=[b_S[d]])
                    K.op(DVE, lambda: nc.vector.memset(Sb[d][:], 0.0), W=[b_Sb[d]])
                for d in range(2):
                    for g in range(5):
                        ccs = list(range(g * 4, min(g * 4 + 4, NTT)))
                        bb, ps = bank()
                        for j, c in enumerate(ccs):
                            cs = slice(c * 128, (c + 1) * 128)
                            for dkc in range(2):
                                K.op(PE, lambda: nc.tensor.matmul(ps[:, j * 128:(j + 1) * 128], kt[d][:, dkc, cs], qt[d][:, dkc, cs],
                                                                  start=(dkc == 0), stop=(dkc == 1)), R=[b_qk[d]], W=[bb],
                                     inc=(dkc == 1 and j == len(ccs) - 1))
                        nn = len(ccs)
                        K.op(DVE, lambda: nc.vector.tensor_tensor(
                            out=Am[d][:, g * 4:g * 4 + nn, :], in0=ps[:, 0:nn * 128].rearrange("p (a b) -> p a b", b=128),
                            in1=mskS[:, d, :].unsqueeze(1).to_broadcast([128, nn, 128]), op=ALU.mult), R=[bb, B_const], W=[b_Am[d]])
                    for g in range(NTT // 2):
                        bb, ps = bank()
                        for j in range(2):
                            c = g * 2 + j
                            cs = slice(c * 128, (c + 1) * 128)
                            for dkc in range(2):
                                K.op(PE, lambda: nc.tensor.matmul(ps[:, (j * 2 + dkc) * 128:(j * 2 + dkc + 1) * 128], kt[d][:, dkc, cs],
                                                                  identb[:], start=True, stop=True), R=[b_qk[d], B_const], W=[bb],
                                     inc=(dkc == 1 and j == 1))
                        K.op(ACT, lambda: nc.scalar.copy(out=khtok[d][:, g * 2:g * 2 + 2, :],
                                                         in_=ps[:, :].rearrange("p (a b) -> p a b", b=256)), R=[bb], W=[b_kht[d]])
                seen = set()
                for i in range(NTT):
                    for d in range(2):
                        c = order[d][i]
                        cs = slice(c * 128, (c + 1) * 128)
                        bU, psU = bank(4 + d * 2 + i % 2)
                        for dkc in range(2):
                            K.op(PE, lambda: nc.tensor.matmul(psU[:, dkc * 256:(dkc + 1) * 256], khtok[d][:, c, dkc * 128:(dkc + 1) * 128],
                                                              vh[:, c, :], start=True, stop=True), R=[b_kht[d], b_v], W=[bU],
                                 inc=(dkc == 1))
                        bO, psO = bank(0 + d * 2 + i % 2)
                        for dkc in range(2):
                            K.op(PE, lambda: nc.tensor.matmul(psO[:, 0:256], qt[d][:, dkc, cs], Sb[d][:, dkc, :],
                                                              start=(dkc == 0), stop=False), R=[b_qk[d], b_Sb[d]], W=[bO], inc=False)
                        K.op(PE, lambda: nc.tensor.matmul(psO[:, 0:256], Am[d][:, c, :], vh[:, c, :], start=False, stop=True),
                             R=[b_Am[d], b_v], W=[bO])
                        for dkc in range(2):
                            K.op(DVE, lambda: nc.vector.tensor_scalar(out=S[d][:, dkc, :], in0=S[d][:, dkc, :],
                                                                      scalar1=EB[d][:, dkc, c:c + 1], scalar2=None, op0=ALU.mult),
                                 R=[b_qk[d]], W=[b_S[d]])
                        for dkc in range(2):
                            K.op(DVE, lambda: nc.vector.scalar_tensor_tensor(
                                out=S[d][:, dkc, :], in0=psU[:, dkc * 256:(dkc + 1) * 256], scalar=EB[d][:, dkc, c:c + 1],
                                in1=S[d][:, dkc, :], op0=ALU.mult, op1=ALU.add), R=[bU, b_qk[d]], W=[b_S[d]])
                        K.op(ACT, lambda: nc.scalar.copy(out=Sb[d][:], in_=S[d][:]), R=[b_S[d]], W=[b_Sb[d]])
                        if c not in seen:
                            seen.add(c)
                            K.op(DVE, lambda: nc.vector.tensor_copy(out=oacc[:, c, :], in_=psO[:, 0:256]), R=[bO], W=[b_oa])
                        else:
                            K.op(DVE, lambda: nc.vector.tensor_tensor(out=oacc[:, c, :], in0=psO[:, 0:256], in1=oacc[:, c, :],
                                                                      op=ALU.add), R=[bO], W=[b_oa])
                if "gla_dbg" in dbg and h == 0:
                    K.dma(SQ, dO, oacc[:], R=[b_oa], W=[Buf()])
                K.op(ACT, lambda: nc.scalar.activation(out=sqo[:], in_=oacc[:], func=AF.Square), R=[b_oa], W=[b_ss])
                K.op(DVE, lambda: nc.vector.tensor_reduce(out=ss[:], in_=sqo[:], axis=AX.X, op=ALU.add), R=[], W=[b_ss])
                K.op(ACT, lambda: nc.scalar.activation(out=ss[:], in_=ss[:], func=AF.Sqrt, bias=EPS, scale=1.0 / 256), R=[], W=[b_ss])
                K.op(DVE, lambda: nc.vector.reciprocal(out=ss[:], in_=ss[:]), R=[], W=[b_ss])
                K.op(DVE, lambda: nc.vector.tensor_tensor(out=oacc[:], in0=oacc[:], in1=ss[:].unsqueeze(2).to_broadcast([128, NTT, 256]),
                                                          op=ALU.mult), R=[b_ss], W=[b_oa])
                K.op(DVE, lambda: nc.vector.tensor_tensor(out=oacc[:], in0=oacc[:],
                                                          in1=gon[:].unsqueeze(1).to_broadcast([128, NTT, 256]), op=ALU.mult),
                     R=[b_in], W=[b_oa])
                K.op(ACT, lambda: nc.scalar.activation(out=sqo[:], in_=ggh[:], func=AF.Silu), R=[b_v], W=[b_ss])
                K.op(DVE, lambda: nc.vector.tensor_tensor(out=glatok[:], in0=oacc[:], in1=sqo[:], op=ALU.mult),
                     R=[b_oa, b_ss], W=[b_gt])
                for dvc in range(2):
                    for g in range(5):
                        tts = list(range(g * 4, min(g * 4 + 4, NTT)))
                        bb, ps = bank()
                        for j, tt in enumerate(tts):
                            K.op(PE, lambda: nc.tensor.matmul(ps[:, j * 128:(j + 1) * 128], glatok[:, tt, dvc * 128:(dvc + 1) * 128],
                                                              identb[:], start=True, stop=True), R=[b_gt, B_const], W=[bb],
                                 inc=(j == len(tts) - 1))
                        nn = len(tts) * 128
                        K.op(ACT, lambda: nc.scalar.copy(out=gstg[:, g * 512:g * 512 + nn], in_=ps[:, 0:nn]), R=[bb], W=[b_gs])
                    K.dma(SQ, glaT[h * 2 + dvc], gstg[:], R=[b_gs], W=[B_o])
            K.barrier()

        def pool_stage(l):
            B_o = Buf()
            WIN = (2, 4, 8, 16)
            pin = K.sb("pin", [128, T], BF16)
            b_pin = Buf()
            Pc = K.sb("Pc", [128, TC + 16], F32)
            Pl = K.sb("Pl", [128, TL + 16], F32)
            b_P = Buf()
            K.op(DVE, lambda: nc.vector.memset(Pc[:], 0.0), W=[b_P])
            K.op(DVE, lambda: nc.vector.memset(Pl[:], 0.0), W=[b_P])
            qa = K.sb("qa", [128, TL + 16], F32)
            qb = K.sb("qb", [128, TL + 16], F32)
            b_q = Buf()
            icnt = K.sb("icnt", [128, T], F32)
            b_ic = Buf()
            pooled = K.sb("pooled", [128, 8, T], BF16)
            b_pl = Buf()
            pw = K.sb("pw", [128, 2, 256], BF16)
            b_pw = Buf()
            pstg = K.sb("pstg", [128, T], BF16)
            b_ps = Buf()
            for ch in range(8):
                g = ch // 2
                w = WIN[g]
                K.dma(SQ, pin[:], pinT[ch], W=[b_pin])
                if ch % 2 == 0:
                    K.dma(SQ, icnt[:], invc_d[g], W=[b_ic])
                K.op(ACT, lambda: nc.scalar.copy(out=Pc[:, 8:8 + TC], in_=pin[:, 0:TC]), R=[b_pin], W=[b_P])
                K.op(ACT, lambda: nc.scalar.copy(out=Pl[:, 8:8 + TL], in_=pin[:, TC:T]), R=[b_pin], W=[b_P])
                for (P, n, off) in ((Pc, TC, 0), (Pl, TL, TC)):
                    cur, ln = P, n + 16
                    step = 1
                    nxts = [qa, qb]
                    ni = 0
                    while step < w:
                        nx = nxts[ni % 2]
                        ni += 1
                        K.op(DVE, lambda cur=cur, nx=nx, ln=ln, step=step: nc.vector.tensor_tensor(
                            out=nx[:, 0:ln - step], in0=cur[:, 0:ln - step], in1=cur[:, step:ln], op=ALU.add), R=[b_P], W=[b_q])
                        cur, ln = nx, ln - step
                        step *= 2
                    o0 = 8 - w // 2
                    other = nxts[ni % 2]
                    K.op(DVE, lambda cur=cur, other=other, o0=o0, n=n, off=off: nc.vector.tensor_tensor(
                        out=other[:, 0:n], in0=cur[:, o0:o0 + n], in1=icnt[:, off:off + n], op=ALU.mult), R=[b_ic], W=[b_q])
                    K.op(DVE, lambda other=other, P=P, n=n, off=off: nc.vector.tensor_tensor(
                        out=pooled[:, ch, off:off + n], in0=other[:, 0:n], in1=P[:, 8:8 + n], op=ALU.subtract), R=[b_P], W=[b_q, b_pl])
            for g in range(4):
                K.dma(GQ, pw[:], pool_w[l, g].rearrange("(k p) n -> p k n", p=128), W=[b_pw])
                for dc in range(2):
                    for (off, n) in TILES5:
                        bb, ps = bank()
                        for cc in range(2):
                            K.op(PE, lambda: nc.tensor.matmul(ps[:, 0:n], pw[:, cc, dc * 128:(dc + 1) * 128], pooled[:, g * 2 + cc, off:off + n],
                                                              start=(cc == 0), stop=(cc == 1)), R=[b_pw, b_pl], W=[bb], inc=(cc == 1))
                        col = l * 8 + g * 2 + dc
                        K.op(ACT, lambda: nc.scalar.activation(out=pstg[:, off:off + n], in_=ps[:, 0:n], func=AF.Copy,
                                                               scale=pscT[:, col:col + 1]), R=[bb, B_mod], W=[b_ps])
                    K.dma(SQ, poolT[g * 2 + dc], pstg[:], R=[b_ps], W=[B_o])
            K.barrier()

        GT = 576
        GROUPS = [(0, 576, [(0, 256, 1), (256, 320, 0)])] + [(576 * i, 576, [(0, 288, 0), (288, 288, 0)]) for i in range(1, 4)]
        def tl_stage(l):
            xT_v = xT.rearrange("c p t -> p c t")
            wall = K.sb("tw", [128, 4 * 8192], BF16)
            wsl = [wall[:, i * 8192:(i + 1) * 8192] for i in range(4)]
            b_wsl = [Buf() for _ in range(4)]
            b_w2 = [Buf(), Buf()]
            nld = [0]
            nld2 = [0]

            def wload(src_ap, kc, cw):
                i = nld[0] % 4
                nld[0] += 1
                v = wsl[i][:, 0:kc * cw].rearrange("p (k n) -> p k n", n=cw)
                K.dma(GQ, v, src_ap.rearrange("(k p) n -> p k n", p=128), W=[b_wsl[i], b_w2[0], b_w2[1]])
                return v, b_wsl[i]

            def w2load(src_ap):
                i = nld2[0] % 2
                nld2[0] += 1
                v = wall[:, i * 11264:(i + 1) * 11264].rearrange("p (k n) -> p k n", n=256)
                K.dma(GQ, v, src_ap.rearrange("(k p) n -> p k n", p=128), W=[b_w2[i], b_wsl[0], b_wsl[1], b_wsl[2], b_wsl[3]])
                return v, b_w2[i]
            xg = K.sb("xg", [128, DC, GT], F32)
            b_xg = Buf()
            mT = K.sb("mT", [128, DC, GT], BF16)
            b_mT = Buf()
            for (g0, gsz, subs) in GROUPS:
                K.dma(SQ, xg[:, :, 0:gsz], xT_v[:, :, g0:g0 + gsz], R=[B_xT], W=[b_xg])
                with ExitStack() as sti:
                    K.es = sti
                    br = [K.sb(f"br{n}", [128, 8, GT], BF16) for n in range(3)]
                    b_br = Buf()
                    for n, src in enumerate((poolT, mlaT, glaT)):
                        K.dma(SQ, br[n][:, :, 0:gsz], src.rearrange("c p t -> p c t")[:, :, g0:g0 + gsz], W=[b_br])
                    gt = [K.sb(f"gt{i}", [128, 4, GT], BF16) for i in range(2)]
                    b_gt = [Buf(), Buf()]
                    macc = K.sb("macc", [128, 4, GT], F32)
                    mtmp = K.sb("mtmp", [128, 320], F32)
                    b_ma = Buf()
                    ng = 0
                    for cg in range(4):
                        for n in range(3):
                            wv, bw = wload(w_branch[l, n, :, cg * 512:(cg + 1) * 512], 8, 512)
                            gi2 = ng % 2
                            ng += 1
                            K.dma(SQ, gt[gi2][:, :, 0:gsz], gateT[n * 16 + cg * 4:n * 16 + cg * 4 + 4].rearrange("c p t -> p c t")[:, :, g0:g0 + gsz],
                                  W=[b_gt[gi2]])
                            for j in range(4):
                                dch = cg * 4 + j
                                for (so, sn, jj) in subs:
                                    bb, ps = bank()
                                    for c in range(8):
                                        K.op(PE, lambda: nc.tensor.matmul(ps[:, 0:sn], wv[:, c, j * 128:(j + 1) * 128], br[n][:, c, so:so + sn],
                                                                          start=(c == 0), stop=(c == 7)), R=[bw, b_br], W=[bb], inc=(c == 7))
                                    if n == 0:
                                        K.op(DVE, lambda: nc.vector.tensor_tensor(out=macc[:, j, so:so + sn], in0=ps[:, 0:sn],
                                                                                  in1=gt[gi2][:, j, so:so + sn], op=ALU.mult),
                                             R=[bb, b_gt[gi2]], W=[b_ma])
                                    else:
                                        K.op(DVE, lambda: nc.vector.tensor_tensor(out=mtmp[:, 0:sn], in0=ps[:, 0:sn],
                                                                                  in1=gt[gi2][:, j, so:so + sn], op=ALU.mult),
                                             R=[bb, b_gt[gi2]], W=[b_ma])
                                        if n == 1:
                                            K.op(DVE, lambda: nc.vector.tensor_tensor(out=macc[:, j, so:so + sn], in0=macc[:, j, so:so + sn],
                                                                                      in1=mtmp[:, 0:sn], op=ALU.add), R=[], W=[b_ma])
                                        else:
                                            K.op(DVE, lambda: nc.vector.tensor_tensor(out=mT[:, dch, so:so + sn], in0=macc[:, j, so:so + sn],
                                                                                      in1=mtmp[:, 0:sn], op=ALU.add), R=[b_ma], W=[b_mT])
                    K.barrier(only=[K.PE, K.ACT, K.DVE, K.SQ])
                K.es = tl_es[0]
                for cg in range(4):
                    wv, bw = wload(w_out[l, :, cg * 512:(cg + 1) * 512], 16, 512)
                    for j in range(4):
                        dch = cg * 4 + j
                        for (so, sn, jj) in subs:
                            bb, ps = bank()
                            for k in range(16):
                                K.op(PE, lambda: nc.tensor.matmul(ps[:, 0:sn], wv[:, k, j * 128:(j + 1) * 128], mT[:, k, so:so + sn],
                                                                  start=(k == 0), stop=(k == 15)), R=[bw, b_mT], W=[bb], inc=(k == 15))
                            K.op(DVE, lambda: nc.vector.scalar_tensor_tensor(
                                out=xg[:, dch, so:so + sn], in0=ps[:, 0:sn], scalar=modT[:, l, 32 + dch, jj:jj + 1],
                                in1=xg[:, dch, so:so + sn], op0=ALU.mult, op1=ALU.add), R=[bb, B_mod], W=[b_xg])
                with ExitStack() as sti:
                    K.es = sti
                    sqb = K.sb("f_sq", [128, DC, 320], BF16)
                    rs = K.sb("f_rs", [128, 320], F32)
                    tmp = [K.sb(f"f_t{i}", [128, 320], F32) for i in range(2)]
                    b_sq, b_rs, b_tm = Buf(), Buf(), [Buf(), Buf()]
                    for (so, sn, jj) in subs:
                        K.op(ACT, lambda: nc.scalar.activation(out=sqb[:, :, 0:sn], in_=xg[:, :, so:so + sn], func=AF.Square),
                             R=[b_xg], W=[b_sq])
                        bb, ps = bank()
                        for c in range(DC):
                            K.op(PE, lambda: nc.tensor.matmul(ps[:, 0:sn], onesb[:], sqb[:, c, 0:sn], start=(c == 0), stop=(c == DC - 1)),
                                 R=[b_sq, B_const], W=[bb], inc=(c == DC - 1))
                        K.op(ACT, lambda: nc.scalar.activation(out=rs[:, 0:sn], in_=ps[:, 0:sn], func=AF.Sqrt, bias=EPS, scale=1.0 / D),
                             R=[bb], W=[b_rs])
                        K.op(DVE, lambda: nc.vector.reciprocal(out=rs[:, 0:sn], in_=rs[:, 0:sn]), R=[], W=[b_rs])
                        for c in range(DC):
                            q = c % 2
                            K.op(DVE, lambda: nc.vector.scalar_tensor_tensor(
                                out=tmp[q][:, 0:sn], in0=xg[:, c, so:so + sn], scalar=A2[:, l, c, jj:jj + 1], in1=rs[:, 0:sn],
                                op0=ALU.mult, op1=ALU.mult), R=[b_xg, b_rs, B_mod], W=[b_tm[q]])
                            K.op(ACT, lambda: nc.scalar.activation(out=mT[:, c, so:so + sn], in_=tmp[q][:, 0:sn], func=AF.Identity,
                                                                   bias=modT[:, l, 48 + c, jj:jj + 1], scale=1.0),
                                 R=[b_tm[q], B_mod], W=[b_mT])
                    uT = K.sb("uT", [128, FC, GT], BF16)
                    b_uT = Buf()
                    s1 = [K.sb(f"s1_{i}", [128, 320], F32) for i in range(2)]
                    b_s1 = [Buf(), Buf()]
                    ns = 0
                    for fg in range(FC // 4):
                        w1v, bw1 = wload(ffn_w1[l, :, fg * 512:(fg + 1) * 512], 16, 512)
                        w3v, bw3 = wload(ffn_w3[l, :, fg * 512:(fg + 1) * 512], 16, 512)
                        for j in range(4):
                            f = fg * 4 + j
                            for (so, sn, jj) in subs:
                                b1, p1 = bank()
                                b3, p3 = bank()
                                for k in range(16):
                                    K.op(PE, lambda: nc.tensor.matmul(p1[:, 0:sn], w1v[:, k, j * 128:(j + 1) * 128], mT[:, k, so:so + sn],
                                                                      start=(k == 0), stop=(k == 15)), R=[bw1, b_mT], W=[b1], inc=(k == 15))
                                for k in range(16):
                                    K.op(PE, lambda: nc.tensor.matmul(p3[:, 0:sn], w3v[:, k, j * 128:(j + 1) * 128], mT[:, k, so:so + sn],
                                                                      start=(k == 0), stop=(k == 15)), R=[bw3, b_mT], W=[b3], inc=(k == 15))
                                si = ns % 2
                                ns += 1
                                K.op(ACT, lambda: nc.scalar.activation(out=s1[si][:, 0:sn], in_=p1[:, 0:sn], func=AF.Silu),
                                     R=[b1], W=[b_s1[si]])
                                K.op(DVE, lambda: nc.vector.tensor_tensor(out=uT[:, f, so:so + sn], in0=p3[:, 0:sn], in1=s1[si][:, 0:sn],
                                                                          op=ALU.mult), R=[b3, b_s1[si]], W=[b_uT])
                    for dch in range(DC):
                        if dch % 2 == 0:
                            wv, bw = w2load(ffn_w2[l, :, dch * 128:(dch + 2) * 128])
                        jo = (dch % 2) * 128
                        for (so, sn, jj) in subs:
                            bb, ps = bank()
                            for f in range(FC):
                                K.op(PE, lambda: nc.tensor.matmul(ps[:, 0:sn], wv[:, f, jo:jo + 128], uT[:, f, so:so + sn],
                                                                  start=(f == 0), stop=(f == FC - 1)), R=[bw, b_uT], W=[bb], inc=(f == FC - 1))
                            K.op(DVE, lambda: nc.vector.scalar_tensor_tensor(
                                out=xg[:, dch, so:so + sn], in0=ps[:, 0:sn], scalar=modT[:, l, 80 + dch, jj:jj + 1],
                                in1=xg[:, dch, so:so + sn], op0=ALU.mult, op1=ALU.add), R=[bb, B_mod], W=[b_xg])
                    K.dma(SQ, xT_v[:, :, g0:g0 + gsz], xg[:, :, 0:gsz], R=[b_xg], W=[B_xT])
                    K.barrier(only=[K.PE, K.ACT, K.DVE, K.SQ])
                K.es = tl_es[0]
            K.barrier()
        tl_es = [None]

        def out_stage():
            xT_v = xT.rearrange("c p t -> p c t")
            xt = [K.sb(f"o_x{i}", [128, DC, 128], F32) for i in range(2)]
            og = [K.sb(f"o_g{i}", [128, D], F32) for i in range(2)]
            b_xt, b_og = [Buf(), Buf()], [Buf(), Buf()]
            B_out = Buf()
            for tt in range(2, NTT):
                i = tt % 2
                K.dma(SQ, xt[i][:], xT_v[:, :, tt * 128:(tt + 1) * 128], R=[B_xT], W=[b_xt[i]])
                for g in range(4):
                    bb, ps = bank()
                    for j in range(4):
                        c = g * 4 + j
                        K.op(PE, lambda: nc.tensor.matmul(ps[:, j * 128:(j + 1) * 128], xt[i][:, c, :], ident[:], start=True, stop=True),
                             R=[b_xt[i], B_const], W=[bb], inc=(j == 3))
                    if g % 2 == 0:
                        K.op(ACT, lambda: nc.scalar.copy(out=og[i][:, g * 512:(g + 1) * 512], in_=ps[:, :]), R=[bb], W=[b_og[i]])
                    else:
                        K.op(DVE, lambda: nc.vector.tensor_copy(out=og[i][:, g * 512:(g + 1) * 512], in_=ps[:, :]), R=[bb], W=[b_og[i]])
                K.dma(SQ, out_d[(tt - 2) * 128:(tt - 1) * 128, :], og[i][:], R=[b_og[i]], W=[B_out])
            K.barrier()

        for l in range(nlayers):
            with ExitStack() as st:
                K.es = st
                hT = K.sb("hT", [128, DC, T], BF16)
                with ExitStack() as st1:
                    K.es = st1
                    norm_mod(nc, K, bank, xT, B_xT, hT, B_hT, A1, modT, 0, l, B_mod, onesb, B_const, TILES5)
                    K.barrier()
                K.es = st
                stg32 = [K.sb(f"stg32_{i}", [128, T], F32) for i in range(2)]
                stg16 = [K.sb(f"stg16_{i}", [128, T], BF16) for i in range(3)]
                b_s32 = [Buf() for _ in range(2)]
                b_s16 = [Buf() for _ in range(3)]
                wsl = [K.sb(f"win{i}", [128, 16, 512], BF16) for i in range(3)]
                b_wsl = [Buf() for _ in range(3)]
                cnt = {"ld": 0, "s32": 0, "s16": 0, "ev": 0}
                B_z = Buf("z")

                def fm_group(col0, width, chunks):
                    i = cnt["ld"] % 3
                    cnt["ld"] += 1
                    K.dma(GQ, wsl[i][:, :, 0:width], w_in[l, :, col0:col0 + width].rearrange("(k p) n -> p k n", p=128),
                          W=[b_wsl[i]])
                    for (co, M, dst, dt, func) in chunks:
                        bks = [bank() for _ in TILES5]
                        for k in range(16):
                            for ti, (off, n) in enumerate(TILES5):
                                bb, ps = bks[ti]
                                K.op(PE, lambda i=i, k=k, co=co, M=M, off=off, n=n, ps=ps: nc.tensor.matmul(
                                    ps[0:M, 0:n], wsl[i][:, k, co:co + M], hT[:, k, off:off + n],
                                    start=(k == 0), stop=(k == 15)),
                                    R=[b_wsl[i], B_hT], W=[bb], inc=(k == 15))
                        if dt is F32:
                            si = cnt["s32"] % 2
                            cnt["s32"] += 1
                            stg, bs = stg32[si], b_s32[si]
                        else:
                            si = cnt["s16"] % 3
                            cnt["s16"] += 1
                            stg, bs = stg16[si], b_s16[si]
                        for ti, (off, n) in enumerate(TILES5):
                            bb, ps = bks[ti]
                            useact = (func is not None) or (cnt["ev"] % 2 == 0)
                            cnt["ev"] += 1
                            if useact:
                                K.op(ACT, lambda M=M, off=off, n=n, ps=ps, stg=stg, func=func: nc.scalar.activation(
                                    out=stg[0:M, off:off + n], in_=ps[0:M, 0:n], func=(func or AF.Copy)), R=[bb], W=[bs])
                            else:
                                K.op(DVE, lambda M=M, off=off, n=n, ps=ps, stg=stg: nc.vector.tensor_copy(
                                    out=stg[0:M, off:off + n], in_=ps[0:M, 0:n]), R=[bb], W=[bs])
                        K.dma(SQ, dst, stg[0:M, :], R=[bs], W=[B_z])

                fm_group(C_CKV, 512, [(j * 128, 128, ckvT[j], F32, None) for j in range(4)])
                fm_group(C_KR, 64, [(0, 64, krT, F32, None)])
                fm_group(C_LRF, 32, [(0, 16, lrfT, F32, None), (16, 16, lrbT, F32, None)])
                fm_group(C_CQ, 512, [(j * 128, 128, cqT[j], F32, None) for j in range(4)])
                for g in range(2):
                    fm_group(C_GK + g * 512, 512, [(j * 128, 128, gkT[g * 4 + j], BF16, None) for j in range(4)])
                for g in range(2):
                    fm_group(C_GQ + g * 512, 512, [(j * 128, 128, gqT[g * 4 + j], BF16, None) for j in range(4)])
                for g in range(2):
                    fm_group(C_PIN + g * 512, 512, [(j * 128, 128, pinT[g * 4 + j], BF16, None) for j in range(4)])
                for g in range(12):
                    fm_group(C_GATE + g * 512, 512, [(j * 128, 128, gateT[g * 4 + j], BF16, AF.Sigmoid) for j in range(4)])
                wtm = K.sb("wtm", [128, 16, 1024], BF16)
                b_wtm = Buf()
                stm = [K.sb(f"stm{i}", [128, 1024], BF16) for i in range(2)]
                b_stm = [Buf(), Buf()]
                for (c0, dst) in ((C_GV, gvtm), (C_GG, ggtm)):
                    for hf in range(2):
                        K.dma(GQ, wtm[:, :, hf * 512:(hf + 1) * 512],
                              w_in[l, :, c0 + hf * 512:c0 + (hf + 1) * 512].rearrange("(k p) n -> p k n", p=128), W=[b_wtm])
                    for tt in range(NTT):
                        si = tt % 2
                        for hf in range(2):
                            bb, ps = bank()
                            for k in range(16):
                                K.op(PE, lambda k=k, tt=tt, hf=hf, ps=ps: nc.tensor.matmul(
                                    ps[:, :], hT[:, k, tt * 128:(tt + 1) * 128], wtm[:, k, hf * 512:(hf + 1) * 512],
                                    start=(k == 0), stop=(k == 15)), R=[b_wtm, B_hT], W=[bb], inc=(k == 15))
                            if hf == 0:
                                K.op(ACT, lambda ps=ps, si=si: nc.scalar.copy(out=stm[si][:, 0:512], in_=ps[:, :]),
                                     R=[bb], W=[b_stm[si]])
                            else:
                                K.op(DVE, lambda ps=ps, si=si: nc.vector.tensor_copy(out=stm[si][:, 512:1024], in_=ps[:, :]),
                                     R=[bb], W=[b_stm[si]])
                        K.dma(SQ, dst[tt], stm[si][:], R=[b_stm[si]], W=[B_z])
                K.barrier()
            K.es = es
            if "stop_l2" in dbg:
                return finish(nc, K, out_d, modT, dbg)
            with ExitStack() as st:
                K.es = st
                stop = mla_stage(l)
            K.es = es
            if stop or "stop_mla" in dbg:
                return finish(nc, K, out_d, modT, dbg)
            with ExitStack() as st:
                K.es = st
                gla_stage(l)
            K.es = es
            if "stop_gla" in dbg:
                return finish(nc, K, out_d, modT, dbg)
            with ExitStack() as st:
                K.es = st
                pool_stage(l)
            K.es = es
            if "stop_pool" in dbg:
                return finish(nc, K, out_d, modT, dbg)
            with ExitStack() as st:
                K.es = st
                tl_es[0] = st
                tl_stage(l)
            K.es = es
            if "stop_tl" in dbg:
                return finish(nc, K, out_d, modT, dbg)
        with ExitStack() as st:
            K.es = st
            out_stage()
        K.es = es

        return finish(nc, K, out_d, modT, dbg)


def norm_mod(nc, K, bank, xT, B_xT, hT, B_hT, Ax, modT, shift_lo, l, B_mod, onesb, B_const, tiles):
    PE, ACT, DVE, SQ = K.PE, K.ACT, K.DVE, K.SQ
    xt = [K.sb(f"nm_x{i}", [128, DC, 512], F32) for i in range(2)]
    b_xt = [Buf(), Buf()]
    sqb = K.sb("nm_sq", [128, DC, 512], BF16)
    b_sq = Buf()
    r1 = K.sb("nm_r1", [128, 512], F32)
    rstd = K.sb("nm_rstd", [128, 512], F32)
    b_r = Buf()
    tmp = [K.sb(f"nm_t{i}", [128, 512], F32) for i in range(2)]
    b_tmp = [Buf(), Buf()]
    xT_v = xT.rearrange("c p t -> p c t")
    for ti, (off, n) in enumerate(tiles):
        i = ti % 2
        j = 1 if off < TC else 0
        K.dma(SQ, xt[i][:, :, 0:n], xT_v[:, :, off:off + n], R=[B_xT], W=[b_xt[i]])
        K.op(ACT, lambda i=i, n=n: nc.scalar.activation(out=sqb[:, :, 0:n], in_=xt[i][:, :, 0:n], func=AF.Square),
             R=[b_xt[i]], W=[b_sq])
        bb, ps = bank()
        for c in range(DC):
            K.op(PE, lambda c=c, n=n, ps=ps: nc.tensor.matmul(ps[:, 0:n], onesb[:], sqb[:, c, 0:n],
                                                               start=(c == 0), stop=(c == DC - 1)),
                 R=[b_sq, B_const], W=[bb], inc=(c == DC - 1))
        K.op(ACT, lambda n=n, ps=ps: nc.scalar.activation(out=r1[:, 0:n], in_=ps[:, 0:n], func=AF.Sqrt,
                                                          bias=EPS, scale=1.0 / D), R=[bb], W=[b_r])
        K.op(DVE, lambda n=n: nc.vector.reciprocal(out=rstd[:, 0:n], in_=r1[:, 0:n]), R=[b_r], W=[b_r])
        for c in range(DC):
            q = c % 2
            K.op(DVE, lambda c=c, n=n, i=i, q=q, j=j: nc.vector.scalar_tensor_tensor(
                out=tmp[q][:, 0:n], in0=xt[i][:, c, 0:n], scalar=Ax[:, l, c, j:j + 1], in1=rstd[:, 0:n],
                op0=ALU.mult, op1=ALU.mult), R=[b_xt[i], b_r, B_mod], W=[b_tmp[q]])
            K.op(ACT, lambda c=c, n=n, q=q, j=j, off=off: nc.scalar.activation(
                out=hT[:, c, off:off + n], in_=tmp[q][:, 0:n], func=AF.Identity,
                bias=modT[:, l, shift_lo + c, j:j + 1], scale=1.0), R=[b_tmp[q], B_mod], W=[B_hT])


def finish(nc, K, out_d, modT, dbg):
    B = Buf()
    if "no_out" not in dbg:
        pass
    K.barrier()
    return nc


def prep_inputs(inp, b):
    f = lambda a: np.ascontiguousarray(a, dtype=np.float32)
    m = {}
    m["xin"] = f(np.concatenate([inp["ctx"][b], inp["x"][b]], 0))
    m["cvec"] = f(np.concatenate([inp["c"][b].reshape(16, 128), inp["c_ctx"].reshape(16, 128)], 0))
    m["w_mod"] = f(inp["w_mod"])
    m["b_mod"] = f(inp["b_mod"].reshape(-1, 128))
    m["norm1_g"] = f(inp["norm1_g"].reshape(-1, 128))
    m["norm2_g"] = f(inp["norm2_g"].reshape(-1, 128))
    m["w_in"] = f(inp["w_in"])
    m["w_kv_up"] = f(inp["mla_w_kv_up"])
    m["w_q_up"] = f(inp["mla_w_q_up"])
    m["qng"] = f(inp["mla_q_norm_g"].reshape(-1, 128))
    m["kvng"] = f(inp["mla_kv_norm_g"].reshape(-1, 128))
    m["hg_n"] = f(np.concatenate([inp["mla_q_head_g"][:, :128], inp["mla_k_head_g"][:, :128]], 0))
    m["hg_r"] = f(np.concatenate([inp["mla_q_head_g"][:, 128:], inp["mla_k_head_g"][:, 128:]], 0))
    m["gla_wg"] = f(inp["gla_w_gate_up"])
    m["gla_b"] = f(inp["gla_b_gate"].reshape(-1, 128))
    m["gon_rep"] = f(np.broadcast_to(inp["gla_out_norm_g"][:, None, :], (L, 128, 256)))
    m["pool_w"] = f(inp["pool_w"])
    m["pool_scale"] = f(inp["pool_scale"].reshape(-1, 128))
    m["w_branch"] = f(inp["w_branch"])
    m["w_out"] = f(inp["w_out"])
    m["ffn_w1"] = f(inp["ffn_w1"])
    m["ffn_w3"] = f(inp["ffn_w3"])
    m["ffn_w2"] = f(inp["ffn_w2"])
    return m


NCORES = 4


def kernel(**inputs):
    inp = {k: np.asarray(v) for k, v in inputs.items()}
    consts = host_consts()
    nc = build(L)
    shared = None
    in_maps = []
    for core in range(NCORES):
        b = core % 4
        if core < 4:
            m = prep_inputs(inp, b)
            if shared is None:
                shared = m
            else:
                for k in m:
                    if k not in ("xin", "cvec"):
                        m[k] = shared[k]
            m.update(consts)
            in_maps.append(m)
        else:
            in_maps.append(in_maps[b])
    res = run_bass_kernel_spmd(nc, in_maps, core_ids=list(range(NCORES)))
    out = np.stack([np.asarray(res.results[b]["out"], dtype=np.float32) for b in range(4)], 0)
    return out


def host_consts():
    c = {}
    half = 32
    inv_freq = 1.0 / (10000.0 ** (np.arange(0, half, 2, dtype=np.float32) / half))
    row = np.repeat(np.arange(TL // 64), 64).astype(np.float32)
    col = np.tile(np.arange(64), TL // 64).astype(np.float32)
    ang_r = row[:, None] * inv_freq[None, :]
    ang_c = col[:, None] * inv_freq[None, :]
    cosL = np.concatenate([np.cos(ang_r), np.cos(ang_r), np.cos(ang_c), np.cos(ang_c)], 1)
    sinL = np.concatenate([np.sin(ang_r), np.sin(ang_r), np.sin(ang_c), np.sin(ang_c)], 1)
    cosT = np.concatenate([np.ones((TC, 64), np.float32), cosL.astype(np.float32)], 0).T
    sinT = np.concatenate([np.zeros((TC, 64), np.float32), sinL.astype(np.float32)], 0).T
    c["cosT"] = np.ascontiguousarray(cosT, dtype=np.float32)
    c["sinT"] = np.ascontiguousarray(sinT, dtype=np.float32)
    Rm = np.zeros((64, 64), np.float32)
    for g in (0, 32):
        for i in range(16):
            Rm[g + 16 + i, g + i] = -1.0
            Rm[g + i, g + 16 + i] = 1.0
    c["Rm"] = Rm
    c["ident_in"] = np.eye(128, dtype=np.float32)
    s_idx = np.arange(128)[:, None]
    t_idx = np.arange(128)[None, :]
    c["masks"] = np.stack([(s_idx <= t_idx), (s_idx >= t_idx)], 0).astype(np.float32)
    inv = np.zeros((4, T), np.float32)
    for gi, w in enumerate((2, 4, 8, 16)):
        for (o, n) in ((0, TC), (TC, TL)):
            t = np.arange(n)
            lo = np.clip(t - w // 2, 0, n - 1)
            hi = np.clip(t + w // 2 - 1, 0, n - 1)
            inv[gi, o:o + n] = 1.0 / (hi - lo + 1)
    c["invcnt"] = np.ascontiguousarray(np.broadcast_to(inv[:, None, :], (4, 128, T)), dtype=np.float32)
    return c
```

```python
import numpy as np
from contextlib import ExitStack
import concourse.bass as bass
import concourse.mybir as mybir
from concourse.bass_utils import run_bass_kernel_spmd

F32, BF16 = mybir.dt.float32, mybir.dt.bfloat16
AF = mybir.ActivationFunctionType
ALU = mybir.AluOpType
AX = mybir.AxisListType

D = 2048
DC = 16
TC = 256
TL = 2048
T = TC + TL
NTT = T // 128
L = 4
EPS = 1e-6
FF = 5632
FC = FF // 128
TILES5 = [(0, 256), (256, 512), (768, 512), (1280, 512), (1792, 512)]
C_CKV, C_KR, C_GK, C_GV, C_LRF, C_LRB, C_CQ, C_GQ, C_GG, C_PIN, C_GATE = 0, 512, 576, 1600, 2624, 2640, 2656, 3168, 4192, 5216, 6240
NSLOT = 8


class Stop(Exception):
    pass


class Eng:
    def __init__(s, K, name, e, is_dma=False):
        s.K, s.name, s.e, s.is_dma = K, name, e, is_dma
        s.seen = {}
        if is_dma:
            s.slots = [K.newsem(f"{name}{i}") for i in range(NSLOT)]
            s.slot_cnt = [0] * NSLOT
            s.n = 0
        else:
            s.sem = K.newsem(name)
            s.cnt = 0


class Buf:
    __slots__ = ("name", "w", "r")

    def __init__(s, name=""):
        s.name, s.w, s.r = name, None, {}


class Kern:
    def __init__(s, nc, es):
        s.nc, s.es = nc, es
        s.PE = Eng(s, "pe", nc.tensor)
        s.ACT = Eng(s, "act", nc.scalar)
        s.DVE = Eng(s, "dve", nc.vector)
        s.POOL = Eng(s, "pool", nc.gpsimd)
        s.SQ = Eng(s, "sq", nc.sync, True)
        s.GQ = Eng(s, "gq", nc.gpsimd, True)
        s.GQ.seen = s.POOL.seen
        s.engs = [s.PE, s.ACT, s.DVE, s.POOL]
        s.qs = [s.SQ, s.GQ]
        s.nbank = 0

    def newsem(s, name):
        return s.es.enter_context(s.nc.semaphore(name))

    def sb(s, name, shape, dt):
        s.uid = getattr(s, "uid", 0) + 1
        return s.es.enter_context(s.nc.sbuf_tensor(f"{name}_u{s.uid}", list(shape), dt))

    def wait(s, E, tok):
        if tok is None:
            return
        sem, val = tok
        if E is s.PE and sem is s.PE.sem:
            return
        if E.seen.get(id(sem), 0) >= val:
            return
        E.e.wait_ge(sem, val)
        E.seen[id(sem)] = val

    def _deps(s, E, R, W):
        for b in R:
            s.wait(E, b.w)
            if b.name.startswith("bank"):
                for t in list(b.r.values()):
                    if t[0] is not getattr(E, "sem", None):
                        s.wait(E, t)
        for b in W:
            s.wait(E, b.w)
            for t in list(b.r.values()):
                s.wait(E, t)

    def _upd(s, tok, R, W):
        for b in R:
            old = b.r.get(id(tok[0]))
            if old is None or old[1] < tok[1]:
                b.r[id(tok[0])] = tok
        for b in W:
            b.w = tok
            b.r = {}

    def op(s, E, fn, R=(), W=(), inc=True):
        s._deps(E, R, W)
        ins = fn()
        if inc:
            E.cnt += 1
            ins.then_inc(E.sem, 1)
            tok = (E.sem, E.cnt)
        else:
            tok = (E.sem, E.cnt + 1)
        s._upd(tok, R, W)
        return ins

    def dma(s, Q, out, in_, R=(), W=(), **kw):
        slot = Q.n % NSLOT
        Q.n += 1
        if Q.slot_cnt[slot] > 0:
            s.wait(Q, (Q.slots[slot], 16 * Q.slot_cnt[slot]))
        s._deps(Q, R, W)
        ins = Q.e.dma_start(out=out, in_=in_, **kw)
        Q.slot_cnt[slot] += 1
        ins.then_inc(Q.slots[slot], 16)
        tok = (Q.slots[slot], 16 * Q.slot_cnt[slot])
        s._upd(tok, R, W)
        return ins

    def all_tokens(s):
        toks = [(E.sem, E.cnt) for E in s.engs if E.cnt > 0]
        for Q in s.qs:
            for i in range(NSLOT):
                if Q.slot_cnt[i] > 0:
                    toks.append((Q.slots[i], 16 * Q.slot_cnt[i]))
        return toks

    def barrier(s, only=None):
        toks = s.all_tokens()
        for E in (only or (s.engs + [s.SQ])):
            for t in toks:
                s.wait(E, t)


def build(nlayers=L, dbg=None):
    nc = bass.Bass("TRN2", target_bir_lowering=False)
    dbg = dbg or {}

    def din(name, shape, dt=F32):
        return nc.dram_tensor(name, list(shape), dt, kind="ExternalInput").ap()

    def dscr(name, shape, dt):
        kind = "ExternalOutput" if name in dbg else "Internal"
        return nc.dram_tensor(name, list(shape), dt, kind=kind).ap()

    xin = din("xin", [T, D])
    cvec = din("cvec", [32, 128])
    w_mod = din("w_mod", [L, D, 6 * D])
    b_mod = din("b_mod", [L * 96, 128])
    norm1_g = din("norm1_g", [L * 16, 128])
    norm2_g = din("norm2_g", [L * 16, 128])
    w_in = din("w_in", [L, D, 12384])
    ident_d = din("ident_in", [128, 128])
    out_d = nc.dram_tensor("out", [TL, D], F32, kind="ExternalOutput").ap()
    w_kv_up = din("w_kv_up", [L, 512, 2048])
    w_q_up = din("w_q_up", [L, 512, 1536])
    qng_d = din("qng", [L * 4, 128])
    kvng_d = din("kvng", [L * 4, 128])
    hgn_d = din("hg_n", [2 * L, 128])
    hgr_d = din("hg_r", [2 * L, 64])
    cos_d = din("cosT", [64, T])
    sin_d = din("sinT", [64, T])
    rm_d = din("Rm", [64, 64])
    wg_d = din("gla_wg", [L, 2, 16, 1024])
    gb_d = din("gla_b", [L * 16, 128])
    gon_d = din("gon_rep", [L, 128, 256])
    msk_d = din("masks", [2, 128, 128])
    pool_w = din("pool_w", [L, 4, 256, 256])
    pscale_d = din("pool_scale", [L * 8, 128])
    invc_d = din("invcnt", [4, 128, T])
    w_branch = din("w_branch", [L, 3, 1024, D])
    w_out = din("w_out", [L, D, D])
    ffn_w1 = din("ffn_w1", [L, D, FF])
    ffn_w3 = din("ffn_w3", [L, D, FF])
    ffn_w2 = din("ffn_w2", [L, FF, D])

    xT = dscr("xT", [DC, 128, T], F32)
    ckvT = dscr("ckvT", [4, 128, T], F32)
    cqT = dscr("cqT", [4, 128, T], F32)
    krT = dscr("krT", [64, T], F32)
    lrfT = dscr("lrfT", [16, T], F32)
    lrbT = dscr("lrbT", [16, T], F32)
    gkT = dscr("gkT", [8, 128, T], BF16)
    gqT = dscr("gqT", [8, 128, T], BF16)
    pinT = dscr("pinT", [8, 128, T], BF16)
    gateT = dscr("gateT", [48, 128, T], BF16)
    gvtm = dscr("gvtm", [NTT, 128, 1024], BF16)
    ggtm = dscr("ggtm", [NTT, 128, 1024], BF16)
    mlaT = dscr("mlaT", [8, 128, T], BF16)
    glaT = dscr("glaT", [8, 128, T], BF16)
    poolT = dscr("poolT", [8, 128, T], BF16)
    if "gla_dbg" in dbg:
        dL = dscr("dL", [128, T], F32); dC = dscr("dC", [128, T], F32); dO = dscr("dO", [128, NTT, 256], F32)
        dC1 = dscr("dC1", [128, T], F32)

    with ExitStack() as es:
        K = Kern(nc, es)
        PE, ACT, DVE, POOL, SQ, GQ = K.PE, K.ACT, K.DVE, K.POOL, K.SQ, K.GQ
        ps_t = es.enter_context(nc.psum_tensor("ps", [128, 8, 512], F32))
        banks = [Buf(f"bank{i}") for i in range(8)]

        def bank(i=None):
            if i is None:
                i = K.nbank % 8
                K.nbank += 1
            return banks[i], ps_t[:, i, :]

        ident = K.sb("ident", [128, 128], F32)
        identb = K.sb("identb", [128, 128], BF16)
        onesb = K.sb("onesb", [128, 128], BF16)
        modT = K.sb("modT", [128, L, 96, 2], F32)
        A1 = K.sb("A1", [128, L, 16, 2], F32)
        A2 = K.sb("A2", [128, L, 16, 2], F32)
        n1gT = K.sb("n1gT", [128, L * 16], F32)
        n2gT = K.sb("n2gT", [128, L * 16], F32)
        bmodT = K.sb("bmodT", [128, L * 96], F32)
        sT = K.sb("sT", [128, 32], BF16)
        qngT = K.sb("qngT", [128, L * 4], F32)
        kvngT = K.sb("kvngT", [128, L * 4], F32)
        hgnT = K.sb("hgnT", [128, 2 * L], F32)
        hgrT = K.sb("hgrT", [64, 2 * L], F32)
        Rmb = K.sb("Rmb", [64, 64], BF16)
        ngbT = K.sb("ngbT", [128, L * 16], F32)
        pscT = K.sb("pscT", [128, L * 8], F32)
        mskS = K.sb("mskS", [128, 2, 128], F32)
        B_const = Buf("const")
        B_mod = Buf("mod")
        B_hT = Buf("hT")
        B_xT = Buf("xT_dram")

        K.dma(SQ, ident[:], ident_d, W=[B_const])
        K.op(DVE, lambda: nc.vector.tensor_copy(out=identb[:], in_=ident[:]), R=[B_const], W=[B_const])
        K.op(DVE, lambda: nc.vector.memset(onesb[:], 1.0), W=[B_const])

        def load_cols(src_ap, R, dst_ap, scope, func=None, w=128):
            tmp = K.sb(f"lc_{scope}", [128, 128], F32)
            b_tmp = Buf()
            K.dma(SQ, tmp[0:R, 0:w], src_ap, W=[b_tmp])
            bb, ps = bank()
            K.op(PE, lambda: nc.tensor.matmul(ps[0:w, 0:R], tmp[0:R, 0:w], ident[0:R, 0:R], start=True, stop=True),
                 R=[b_tmp, B_const], W=[bb])
            if func is None:
                K.op(DVE, lambda: nc.vector.tensor_copy(out=dst_ap, in_=ps[0:w, 0:R]), R=[bb], W=[B_mod])
            else:
                K.op(ACT, lambda: nc.scalar.activation(out=dst_ap, in_=ps[0:w, 0:R], func=func), R=[bb], W=[B_mod])

        with ExitStack() as st:
            K.es = st
            load_cols(cvec, 32, sT[:, :], "cv", func=AF.Silu)
            load_cols(norm1_g, L * 16, n1gT[:, :], "n1")
            load_cols(norm2_g, L * 16, n2gT[:, :], "n2")
            load_cols(qng_d, L * 4, qngT[:, :], "qng")
            load_cols(kvng_d, L * 4, kvngT[:, :], "kvng")
            load_cols(hgn_d, 2 * L, hgnT[:, :], "hgn")
            load_cols(hgr_d, 2 * L, hgrT[:, :], "hgr", w=64)
            K.dma(GQ, Rmb[:], rm_d, W=[B_const])
            load_cols(gb_d, L * 16, ngbT[:, :], "gb")
            K.op(DVE, lambda: nc.vector.tensor_scalar(out=ngbT[:], in0=ngbT[:], scalar1=-1.0, scalar2=None, op0=ALU.mult),
                 R=[B_mod], W=[B_mod])
            load_cols(pscale_d, L * 8, pscT[:, :], "psc")
            K.dma(SQ, mskS[:], msk_d.rearrange("a p n -> p a n"), W=[B_const])
            for i in range(3):
                load_cols(b_mod[i * 128:(i + 1) * 128, :], 128, bmodT[:, i * 128:(i + 1) * 128], f"bm{i}")
            xl = [K.sb(f"xl{i}", [128, D], F32) for i in range(2)]
            xo = [K.sb(f"xo{i}", [128, DC, 128], F32) for i in range(2)]
            b_xl = [Buf(), Buf()]
            b_xo = [Buf(), Buf()]
            xT_v = xT.rearrange("c p t -> p c t")
            for tt in range(NTT):
                i = tt % 2
                K.dma(SQ, xl[i][:], xin[tt * 128:(tt + 1) * 128, :], W=[b_xl[i]])
                for g in range(4):
                    bb, ps = bank()
                    for j in range(4):
                        c = g * 4 + j
                        K.op(PE, lambda c=c, j=j, ps=ps, i=i: nc.tensor.matmul(
                            ps[:, j * 128:(j + 1) * 128], xl[i][:, c * 128:(c + 1) * 128], ident[:], start=True, stop=True),
                            R=[b_xl[i], B_const], W=[bb], inc=(j == 3))
                    E = ACT if g % 2 == 0 else DVE
                    dst = xo[i][:, g * 4:(g + 1) * 4, :]
                    src = ps.rearrange("p (a b) -> p a b", a=4)
                    if E is ACT:
                        K.op(ACT, lambda dst=dst, src=src: nc.scalar.copy(out=dst, in_=src), R=[bb], W=[b_xo[i]])
                    else:
                        K.op(DVE, lambda dst=dst, src=src: nc.vector.tensor_copy(out=dst, in_=src), R=[bb], W=[b_xo[i]])
                K.dma(SQ, xT_v[:, :, tt * 128:(tt + 1) * 128], xo[i][:], R=[b_xo[i]], W=[B_xT])
            wsl = [K.sb(f"wmod{i}", [128, 16, 512], BF16) for i in range(3)]
            b_wsl = [Buf() for _ in range(3)]
            nld = 0
            for l in range(nlayers):
                bb, ps = bank()
                for cg in range(24):
                    i = nld % 3
                    nld += 1
                    K.dma(GQ, wsl[i][:], w_mod[l, :, cg * 512:(cg + 1) * 512].rearrange("(k p) n -> p k n", p=128), W=[b_wsl[i]])
                    for j in range(4):
                        ch = cg * 4 + j
                        for k in range(16):
                            K.op(PE, lambda i=i, j=j, k=k, ch=ch, ps=ps: nc.tensor.matmul(
                                ps[:, ch * 2:(ch + 1) * 2], wsl[i][:, k, j * 128:(j + 1) * 128], sT[:, k::16],
                                start=(k == 0), stop=(k == 15)),
                                R=[b_wsl[i], B_mod], W=[bb], inc=(k == 15))
                K.op(DVE, lambda l=l, ps=ps: nc.vector.tensor_tensor(
                    out=modT[:, l, :, :], in0=ps[:, 0:192].rearrange("p (a b) -> p a b", b=2),
                    in1=bmodT[:, l * 96:(l + 1) * 96].unsqueeze(2).to_broadcast([128, 96, 2]), op=ALU.add),
                    R=[bb, B_mod], W=[B_mod])
                for (Ax, ng, lo) in ((A1, n1gT, 16), (A2, n2gT, 64)):
                    K.op(DVE, lambda l=l, Ax=Ax, ng=ng, lo=lo: nc.vector.scalar_tensor_tensor(
                        out=Ax[:, l, :, :], in0=modT[:, l, lo:lo + 16, :], scalar=1.0,
                        in1=ng[:, l * 16:(l + 1) * 16].unsqueeze(2).to_broadcast([128, 16, 2]),
                        op0=ALU.add, op1=ALU.mult), R=[B_mod], W=[B_mod])
            K.barrier()
        K.es = es

        if "stop_pre" in dbg:
            return finish(nc, K, out_d, modT, dbg)


        def mla_stage(l):
            SCALE = 192.0 ** -0.5
            B_z = Buf()
            kvnT = K.sb("kvnT", [128, 4, T], BF16)
            qnT = K.sb("qnT", [128, 4, T], BF16)
            b_kvn, b_qn = Buf(), Buf()
            wkv = K.sb("wkv", [128, 4, 2048], BF16)
            wq = K.sb("wq", [128, 4, 1536], BF16)
            b_w = Buf()
            for hf in range(2):
                K.dma(GQ, wkv[:, :, hf * 1024:(hf + 1) * 1024],
                      w_kv_up[l, :, hf * 1024:(hf + 1) * 1024].rearrange("(k p) n -> p k n", p=128), W=[b_w])
            K.dma(GQ, wq[:], w_q_up[l].rearrange("(k p) n -> p k n", p=128), W=[b_w])
            cosS = K.sb("cosS", [64, T], BF16)
            sinS = K.sb("sinS", [64, T], BF16)
            b_tab = Buf()
            K.dma(GQ, cosS[:], cos_d, W=[b_tab], max_dma_last_dim=4096)
            K.dma(GQ, sinS[:], sin_d, W=[b_tab], max_dma_last_dim=4096)
            krsq = K.sb("krsq", [128, T], BF16)
            K.op(DVE, lambda: nc.vector.memset(krsq[64:128, :], 0.0), W=[b_tab])
            krot = K.sb("krot", [64, T], F32)
            b_kr = Buf()
            raw = K.sb("raw", [128, T], F32)
            rawr = K.sb("rawr", [64, T], F32)
            rawrb = K.sb("rawrb", [64, T], BF16)
            sq = K.sb("sq", [128, T], BF16)
            sqr = K.sb("sqr", [128, T], BF16)
            K.op(DVE, lambda: nc.vector.memset(sqr[64:128, :], 0.0), W=[b_tab])
            rstd = K.sb("rstd", [128, T], F32)
            r1 = rstd
            t1 = K.sb("t1", [64, T], F32)
            t2 = K.sb("t2", [64, T], F32)
            b_raw, b_rawr, b_sq, b_r = Buf(), Buf(), Buf(), Buf()
            b_t = Buf()

            outer = K.es
            st_in = ExitStack()
            K.es = st_in
            krS = K.sb("krS", [64, T], F32)
            K.dma(SQ, krS[:], krT, W=[b_tab])
            lt = [K.sb(f"lt{i}", [128, 4, 512], F32) for i in range(2)]
            b_lt = [Buf(), Buf()]
            lsq = K.sb("lsq", [128, 4, 512], BF16)
            b_lsq = Buf()
            nld = 0
            for (src, gT, dstT, b_dst) in ((ckvT, kvngT, kvnT, b_kvn), (cqT, qngT, qnT, b_qn)):
                src_v = src.rearrange("c p t -> p c t")
                for (off, n) in TILES5:
                    i = nld % 2
                    nld += 1
                    K.dma(SQ, lt[i][:, :, 0:n], src_v[:, :, off:off + n], W=[b_lt[i]])
                    K.op(ACT, lambda i=i, n=n: nc.scalar.activation(out=lsq[:, :, 0:n], in_=lt[i][:, :, 0:n], func=AF.Square),
                         R=[b_lt[i]], W=[b_lsq])
                    bb, ps = bank()
                    for c in range(4):
                        K.op(PE, lambda c=c, n=n, ps=ps: nc.tensor.matmul(ps[:, 0:n], onesb[:], lsq[:, c, 0:n],
                                                                           start=(c == 0), stop=(c == 3)),
                             R=[b_lsq, B_const], W=[bb], inc=(c == 3))
                    K.op(ACT, lambda n=n, ps=ps: nc.scalar.activation(out=r1[:, 0:n], in_=ps[:, 0:n], func=AF.Sqrt,
                                                                      bias=EPS, scale=1.0 / 512), R=[bb], W=[b_r])
                    K.op(DVE, lambda n=n: nc.vector.reciprocal(out=rstd[:, 0:n], in_=r1[:, 0:n]), R=[b_r], W=[b_r])
                    for c in range(4):
                        K.op(DVE, lambda c=c, n=n, i=i, off=off, gT=gT, dstT=dstT: nc.vector.scalar_tensor_tensor(
                            out=dstT[:, c, off:off + n], in0=lt[i][:, c, 0:n], scalar=gT[:, l * 4 + c:l * 4 + c + 1],
                            in1=rstd[:, 0:n], op0=ALU.mult, op1=ALU.mult), R=[b_lt[i], b_r, B_mod], W=[b_dst])

            K.op(ACT, lambda: nc.scalar.activation(out=krsq[0:64, :], in_=krS[:], func=AF.Square), R=[b_tab], W=[b_kr])

            def rope_apply(srcf, srcb, b_src, dst, b_dst):
                for (off, n) in TILES5:
                    bb, ps = bank()
                    K.op(PE, lambda off=off, n=n, ps=ps: nc.tensor.matmul(ps[0:64, 0:n], Rmb[:, :], srcb[:, off:off + n],
                                                                           start=True, stop=True),
                         R=[b_src, B_const], W=[bb])
                    K.op(DVE, lambda off=off, n=n, ps=ps: nc.vector.tensor_tensor(
                        out=t2[:, off:off + n], in0=ps[0:64, 0:n], in1=sinS[:, off:off + n], op=ALU.mult),
                        R=[bb, b_tab], W=[b_t])
                K.op(DVE, lambda: nc.vector.tensor_tensor(out=t1[:], in0=srcf[:], in1=cosS[:], op=ALU.mult),
                     R=[b_src, b_tab, b_t], W=[b_t])
                K.op(DVE, lambda: nc.vector.tensor_tensor(out=dst[:], in0=t1[:], in1=t2[:], op=ALU.add),
                     R=[b_t], W=[b_dst])

            K.op(DVE, lambda: nc.vector.tensor_scalar(out=rawr[:], in0=krS[:], scalar1=hgrT[:, L + l:L + l + 1], scalar2=None,
                                                      op0=ALU.mult), R=[b_tab, B_mod], W=[b_rawr])
            K.op(ACT, lambda: nc.scalar.copy(out=rawrb[:], in_=rawr[:]), R=[b_rawr], W=[b_rawr])
            rope_apply(rawr, rawrb, b_rawr, krot, b_kr)
            K.barrier()
            st_in.close()
            K.es = outer
            if "mla_a" in dbg:
                return True

            b_hd = [Buf(), Buf()]
            khat = [K.sb(f"khat{i}", [128, T], BF16) for i in range(2)]
            krope = [K.sb(f"krope{i}", [128, T], BF16) for i in range(2)]
            qhat = [K.sb(f"qhat{i}", [128, T], BF16) for i in range(2)]
            qrope = [K.sb(f"qrope{i}", [128, T], BF16) for i in range(2)]
            for i in range(2):
                K.op(DVE, lambda i=i: nc.vector.memset(krope[i][64:128, :], 0.0), W=[b_hd[i]])
                K.op(DVE, lambda i=i: nc.vector.memset(qrope[i][64:128, :], 0.0), W=[b_hd[i]])
            vh = [K.sb(f"vh{i}", [128, NTT, 128], BF16) for i in range(2)]
            PT = [K.sb(f"PT{i}", [128, 512], BF16) for i in range(2)]
            b_PT = [Buf() for _ in range(2)]
            mstg = [K.sb("mstg", [128, T], BF16)] * 2
            b_mstg = [Buf()] * 2
            rl = K.sb("rl", [128, 512], F32)
            b_rl = Buf()
            npt = 0

            def head_norm(nope_w, rope_w, srcT, b_src, g_col, dst_hat, dst_rope, b_dst, rope_src):
                for ti, (off, n) in enumerate(TILES5):
                    bb, ps = bank()
                    for c in range(4):
                        K.op(PE, lambda c=c, off=off, n=n, ps=ps: nc.tensor.matmul(
                            ps[:, 0:n], nope_w(c), srcT[:, c, off:off + n], start=(c == 0), stop=(c == 3)),
                            R=[b_w, b_src], W=[bb], inc=(c == 3))
                    K.op(ACT, lambda off=off, n=n, ps=ps: nc.scalar.copy(out=raw[:, off:off + n], in_=ps[:, 0:n]),
                         R=[bb], W=[b_raw])
                    if rope_w is not None:
                        bb2, ps2 = bank()
                        for c in range(4):
                            K.op(PE, lambda c=c, off=off, n=n, ps2=ps2: nc.tensor.matmul(
                                ps2[0:64, 0:n], rope_w(c), srcT[:, c, off:off + n], start=(c == 0), stop=(c == 3)),
                                R=[b_w, b_src], W=[bb2], inc=(c == 3))
                        K.op(DVE, lambda off=off, n=n, ps2=ps2: nc.vector.tensor_scalar(
                            out=rawr[:, off:off + n], in0=ps2[0:64, 0:n], scalar1=hgrT[:, l:l + 1], scalar2=None,
                            op0=ALU.mult), R=[bb2, B_mod], W=[b_rawr])
                        K.op(ACT, lambda off=off, n=n, ps2=ps2: nc.scalar.activation(
                            out=sqr[0:64, off:off + n], in_=ps2[0:64, 0:n], func=AF.Square), R=[], W=[bb2, b_sq])
                K.op(ACT, lambda: nc.scalar.activation(out=sq[:], in_=raw[:], func=AF.Square), R=[b_raw], W=[b_sq])
                rsq = sqr if rope_w is not None else krsq
                for ti, (off, n) in enumerate(TILES5):
                    bb, ps = bank()
                    K.op(PE, lambda off=off, n=n, ps=ps: nc.tensor.matmul(ps[:, 0:n], onesb[:], sq[:, off:off + n],
                                                                           start=True, stop=False),
                         R=[b_sq, B_const], W=[bb], inc=False)
                    K.op(PE, lambda off=off, n=n, ps=ps: nc.tensor.matmul(ps[:, 0:n], onesb[:, :], rsq[:, off:off + n],
                                                                           start=False, stop=True),
                         R=[b_sq, b_kr, B_const], W=[bb])
                    K.op(ACT, lambda off=off, n=n, ps=ps: nc.scalar.activation(
                        out=r1[:, off:off + n], in_=ps[:, 0:n], func=AF.Sqrt, bias=EPS, scale=1.0 / 192), R=[bb], W=[b_r])
                K.op(DVE, lambda: nc.vector.reciprocal(out=rstd[:], in_=r1[:]), R=[b_r], W=[b_r])
                K.op(DVE, lambda: nc.vector.scalar_tensor_tensor(
                    out=dst_hat[:], in0=raw[:], scalar=g_col, in1=rstd[:], op0=ALU.mult, op1=ALU.mult),
                    R=[b_raw, b_r, B_mod], W=[b_dst])
                if rope_w is not None:
                    K.op(ACT, lambda: nc.scalar.copy(out=rawrb[:], in_=rawr[:]), R=[b_rawr], W=[b_rawr])
                    rope_apply(rawr, rawrb, b_rawr, t1, b_t)
                    K.op(DVE, lambda: nc.vector.tensor_tensor(out=dst_rope[0:64, :], in0=t1[:], in1=rstd[0:64, :], op=ALU.mult),
                         R=[b_t, b_r], W=[b_dst])
                else:
                    K.op(DVE, lambda: nc.vector.tensor_tensor(out=dst_rope[0:64, :], in0=krot[:], in1=rstd[0:64, :], op=ALU.mult),
                         R=[b_kr, b_r], W=[b_dst])

            for h in range(8):
                hb = h % 2
                head_norm(lambda c, h=h: wkv[:, c, h * 256:h * 256 + 128], None, kvnT, b_kvn,
                          hgnT[:, L + l:L + l + 1], khat[hb], krope[hb], b_hd[hb], None)
                if "mla_b1" in dbg:
                    K.barrier()
                    return True
                for g in range(5):
                    bb, ps = bank()
                    tts = list(range(g * 4, min(g * 4 + 4, NTT)))
                    for j, tt in enumerate(tts):
                        for c in range(4):
                            K.op(PE, lambda c=c, j=j, tt=tt, ps=ps, h=h: nc.tensor.matmul(
                                ps[:, j * 128:(j + 1) * 128], kvnT[:, c, tt * 128:(tt + 1) * 128],
                                wkv[:, c, h * 256 + 128:h * 256 + 256], start=(c == 0), stop=(c == 3)),
                                R=[b_w, b_kvn], W=[bb], inc=(c == 3 and j == len(tts) - 1))
                    nn = len(tts)
                    K.op(ACT, lambda g=g, nn=nn, ps=ps, hb=hb: nc.scalar.copy(
                        out=vh[hb][:, g * 4:g * 4 + nn, :], in_=ps[:, 0:nn * 128].rearrange("p (a b) -> p a b", b=128)),
                        R=[bb], W=[b_hd[hb]])
                if "mla_b2" in dbg:
                    K.barrier()
                    return True
                head_norm(lambda c, h=h: wq[:, c, h * 192:h * 192 + 128], lambda c, h=h: wq[:, c, h * 192 + 128:h * 192 + 192],
                          qnT, b_qn, hgnT[:, l:l + 1], qhat[hb], qrope[hb], b_hd[hb], True)
                if "mla_b" in dbg:
                    K.barrier()
                    return True
                ms = mstg[hb]
                for qi, (qoff, nq) in enumerate(TILES5):
                    kts = [0, 1] if qoff < TC else list(range(NTT))
                    bO, psO = bank(qi % 2)
                    bL, psL = bank(2 + qi % 2)

                    def s_mm(kt, slot):
                        bS, psS = bank(4 + slot % 4)
                        K.op(PE, lambda: nc.tensor.matmul(psS[:, 0:nq], khat[hb][:, kt * 128:(kt + 1) * 128],
                                                          qhat[hb][:, qoff:qoff + nq], start=True, stop=False),
                             R=[b_hd[hb]], W=[bS], inc=False)
                        K.op(PE, lambda: nc.tensor.matmul(psS[:, 0:nq], krope[hb][:, kt * 128:(kt + 1) * 128],
                                                          qrope[hb][:, qoff:qoff + nq], start=False, stop=True),
                             R=[b_hd[hb]], W=[bS])
                        return bS, psS
                    pend = s_mm(kts[0], 0)
                    for ki, kt in enumerate(kts):
                        bS, psS = pend
                        if ki + 1 < len(kts):
                            pend = s_mm(kts[ki + 1], ki + 1)
                        pi = npt % 2
                        npt += 1
                        K.op(ACT, lambda psS=psS, pi=pi: nc.scalar.activation(out=PT[pi][:, 0:nq], in_=psS[:, 0:nq], func=AF.Exp,
                                                                              scale=SCALE), R=[bS], W=[b_PT[pi]])
                        K.op(PE, lambda kt=kt, pi=pi, ki=ki: nc.tensor.matmul(psO[:, 0:nq], vh[hb][:, kt, :], PT[pi][:, 0:nq],
                                                                              start=(ki == 0), stop=(ki == len(kts) - 1)),
                             R=[b_hd[hb], b_PT[pi]], W=[bO], inc=False)
                        K.op(PE, lambda pi=pi, ki=ki: nc.tensor.matmul(psL[:, 0:nq], onesb[:], PT[pi][:, 0:nq],
                                                                       start=(ki == 0), stop=(ki == len(kts) - 1)),
                             R=[b_PT[pi], B_const], W=[bL])
                    K.op(DVE, lambda: nc.vector.reciprocal(out=rl[:, 0:nq], in_=psL[:, 0:nq]), R=[bL], W=[b_rl])
                    K.op(DVE, lambda: nc.vector.tensor_tensor(out=ms[:, qoff:qoff + nq], in0=psO[:, 0:nq], in1=rl[:, 0:nq],
                                                              op=ALU.mult), R=[bO, b_rl], W=[b_mstg[hb]])
                K.dma(SQ, mlaT[h], ms[:], R=[b_mstg[hb]], W=[B_z])
            K.barrier()

        def gla_stage(l):
            B_o = Buf()
            wg = K.sb("wg", [16, 2, 1024], BF16)
            lr = K.sb("lr", [16, 2, T], BF16)
            gon = K.sb("gon", [128, 256], F32)
            b_in = Buf()
            K.dma(GQ, wg[:], wg_d[l].rearrange("d k n -> k d n"), W=[b_in])
            K.dma(GQ, lr[:, 0, :], lrfT, W=[b_in], max_dma_last_dim=4096)
            K.dma(GQ, lr[:, 1, :], lrbT, W=[b_in], max_dma_last_dim=4096)
            K.dma(SQ, gon[:], gon_d[l], W=[b_in])
            rmask = K.sb("rmask", [128, T], BF16)
            K.op(DVE, lambda: nc.vector.memset(rmask[:], 1.0), W=[b_in])
            K.op(DVE, lambda: nc.vector.memset(rmask[:, 0::128], 0.0), W=[b_in])
            Lf = K.sb("Lf", [128, T], F32)
            Cs = K.sb("Cs", [128, T], F32)
            E1 = K.sb("E1", [128, T], F32)
            E2 = Lf
            ctot = K.sb("ctot", [128, NTT], F32)
            b_L, b_C, b_E = Buf(), Buf(), Buf()
            gkl = K.sb("gkl", [128, T], BF16)
            gql = K.sb("gql", [128, T], BF16)
            b_gl = Buf()
            qt = [K.sb(f"qt{d}", [128, 2, T], BF16) for d in range(2)]
            kt = [K.sb(f"kt{d}", [128, 2, T], BF16) for d in range(2)]
            EB = [K.sb(f"EB{d}", [128, 2, NTT], F32) for d in range(2)]
            b_qk = [Buf(), Buf()]
            vh = K.sb("gvh", [128, NTT, 256], BF16)
            ggh = K.sb("ggh", [128, NTT, 256], BF16)
            b_v = Buf()
            S = [K.sb(f"S{d}", [128, 2, 256], F32) for d in range(2)]
            Sb = [K.sb(f"Sb{d}", [128, 2, 256], BF16) for d in range(2)]
            b_S = [Buf(), Buf()]
            b_Sb = [Buf(), Buf()]
            Am = [K.sb(f"Am{d}", [128, NTT, 128], BF16) for d in range(2)]
            b_Am = [Buf(), Buf()]
            khtok = [K.sb(f"khtok{d}", [128, NTT, 256], BF16) for d in range(2)]
            b_kht = [Buf(), Buf()]
            oacc = K.sb("oacc", [128, NTT, 256], F32)
            b_oa = Buf()
            sqo = K.sb("sqo", [128, NTT, 256], BF16)
            ss = K.sb("ss", [128, NTT], F32)
            b_ss = Buf()
            glatok = K.sb("glatok", [128, NTT, 256], BF16)
            b_gt = Buf()
            gstg = K.sb("gstg", [128, T], BF16)
            b_gs = Buf()
            order = [list(range(NTT)), [1, 0] + list(range(NTT - 1, 1, -1))]
            for h in range(4):
                K.dma(SQ, vh[:], gvtm[:, :, h * 256:(h + 1) * 256].rearrange("t p n -> p t n"), W=[b_v])
                K.dma(SQ, ggh[:], ggtm[:, :, h * 256:(h + 1) * 256].rearrange("t p n -> p t n"), W=[b_v])
                for d in range(2):
                    for dkc in range(2):
                        ch = h * 2 + dkc
                        K.dma(SQ, gkl[:], gkT[ch], W=[b_gl])
                        K.dma(SQ, gql[:], gqT[ch], W=[b_gl])
                        for (off, n) in TILES5:
                            bb, ps = bank()
                            K.op(PE, lambda: nc.tensor.matmul(ps[:, 0:n], wg[:, d, ch * 128:(ch + 1) * 128], lr[:, d, off:off + n],
                                                              start=True, stop=True), R=[b_in], W=[bb])
                            col = l * 16 + d * 8 + ch
                            K.op(ACT, lambda: nc.scalar.activation(out=Lf[:, off:off + n], in_=ps[:, 0:n], func=AF.Exp,
                                                                   bias=ngbT[:, col:col + 1], scale=-1.0), R=[bb, B_mod], W=[b_L])
                        K.op(ACT, lambda: nc.scalar.activation(out=Lf[:], in_=Lf[:], func=AF.Ln, bias=1.0, scale=1.0),
                             R=[b_L], W=[b_L])
                        K.op(DVE, lambda: nc.vector.tensor_tensor_scan(out=Cs[:], data0=rmask[:], data1=Lf[:], initial=0.0,
                                                                       op0=ALU.mult, op1=ALU.add), R=[b_L, b_in], W=[b_C])
                        if "gla_dbg" in dbg and h == 0 and d == 0 and dkc == 0:
                            K.dma(SQ, dL, Lf[:], R=[b_L], W=[Buf()])
                            K.dma(SQ, dC, Cs[:], R=[b_C], W=[Buf()])
                        if d == 1:
                            K.op(DVE, lambda: nc.vector.tensor_tensor(out=Lf[:], in0=Lf[:], in1=Cs[:], op=ALU.subtract),
                                 R=[b_C], W=[b_L])
                            K.op(DVE, lambda: nc.vector.tensor_copy(out=ctot[:], in_=Cs[:, 127::128]), R=[b_C], W=[b_L])
                            K.op(DVE, lambda: nc.vector.tensor_tensor(
                                out=Cs[:].rearrange("p (c t) -> p c t", t=128), in0=Lf[:].rearrange("p (c t) -> p c t", t=128),
                                in1=ctot[:].unsqueeze(2).to_broadcast([128, NTT, 128]), op=ALU.add),
                                R=[b_L], W=[b_C])
                        if "gla_dbg" in dbg and h == 0 and d == 1 and dkc == 0:
                            K.dma(SQ, dC1, Cs[:], R=[b_C], W=[Buf()])
                        K.op(ACT, lambda: nc.scalar.activation(out=E1[:], in_=Cs[:], func=AF.Exp, scale=-1.0 / 16), R=[b_C], W=[b_E])
                        K.op(ACT, lambda: nc.scalar.activation(out=E2[:], in_=Cs[:], func=AF.Exp, scale=1.0 / 16), R=[b_C], W=[b_E, b_L])
                        ebv = E1[:, 127::128] if d == 0 else E1[:, 0::128]
                        K.op(DVE, lambda: nc.vector.tensor_copy(out=EB[d][:, dkc, :], in_=ebv), R=[b_E], W=[b_qk[d]])
                        K.op(DVE, lambda: nc.vector.scalar_tensor_tensor(out=qt[d][:, dkc, :], in0=gql[:], scalar=1.0 / 16, in1=E1[:],
                                                                         op0=ALU.mult, op1=ALU.mult), R=[b_gl, b_E], W=[b_qk[d]])
                        K.op(DVE, lambda: nc.vector.tensor_tensor(out=kt[d][:, dkc, :], in0=gkl[:], in1=E2[:], op=ALU.mult),
                             R=[b_gl, b_E, b_L], W=[b_qk[d]])
                    K.op(DVE, lambda: nc.vector.memset(S[d][:], 0.0), W=[b_S[d]])
                    K.op(DVE, lambda: nc.vector.memset(Sb[d][:], 0.0), W=[b_Sb[d]])
                for d in range(2):
                    for g in range(5):
                        ccs = list(range(g * 4, min(g * 4 + 4, NTT)))
                        bb, ps = bank()
                        for j, c in enumerate(ccs):
                            cs = slice(c * 128, (c + 1) * 128)
                            for dkc in range(2):
                                K.op(PE, lambda: nc.tensor.matmul(ps[:, j * 128:(j + 1) * 128], kt[d][:, dkc, cs], qt[d][:, dkc, cs],
                                                                  start=(dkc == 0), stop=(dkc == 1)), R=[b_qk[d]], W=[bb],
                                     inc=(dkc == 1 and j == len(ccs) - 1))
                        nn = len(ccs)
                        K.op(DVE, lambda: nc.vector.tensor_tensor(
                            out=Am[d][:, g * 4:g * 4 + nn, :], in0=ps[:, 0:nn * 128].rearrange("p (a b) -> p a b", b=128),
                            in1=mskS[:, d, :].unsqueeze(1).to_broadcast([128, nn, 128]), op=ALU.mult), R=[bb, B_const], W=[b_Am[d]])
                    for g in range(NTT // 2):
                        bb, ps = bank()
                        for j in range(2):
                            c = g * 2 + j
                            cs = slice(c * 128, (c + 1) * 128)
                            for dkc in range(2):
                                K.op(PE, lambda: nc.tensor.matmul(ps[:, (j * 2 + dkc) * 128:(j * 2 + dkc + 1) * 128], kt[d][:, dkc, cs],
                                                                  identb[:], start=True, stop=True), R=[b_qk[d], B_const], W=[bb],
                                     inc=(dkc == 1 and j == 1))
                        K.op(ACT, lambda: nc.scalar.copy(out=khtok[d][:, g * 2:g * 2 + 2, :],
                                                         in_=ps[:, :].rearrange("p (a b) -> p a b", b=256)), R=[bb], W=[b_kht[d]])
                seen = set()
                for i in range(NTT):
                    for d in range(2):
                        c = order[d][i]
                        cs = slice(c * 128, (c + 1) * 128)
                        bU, psU = bank(4 + d * 2 + i % 2)
                        for dkc in range(2):
                            K.op(PE, lambda: nc.tensor.matmul(psU[:, dkc * 256:(dkc + 1) * 256], khtok[d][:, c, dkc * 128:(dkc + 1) * 128],
                                                              vh[:, c, :], start=True, stop=True), R=[b_kht[d], b_v], W=[bU],
                                 inc=(dkc == 1))
                        bO, psO = bank(0 + d * 2 + i % 2)
                        for dkc in range(2):
                            K.op(PE, lambda: nc.tensor.matmul(psO[:, 0:256], qt[d][:, dkc, cs], Sb[d][:, dkc, :],
                                                              start=(dkc == 0), stop=False), R=[b_qk[d], b_Sb[d]], W=[bO], inc=False)
                        K.op(PE, lambda: nc.tensor.matmul(psO[:, 0:256], Am[d][:, c, :], vh[:, c, :], start=False, stop=True),
                             R=[b_Am[d], b_v], W=[bO])
                        for dkc in range(2):
                            K.op(DVE, lambda: nc.vector.tensor_scalar(out=S[d][:, dkc, :], in0=S[d][:, dkc, :],
                                                                      scalar1=EB[d][:, dkc, c:c + 1], scalar2=None, op0=ALU.mult),
                                 R=[b_qk[d]], W=[b_S[d]])
                        for dkc in range(2):
                            K.op(DVE, lambda: nc.vector.scalar_tensor_tensor(
                                out=S[d][:, dkc, :], in0=psU[:, dkc * 256:(dkc + 1) * 256], scalar=EB[d][:, dkc, c:c + 1],
                                in1=S[d][:, dkc, :], op0=ALU.mult, op1=ALU.add), R=[bU, b_qk[d]], W=[b_S[d]])
                        K.op(ACT, lambda: nc.scalar.copy(out=Sb[d][:], in_=S[d][:]), R=[b_S[d]], W=[b_Sb[d]])
                        if c not in seen:
                            seen.add(c)
                            K.op(DVE, lambda: nc.vector.tensor_copy(out=oacc[:, c, :], in_=psO[:, 0:256]), R=[bO], W=[b_oa])
                        else:
                            K.op(DVE, lambda: nc.vector.tensor_tensor(out=oacc[:, c, :], in0=psO[:, 0:256], in1=oacc[:, c, :],
                                                                      op=ALU.add), R=[bO], W=[b_oa])
                if "gla_dbg" in dbg and h == 0:
                    K.dma(SQ, dO, oacc[:], R=[b_oa], W=[Buf()])
                K.op(ACT, lambda: nc.scalar.activation(out=sqo[:], in_=oacc[:], func=AF.Square), R=[b_oa], W=[b_ss])
                K.op(DVE, lambda: nc.vector.tensor_reduce(out=ss[:], in_=sqo[:], axis=AX.X, op=ALU.add), R=[], W=[b_ss])
                K.op(ACT, lambda: nc.scalar.activation(out=ss[:], in_=ss[:], func=AF.Sqrt, bias=EPS, scale=1.0 / 256), R=[], W=[b_ss])
                K.op(DVE, lambda: nc.vector.reciprocal(out=ss[:], in_=ss[:]), R=[], W=[b_ss])
                K.op(DVE, lambda: nc.vector.tensor_tensor(out=oacc[:], in0=oacc[:], in1=ss[:].unsqueeze(2).to_broadcast([128, NTT, 256]),
                                                          op=ALU.mult), R=[b_ss], W=[b_oa])
                K.op(DVE, lambda: nc.vector.tensor_tensor(out=oacc[:], in0=oacc[:],
                                                          in1=gon[:].unsqueeze(1).to_broadcast([128, NTT, 256]), op=ALU.mult),
                     R=[b_in], W=[b_oa])
                K.op(ACT, lambda: nc.scalar.activation(out=sqo[:], in_=ggh[:], func=AF.Silu), R=[b_v], W=[b_ss])
                K.op(DVE, lambda: nc.vector.tensor_tensor(out=glatok[:], in0=oacc[:], in1=sqo[:], op=ALU.mult),
                     R=[b_oa, b_ss], W=[b_gt])
                for dvc in range(2):
                    for g in range(5):
                        tts = list(range(g * 4, min(g * 4 + 4, NTT)))
                        bb, ps = bank()
                        for j, tt in enumerate(tts):
                            K.op(PE, lambda: nc.tensor.matmul(ps[:, j * 128:(j + 1) * 128], glatok[:, tt, dvc * 128:(dvc + 1) * 128],
                                                              identb[:], start=True, stop=True), R=[b_gt, B_const], W=[bb],
                                 inc=(j == len(tts) - 1))
                        nn = len(tts) * 128
                        K.op(ACT, lambda: nc.scalar.copy(out=gstg[:, g * 512:g * 512 + nn], in_=ps[:, 0:nn]), R=[bb], W=[b_gs])
                    K.dma(SQ, glaT[h * 2 + dvc], gstg[:], R=[b_gs], W=[B_o])
            K.barrier()

        def pool_stage(l):
            B_o = Buf()
            WIN = (2, 4, 8, 16)
            pin = K.sb("pin", [128, T], BF16)
            b_pin = Buf()
            Pc = K.sb("Pc", [128, TC + 16], F32)
            Pl = K.sb("Pl", [128, TL + 16], F32)
            b_P = Buf()
            K.op(DVE, lambda: nc.vector.memset(Pc[:], 0.0), W=[b_P])
            K.op(DVE, lambda: nc.vector.memset(Pl[:], 0.0), W=[b_P])
            qa = K.sb("qa", [128, TL + 16], F32)
            qb = K.sb("qb", [128, TL + 16], F32)
            b_q = Buf()
            icnt = K.sb("icnt", [128, T], F32)
            b_ic = Buf()
            pooled = K.sb("pooled", [128, 8, T], BF16)
            b_pl = Buf()
            pw = K.sb("pw", [128, 2, 256], BF16)
            b_pw = Buf()
            pstg = K.sb("pstg", [128, T], BF16)
            b_ps = Buf()
            for ch in range(8):
                g = ch // 2
                w = WIN[g]
                K.dma(SQ, pin[:], pinT[ch], W=[b_pin])
                if ch % 2 == 0:
                    K.dma(SQ, icnt[:], invc_d[g], W=[b_ic])
                K.op(ACT, lambda: nc.scalar.copy(out=Pc[:, 8:8 + TC], in_=pin[:, 0:TC]), R=[b_pin], W=[b_P])
                K.op(ACT, lambda: nc.scalar.copy(out=Pl[:, 8:8 + TL], in_=pin[:, TC:T]), R=[b_pin], W=[b_P])
                for (P, n, off) in ((Pc, TC, 0), (Pl, TL, TC)):
                    cur, ln = P, n + 16
                    step = 1
                    nxts = [qa, qb]
                    ni = 0
                    while step < w:
                        nx = nxts[ni % 2]
                        ni += 1
                        K.op(DVE, lambda cur=cur, nx=nx, ln=ln, step=step: nc.vector.tensor_tensor(
                            out=nx[:, 0:ln - step], in0=cur[:, 0:ln - step], in1=cur[:, step:ln], op=ALU.add), R=[b_P], W=[b_q])
                        cur, ln = nx, ln - step
                        step *= 2
                    o0 = 8 - w // 2
                    other = nxts[ni % 2]
                    K.op(DVE, lambda cur=cur, other=other, o0=o0, n=n, off=off: nc.vector.tensor_tensor(
                        out=other[:, 0:n], in0=cur[:, o0:o0 + n], in1=icnt[:, off:off + n], op=ALU.mult), R=[b_ic], W=[b_q])
                    K.op(DVE, lambda other=other, P=P, n=n, off=off: nc.vector.tensor_tensor(
                        out=pooled[:, ch, off:off + n], in0=other[:, 0:n], in1=P[:, 8:8 + n], op=ALU.subtract), R=[b_P], W=[b_q, b_pl])
            for g in range(4):
                K.dma(GQ, pw[:], pool_w[l, g].rearrange("(k p) n -> p k n", p=128), W=[b_pw])
                for dc in range(2):
                    for (off, n) in TILES5:
                        bb, ps = bank()
                        for cc in range(2):
                            K.op(PE, lambda: nc.tensor.matmul(ps[:, 0:n], pw[:, cc, dc * 128:(dc + 1) * 128], pooled[:, g * 2 + cc, off:off + n],
                                                              start=(cc == 0), stop=(cc == 1)), R=[b_pw, b_pl], W=[bb], inc=(cc == 1))
                        col = l * 8 + g * 2 + dc
                        K.op(ACT, lambda: nc.scalar.activation(out=pstg[:, off:off + n], in_=ps[:, 0:n], func=AF.Copy,
                                                               scale=pscT[:, col:col + 1]), R=[bb, B_mod], W=[b_ps])
                    K.dma(SQ, poolT[g * 2 + dc], pstg[:], R=[b_ps], W=[B_o])
            K.barrier()

        GT = 576
        GROUPS = [(0, 576, [(0, 256, 1), (256, 320, 0)])] + [(576 * i, 576, [(0, 288, 0), (288, 288, 0)]) for i in range(1, 4)]
        def tl_stage(l):
            xT_v = xT.rearrange("c p t -> p c t")
            wall = K.sb("tw", [128, 4 * 8192], BF16)
            wsl = [wall[:, i * 8192:(i + 1) * 8192] for i in range(4)]
            b_wsl = [Buf() for _ in range(4)]
            b_w2 = [Buf(), Buf()]
            nld = [0]
            nld2 = [0]

            def wload(src_ap, kc, cw):
                i = nld[0] % 4
                nld[0] += 1
                v = wsl[i][:, 0:kc * cw].rearrange("p (k n) -> p k n", n=cw)
                K.dma(GQ, v, src_ap.rearrange("(k p) n -> p k n", p=128), W=[b_wsl[i], b_w2[0], b_w2[1]])
                return v, b_wsl[i]

            def w2load(src_ap):
                i = nld2[0] % 2
                nld2[0] += 1
                v = wall[:, i * 11264:(i + 1) * 11264].rearrange("p (k n) -> p k n", n=256)
                K.dma(GQ, v, src_ap.rearrange("(k p) n -> p k n", p=128), W=[b_w2[i], b_wsl[0], b_wsl[1], b_wsl[2], b_wsl[3]])
                return v, b_w2[i]
            xg = K.sb("xg", [128, DC, GT], F32)
            b_xg = Buf()
            mT = K.sb("mT", [128, DC, GT], BF16)
            b_mT = Buf()
            for (g0, gsz, subs) in GROUPS:
                K.dma(SQ, xg[:, :, 0:gsz], xT_v[:, :, g0:g0 + gsz], R=[B_xT], W=[b_xg])
                with ExitStack() as sti:
                    K.es = sti
                    br = [K.sb(f"br{n}", [128, 8, GT], BF16) for n in range(3)]
                    b_br = Buf()
                    for n, src in enumerate((poolT, mlaT, glaT)):
                        K.dma(SQ, br[n][:, :, 0:gsz], src.rearrange("c p t -> p c t")[:, :, g0:g0 + gsz], W=[b_br])
                    gt = [K.sb(f"gt{i}", [128, 4, GT], BF16) for i in range(2)]
                    b_gt = [Buf(), Buf()]
                    macc = K.sb("macc", [128, 4, GT], F32)
                    mtmp = K.sb("mtmp", [128, 320], F32)
                    b_ma = Buf()
                    ng = 0
                    for cg in range(4):
                        for n in range(3):
                            wv, bw = wload(w_branch[l, n, :, cg * 512:(cg + 1) * 512], 8, 512)
                            gi2 = ng % 2
                            ng += 1
                            K.dma(SQ, gt[gi2][:, :, 0:gsz], gateT[n * 16 + cg * 4:n * 16 + cg * 4 + 4].rearrange("c p t -> p c t")[:, :, g0:g0 + gsz],
                                  W=[b_gt[gi2]])
                            for j in range(4):
                                dch = cg * 4 + j
                                for (so, sn, jj) in subs:
                                    bb, ps = bank()
                                    for c in range(8):
                                        K.op(PE, lambda: nc.tensor.matmul(ps[:, 0:sn], wv[:, c, j * 128:(j + 1) * 128], br[n][:, c, so:so + sn],
                                                                          start=(c == 0), stop=(c == 7)), R=[bw, b_br], W=[bb], inc=(c == 7))
                                    if n == 0:
                                        K.op(DVE, lambda: nc.vector.tensor_tensor(out=macc[:, j, so:so + sn], in0=ps[:, 0:sn],
                                                                                  in1=gt[gi2][:, j, so:so + sn], op=ALU.mult),
                                             R=[bb, b_gt[gi2]], W=[b_ma])
                                    else:
                                        K.op(DVE, lambda: nc.vector.tensor_tensor(out=mtmp[:, 0:sn], in0=ps[:, 0:sn],
                                                                                  in1=gt[gi2][:, j, so:so + sn], op=ALU.mult),
                                             R=[bb, b_gt[gi2]], W=[b_ma])
                                        if n == 1:
                                            K.op(DVE, lambda: nc.vector.tensor_tensor(out=macc[:, j, so:so + sn], in0=macc[:, j, so:so + sn],
                                                                                      in1=mtmp[:, 0:sn], op=ALU.add), R=[], W=[b_ma])
                                        else:
                                            K.op(DVE, lambda: nc.vector.tensor_tensor(out=mT[:, dch, so:so + sn], in0=macc[:, j, so:so + sn],
                                                                                      in1=mtmp[:, 0:sn], op=ALU.add), R=[b_ma], W=[b_mT])
                    K.barrier(only=[K.PE, K.ACT, K.DVE, K.SQ])
                K.es = tl_es[0]
                for cg in range(4):
                    wv, bw = wload(w_out[l, :, cg * 512:(cg + 1) * 512], 16, 512)
                    for j in range(4):
                        dch = cg * 4 + j
                        for (so, sn, jj) in subs:
                            bb, ps = bank()
                            for k in range(16):
                                K.op(PE, lambda: nc.tensor.matmul(ps[:, 0:sn], wv[:, k, j * 128:(j + 1) * 128], mT[:, k, so:so + sn],
                                                                  start=(k == 0), stop=(k == 15)), R=[bw, b_mT], W=[bb], inc=(k == 15))
                            K.op(DVE, lambda: nc.vector.scalar_tensor_tensor(
                                out=xg[:, dch, so:so + sn], in0=ps[:, 0:sn], scalar=modT[:, l, 32 + dch, jj:jj + 1],
                                in1=xg[:, dch, so:so + sn], op0=ALU.mult, op1=ALU.add), R=[bb, B_mod], W=[b_xg])
                with ExitStack() as sti:
                    K.es = sti
                    sqb = K.sb("f_sq", [128, DC, 320], BF16)
                    rs = K.sb("f_rs", [128, 320], F32)
                    tmp = [K.sb(f"f_t{i}", [128, 320], F32) for i in range(2)]
                    b_sq, b_rs, b_tm = Buf(), Buf(), [Buf(), Buf()]
                    for (so, sn, jj) in subs:
                        K.op(ACT, lambda: nc.scalar.activation(out=sqb[:, :, 0:sn], in_=xg[:, :, so:so + sn], func=AF.Square),
                             R=[b_xg], W=[b_sq])
                        bb, ps = bank()
                        for c in range(DC):
                            K.op(PE, lambda: nc.tensor.matmul(ps[:, 0:sn], onesb[:], sqb[:, c, 0:sn], start=(c == 0), stop=(c == DC - 1)),
                                 R=[b_sq, B_const], W=[bb], inc=(c == DC - 1))
                        K.op(ACT, lambda: nc.scalar.activation(out=rs[:, 0:sn], in_=ps[:, 0:sn], func=AF.Sqrt, bias=EPS, scale=1.0 / D),
                             R=[bb], W=[b_rs])
                        K.op(DVE, lambda: nc.vector.reciprocal(out=rs[:, 0:sn], in_=rs[:, 0:sn]), R=[], W=[b_rs])
                        for c in range(DC):
                            q = c % 2
                            K.op(DVE, lambda: nc.vector.scalar_tensor_tensor(
                                out=tmp[q][:, 0:sn], in0=xg[:, c, so:so + sn], scalar=A2[:, l, c, jj:jj + 1], in1=rs[:, 0:sn],
                                op0=ALU.mult, op1=ALU.mult), R=[b_xg, b_rs, B_mod], W=[b_tm[q]])
                            K.op(ACT, lambda: nc.scalar.activation(out=mT[:, c, so:so + sn], in_=tmp[q][:, 0:sn], func=AF.Identity,
                                                                   bias=modT[:, l, 48 + c, jj:jj + 1], scale=1.0),
                                 R=[b_tm[q], B_mod], W=[b_mT])
                    uT = K.sb("uT", [128, FC, GT], BF16)
                    b_uT = Buf()
                    s1 = [K.sb(f"s1_{i}", [128, 320], F32) for i in range(2)]
                    b_s1 = [Buf(), Buf()]
                    ns = 0
                    for fg in range(FC // 4):
                        w1v, bw1 = wload(ffn_w1[l, :, fg * 512:(fg + 1) * 512], 16, 512)
                        w3v, bw3 = wload(ffn_w3[l, :, fg * 512:(fg + 1) * 512], 16, 512)
                        for j in range(4):
                            f = fg * 4 + j
                            for (so, sn, jj) in subs:
                                b1, p1 = bank()
                                b3, p3 = bank()
                                for k in range(16):
                                    K.op(PE, lambda: nc.tensor.matmul(p1[:, 0:sn], w1v[:, k, j * 128:(j + 1) * 128], mT[:, k, so:so + sn],
                                                                      start=(k == 0), stop=(k == 15)), R=[bw1, b_mT], W=[b1], inc=(k == 15))
                                for k in range(16):
                                    K.op(PE, lambda: nc.tensor.matmul(p3[:, 0:sn], w3v[:, k, j * 128:(j + 1) * 128], mT[:, k, so:so + sn],
                                                                      start=(k == 0), stop=(k == 15)), R=[bw3, b_mT], W=[b3], inc=(k == 15))
                                si = ns % 2
                                ns += 1
                                K.op(ACT, lambda: nc.scalar.activation(out=s1[si][:, 0:sn], in_=p1[:, 0:sn], func=AF.Silu),
                                     R=[b1], W=[b_s1[si]])
                                K.op(DVE, lambda: nc.vector.tensor_tensor(out=uT[:, f, so:so + sn], in0=p3[:, 0:sn], in1=s1[si][:, 0:sn],
                                                                          op=ALU.mult), R=[b3, b_s1[si]], W=[b_uT])
                    for dch in range(DC):
                        if dch % 2 == 0:
                            wv, bw = w2load(ffn_w2[l, :, dch * 128:(dch + 2) * 128])
                        jo = (dch % 2) * 128
                        for (so, sn, jj) in subs:
                            bb, ps = bank()
                            for f in range(FC):
                                K.op(PE, lambda: nc.tensor.matmul(ps[:, 0:sn], wv[:, f, jo:jo + 128], uT[:, f, so:so + sn],
                                                                  start=(f == 0), stop=(f == FC - 1)), R=[bw, b_uT], W=[bb], inc=(f == FC - 1))
                            K.op(DVE, lambda: nc.vector.scalar_tensor_tensor(
                                out=xg[:, dch, so:so + sn], in0=ps[:, 0:sn], scalar=modT[:, l, 80 + dch, jj:jj + 1],
                                in1=xg[:, dch, so:so + sn], op0=ALU.mult, op1=ALU.add), R=[bb, B_mod], W=[b_xg])
                    K.dma(SQ, xT_v[:, :, g0:g0 + gsz], xg[:, :, 0:gsz], R=[b_xg], W=[B_xT])
                    K.barrier(only=[K.PE, K.ACT, K.DVE, K.SQ])
                K.es = tl_es[0]
            K.barrier()
        tl_es = [None]

        def out_stage():
            xT_v = xT.rearrange("c p t -> p c t")
            xt = [K.sb(f"o_x{i}", [128, DC, 128], F32) for i in range(2)]
            og = [K.sb(f"o_g{i}", [128, D], F32) for i in range(2)]
            b_xt, b_og = [Buf(), Buf()], [Buf(), Buf()]
            B_out = Buf()
            for tt in range(2, NTT):
                i = tt % 2
                K.dma(SQ, xt[i][:], xT_v[:, :, tt * 128:(tt + 1) * 128], R=[B_xT], W=[b_xt[i]])
                for g in range(4):
                    bb, ps = bank()
                    for j in range(4):
                        c = g * 4 + j
                        K.op(PE, lambda: nc.tensor.matmul(ps[:, j * 128:(j + 1) * 128], xt[i][:, c, :], ident[:], start=True, stop=True),
                             R=[b_xt[i], B_const], W=[bb], inc=(j == 3))
                    if g % 2 == 0:
                        K.op(ACT, lambda: nc.scalar.copy(out=og[i][:, g * 512:(g + 1) * 512], in_=ps[:, :]), R=[bb], W=[b_og[i]])
                    else:
                        K.op(DVE, lambda: nc.vector.tensor_copy(out=og[i][:, g * 512:(g + 1) * 512], in_=ps[:, :]), R=[bb], W=[b_og[i]])
                K.dma(SQ, out_d[(tt - 2) * 128:(tt - 1) * 128, :], og[i][:], R=[b_og[i]], W=[B_out])
            K.barrier()

        for l in range(nlayers):
            with ExitStack() as st:
                K.es = st
                hT = K.sb("hT", [128, DC, T], BF16)
                with ExitStack() as st1:
                    K.es = st1
                    norm_mod(nc, K, bank, xT, B_xT, hT, B_hT, A1, modT, 0, l, B_mod, onesb, B_const, TILES5)
                    K.barrier()
                K.es = st
                stg32 = [K.sb(f"stg32_{i}", [128, T], F32) for i in range(2)]
                stg16 = [K.sb(f"stg16_{i}", [128, T], BF16) for i in range(3)]
                b_s32 = [Buf() for _ in range(2)]
                b_s16 = [Buf() for _ in range(3)]
                wsl = [K.sb(f"win{i}", [128, 16, 512], BF16) for i in range(3)]
                b_wsl = [Buf() for _ in range(3)]
                cnt = {"ld": 0, "s32": 0, "s16": 0, "ev": 0}
                B_z = Buf("z")

                def fm_group(col0, width, chunks):
                    i = cnt["ld"] % 3
                    cnt["ld"] += 1
                    K.dma(GQ, wsl[i][:, :, 0:width], w_in[l, :, col0:col0 + width].rearrange("(k p) n -> p k n", p=128),
                          W=[b_wsl[i]])
                    for (co, M, dst, dt, func) in chunks:
                        bks = [bank() for _ in TILES5]
                        for k in range(16):
                            for ti, (off, n) in enumerate(TILES5):
                                bb, ps = bks[ti]
                                K.op(PE, lambda i=i, k=k, co=co, M=M, off=off, n=n, ps=ps: nc.tensor.matmul(
                                    ps[0:M, 0:n], wsl[i][:, k, co:co + M], hT[:, k, off:off + n],
                                    start=(k == 0), stop=(k == 15)),
                                    R=[b_wsl[i], B_hT], W=[bb], inc=(k == 15))
                        if dt is F32:
                            si = cnt["s32"] % 2
                            cnt["s32"] += 1
                            stg, bs = stg32[si], b_s32[si]
                        else:
                            si = cnt["s16"] % 3
                            cnt["s16"] += 1
                            stg, bs = stg16[si], b_s16[si]
                        for ti, (off, n) in enumerate(TILES5):
                            bb, ps = bks[ti]
                            useact = (func is not None) or (cnt["ev"] % 2 == 0)
                            cnt["ev"] += 1
                            if useact:
                                K.op(ACT, lambda M=M, off=off, n=n, ps=ps, stg=stg, func=func: nc.scalar.activation(
                                    out=stg[0:M, off:off + n], in_=ps[0:M, 0:n], func=(func or AF.Copy)), R=[bb], W=[bs])
                            else:
                                K.op(DVE, lambda M=M, off=off, n=n, ps=ps, stg=stg: nc.vector.tensor_copy(
                                    out=stg[0:M, off:off + n], in_=ps[0:M, 0:n]), R=[bb], W=[bs])
                        K.dma(SQ, dst, stg[0:M, :], R=[bs], W=[B_z])

                fm_group(C_CKV, 512, [(j * 128, 128, ckvT[j], F32, None) for j in range(4)])
                fm_group(C_KR, 64, [(0, 64, krT, F32, None)])
                fm_group(C_LRF, 32, [(0, 16, lrfT, F32, None), (16, 16, lrbT, F32, None)])
                fm_group(C_CQ, 512, [(j * 128, 128, cqT[j], F32, None) for j in range(4)])
                for g in range(2):
                    fm_group(C_GK + g * 512, 512, [(j * 128, 128, gkT[g * 4 + j], BF16, None) for j in range(4)])
                for g in range(2):
                    fm_group(C_GQ + g * 512, 512, [(j * 128, 128, gqT[g * 4 + j], BF16, None) for j in range(4)])
                for g in range(2):
                    fm_group(C_PIN + g * 512, 512, [(j * 128, 128, pinT[g * 4 + j], BF16, None) for j in range(4)])
                for g in range(12):
                    fm_group(C_GATE + g * 512, 512, [(j * 128, 128, gateT[g * 4 + j], BF16, AF.Sigmoid) for j in range(4)])
                wtm = K.sb("wtm", [128, 16, 1024], BF16)
                b_wtm = Buf()
                stm = [K.sb(f"stm{i}", [128, 1024], BF16) for i in range(2)]
                b_stm = [Buf(), Buf()]
                for (c0, dst) in ((C_GV, gvtm), (C_GG, ggtm)):
                    for hf in range(2):
                        K.dma(GQ, wtm[:, :, hf * 512:(hf + 1) * 512],
                              w_in[l, :, c0 + hf * 512:c0 + (hf + 1) * 512].rearrange("(k p) n -> p k n", p=128), W=[b_wtm])
                    for tt in range(NTT):
                        si = tt % 2
                        for hf in range(2):
                            bb, ps = bank()
                            for k in range(16):
                                K.op(PE, lambda k=k, tt=tt, hf=hf, ps=ps: nc.tensor.matmul(
                                    ps[:, :], hT[:, k, tt * 128:(tt + 1) * 128], wtm[:, k, hf * 512:(hf + 1) * 512],
                                    start=(k == 0), stop=(k == 15)), R=[b_wtm, B_hT], W=[bb], inc=(k == 15))
                            if hf == 0:
                                K.op(ACT, lambda ps=ps, si=si: nc.scalar.copy(out=stm[si][:, 0:512], in_=ps[:, :]),
                                     R=[bb], W=[b_stm[si]])
                            else:
                                K.op(DVE, lambda ps=ps, si=si: nc.vector.tensor_copy(out=stm[si][:, 512:1024], in_=ps[:, :]),
                                     R=[bb], W=[b_stm[si]])
                        K.dma(SQ, dst[tt], stm[si][:], R=[b_stm[si]], W=[B_z])
                K.barrier()
            K.es = es
            if "stop_l2" in dbg:
                return finish(nc, K, out_d, modT, dbg)
            with ExitStack() as st:
                K.es = st
                stop = mla_stage(l)
            K.es = es
            if stop or "stop_mla" in dbg:
                return finish(nc, K, out_d, modT, dbg)
            with ExitStack() as st:
                K.es = st
                gla_stage(l)
            K.es = es
            if "stop_gla" in dbg:
                return finish(nc, K, out_d, modT, dbg)
            with ExitStack() as st:
                K.es = st
                pool_stage(l)
            K.es = es
            if "stop_pool" in dbg:
                return finish(nc, K, out_d, modT, dbg)
            with ExitStack() as st:
                K.es = st
                tl_es[0] = st
                tl_stage(l)
            K.es = es
            if "stop_tl" in dbg:
                return finish(nc, K, out_d, modT, dbg)
        with ExitStack() as st:
            K.es = st
            out_stage()
        K.es = es

        return finish(nc, K, out_d, modT, dbg)


def norm_mod(nc, K, bank, xT, B_xT, hT, B_hT, Ax, modT, shift_lo, l, B_mod, onesb, B_const, tiles):
    PE, ACT, DVE, SQ = K.PE, K.ACT, K.DVE, K.SQ
    xt = [K.sb(f"nm_x{i}", [128, DC, 512], F32) for i in range(2)]
    b_xt = [Buf(), Buf()]
    sqb = K.sb("nm_sq", [128, DC, 512], BF16)
    b_sq = Buf()
    r1 = K.sb("nm_r1", [128, 512], F32)
    rstd = K.sb("nm_rstd", [128, 512], F32)
    b_r = Buf()
    tmp = [K.sb(f"nm_t{i}", [128, 512], F32) for i in range(2)]
    b_tmp = [Buf(), Buf()]
    xT_v = xT.rearrange("c p t -> p c t")
    for ti, (off, n) in enumerate(tiles):
        i = ti % 2
        j = 1 if off < TC else 0
        K.dma(SQ, xt[i][:, :, 0:n], xT_v[:, :, off:off + n], R=[B_xT], W=[b_xt[i]])
        K.op(ACT, lambda i=i, n=n: nc.scalar.activation(out=sqb[:, :, 0:n], in_=xt[i][:, :, 0:n], func=AF.Square),
             R=[b_xt[i]], W=[b_sq])
        bb, ps = bank()
        for c in range(DC):
            K.op(PE, lambda c=c, n=n, ps=ps: nc.tensor.matmul(ps[:, 0:n], onesb[:], sqb[:, c, 0:n],
                                                               start=(c == 0), stop=(c == DC - 1)),
                 R=[b_sq, B_const], W=[bb], inc=(c == DC - 1))
        K.op(ACT, lambda n=n, ps=ps: nc.scalar.activation(out=r1[:, 0:n], in_=ps[:, 0:n], func=AF.Sqrt,
                                                          bias=EPS, scale=1.0 / D), R=[bb], W=[b_r])
        K.op(DVE, lambda n=n: nc.vector.reciprocal(out=rstd[:, 0:n], in_=r1[:, 0:n]), R=[b_r], W=[b_r])
        for c in range(DC):
            q = c % 2
            K.op(DVE, lambda c=c, n=n, i=i, q=q, j=j: nc.vector.scalar_tensor_tensor(
                out=tmp[q][:, 0:n], in0=xt[i][:, c, 0:n], scalar=Ax[:, l, c, j:j + 1], in1=rstd[:, 0:n],
                op0=ALU.mult, op1=ALU.mult), R=[b_xt[i], b_r, B_mod], W=[b_tmp[q]])
            K.op(ACT, lambda c=c, n=n, q=q, j=j, off=off: nc.scalar.activation(
                out=hT[:, c, off:off + n], in_=tmp[q][:, 0:n], func=AF.Identity,
                bias=modT[:, l, shift_lo + c, j:j + 1], scale=1.0), R=[b_tmp[q], B_mod], W=[B_hT])


def finish(nc, K, out_d, modT, dbg):
    B = Buf()
    if "no_out" not in dbg:
        pass
    K.barrier()
    return nc


def prep_inputs(inp, b):
    f = lambda a: np.ascontiguousarray(a, dtype=np.float32)
    m = {}
    m["xin"] = f(np.concatenate([inp["ctx"][b], inp["x"][b]], 0))
    m["cvec"] = f(np.concatenate([inp["c"][b].reshape(16, 128), inp["c_ctx"].reshape(16, 128)], 0))
    m["w_mod"] = f(inp["w_mod"])
    m["b_mod"] = f(inp["b_mod"].reshape(-1, 128))
    m["norm1_g"] = f(inp["norm1_g"].reshape(-1, 128))
    m["norm2_g"] = f(inp["norm2_g"].reshape(-1, 128))
    m["w_in"] = f(inp["w_in"])
    m["w_kv_up"] = f(inp["mla_w_kv_up"])
    m["w_q_up"] = f(inp["mla_w_q_up"])
    m["qng"] = f(inp["mla_q_norm_g"].reshape(-1, 128))
    m["kvng"] = f(inp["mla_kv_norm_g"].reshape(-1, 128))
    m["hg_n"] = f(np.concatenate([inp["mla_q_head_g"][:, :128], inp["mla_k_head_g"][:, :128]], 0))
    m["hg_r"] = f(np.concatenate([inp["mla_q_head_g"][:, 128:], inp["mla_k_head_g"][:, 128:]], 0))
    m["gla_wg"] = f(inp["gla_w_gate_up"])
    m["gla_b"] = f(inp["gla_b_gate"].reshape(-1, 128))
    m["gon_rep"] = f(np.broadcast_to(inp["gla_out_norm_g"][:, None, :], (L, 128, 256)))
    m["pool_w"] = f(inp["pool_w"])
    m["pool_scale"] = f(inp["pool_scale"].reshape(-1, 128))
    m["w_branch"] = f(inp["w_branch"])
    m["w_out"] = f(inp["w_out"])
    m["ffn_w1"] = f(inp["ffn_w1"])
    m["ffn_w3"] = f(inp["ffn_w3"])
    m["ffn_w2"] = f(inp["ffn_w2"])
    return m


NCORES = 8
ACTIVE = {0: 0, 1: 1, 4: 2, 5: 3}


def kernel(**inputs):
    inp = {k: np.asarray(v) for k, v in inputs.items()}
    consts = host_consts()
    nc = build(L)
    shared = None
    maps = {}
    for core, b in ACTIVE.items():
        m = prep_inputs(inp, b)
        if shared is None:
            shared = m
        else:
            for k in m:
                if k not in ("xin", "cvec"):
                    m[k] = shared[k]
        m.update(consts)
        maps[core] = m
    zero = {k: np.zeros_like(v) for k, v in shared.items()}
    in_maps = [maps.get(core, zero) for core in range(NCORES)]
    res = run_bass_kernel_spmd(nc, in_maps, core_ids=list(range(NCORES)))
    by_b = {b: core for core, b in ACTIVE.items()}
    out = np.stack([np.asarray(res.results[by_b[b]]["out"], dtype=np.float32) for b in range(4)], 0)
    return out


def host_consts():
    c = {}
    half = 32
    inv_freq = 1.0 / (10000.0 ** (np.arange(0, half, 2, dtype=np.float32) / half))
    row = np.repeat(np.arange(TL // 64), 64).astype(np.float32)
    col = np.tile(np.arange(64), TL // 64).astype(np.float32)
    ang_r = row[:, None] * inv_freq[None, :]
    ang_c = col[:, None] * inv_freq[None, :]
    cosL = np.concatenate([np.cos(ang_r), np.cos(ang_r), np.cos(ang_c), np.cos(ang_c)], 1)
    sinL = np.concatenate([np.sin(ang_r), np.sin(ang_r), np.sin(ang_c), np.sin(ang_c)], 1)
    cosT = np.concatenate([np.ones((TC, 64), np.float32), cosL.astype(np.float32)], 0).T
    sinT = np.concatenate([np.zeros((TC, 64), np.float32), sinL.astype(np.float32)], 0).T
    c["cosT"] = np.ascontiguousarray(cosT, dtype=np.float32)
    c["sinT"] = np.ascontiguousarray(sinT, dtype=np.float32)
    Rm = np.zeros((64, 64), np.float32)
    for g in (0, 32):
        for i in range(16):
            Rm[g + 16 + i, g + i] = -1.0
            Rm[g + i, g + 16 + i] = 1.0
    c["Rm"] = Rm
    c["ident_in"] = np.eye(128, dtype=np.float32)
    s_idx = np.arange(128)[:, None]
    t_idx = np.arange(128)[None, :]
    c["masks"] = np.stack([(s_idx <= t_idx), (s_idx >= t_idx)], 0).astype(np.float32)
    inv = np.zeros((4, T), np.float32)
    for gi, w in enumerate((2, 4, 8, 16)):
        for (o, n) in ((0, TC), (TC, TL)):
            t = np.arange(n)
            lo = np.clip(t - w // 2, 0, n - 1)
            hi = np.clip(t + w // 2 - 1, 0, n - 1)
            inv[gi, o:o + n] = 1.0 / (hi - lo + 1)
    c["invcnt"] = np.ascontiguousarray(np.broadcast_to(inv[:, None, :], (4, 128, T)), dtype=np.float32)
    return c
```
